# Optimizing a Trainium2 kernel written in Bass

```python
import math
import jax, jax.numpy as jnp
from jax import lax
import numpy as np

D_MODEL = 1024
BATCH = 4
SEQ = 4096
DEPTH = 4

MIX_WIDTH = D_MODEL
LRU_WIDTH = D_MODEL // 2
LRU_BLOCKS = 8
LRU_BLOCK = LRU_WIDTH // LRU_BLOCKS
LRU_C = 8.0
GDN_HEAD_DIM = 128
GDN_HEADS = (MIX_WIDTH - LRU_WIDTH) // GDN_HEAD_DIM
GDN_WIDTH = GDN_HEADS * GDN_HEAD_DIM
GDN_CHUNK = 64
CONV_WIDTH = 4
D_FF = 4 * D_MODEL
N_MOD = 6
IN_COLS = 2 * LRU_WIDTH + 4 * GDN_WIDTH + 2 * GDN_HEADS
NORM_EPS = 1e-6

kernel_name = "hymba_rglru_gdn_hybrid"


def rms_norm(x, w):
    xf = x.astype(jnp.float32)
    y = xf * lax.rsqrt(jnp.mean(xf * xf, axis=-1, keepdims=True) + NORM_EPS)
    return (y * w.astype(jnp.float32)).astype(x.dtype)


def causal_depthwise_conv(x, w):
    K = w.shape[0]
    S = x.shape[1]
    xp = jnp.pad(x, ((0, 0), (K - 1, 0), (0, 0)))
    y = xp[:, 0:S] * w[0]
    for k in range(1, K):
        y = y + xp[:, k:k + S] * w[k]
    return y


def rg_lru(x, r_pre, i_pre, lam):
    dt = x.dtype
    xf = x.astype(jnp.float32)
    r = jax.nn.sigmoid(r_pre.astype(jnp.float32))
    i = jax.nn.sigmoid(i_pre.astype(jnp.float32))
    log_a = LRU_C * r * jax.nn.log_sigmoid(lam.astype(jnp.float32))
    a = jnp.exp(log_a)
    mult = jnp.sqrt(jnp.maximum(-jnp.expm1(2.0 * log_a), 1e-12))
    b = mult * (i * xf)

    def combine(left, right):
        a1, b1 = left
        a2, b2 = right
        return a1 * a2, a2 * b1 + b2

    _, h = lax.associative_scan(combine, (a, b), axis=1)
    return h.astype(dt)


def l2_normalize(t):
    return t * lax.rsqrt(jnp.sum(t * t, axis=-1, keepdims=True) + 1e-6)


def gated_delta_rule_chunked(q, k, v, g, beta):
    dt = v.dtype
    B, S, H, Dk = q.shape
    Dv = v.shape[-1]
    C = GDN_CHUNK
    N = S // C
    q = l2_normalize(q.astype(jnp.float32)) * (Dk ** -0.5)
    k = l2_normalize(k.astype(jnp.float32))
    v = v.astype(jnp.float32)

    def chunks(t):
        return t.reshape(B, N, C, H, -1).transpose(0, 3, 1, 2, 4)

    q, k, v = chunks(q), chunks(k), chunks(v)
    g = g.astype(jnp.float32).reshape(B, N, C, H).transpose(0, 3, 1, 2)
    beta = beta.astype(jnp.float32).reshape(B, N, C, H).transpose(0, 3, 1, 2)
    g = jnp.cumsum(g, axis=-1)

    causal = jnp.tril(jnp.ones((C, C), dtype=bool))
    strict = jnp.tril(jnp.ones((C, C), dtype=bool), k=-1)
    decay = jnp.exp(jnp.where(causal, g[..., :, None] - g[..., None, :], -jnp.inf))

    k_beta = k * beta[..., None]
    v_beta = v * beta[..., None]
    Lmat = jnp.where(strict, jnp.einsum('bhnid,bhnjd->bhnij', k_beta, k) * decay, 0.0)
    tmat = Lmat + jnp.eye(C, dtype=jnp.float32)
    u = lax.linalg.triangular_solve(tmat, v_beta, left_side=True, lower=True, unit_diagonal=True)
    w = lax.linalg.triangular_solve(tmat, k_beta * jnp.exp(g)[..., None],
                                    left_side=True, lower=True, unit_diagonal=True)
    attn = jnp.where(causal, jnp.einsum('bhnid,bhnjd->bhnij', q, k) * decay, 0.0)
    q_dec = q * jnp.exp(g)[..., None]
    k_tail = k * jnp.exp(g[..., -1:] - g)[..., None]
    g_last = jnp.exp(g[..., -1])

    def to_front(t):
        return jnp.moveaxis(t, 2, 0)

    xs = (to_front(u), to_front(w), to_front(attn), to_front(q_dec), to_front(k_tail),
          jnp.moveaxis(g_last, 2, 0))

    def step(state, inp):
        u_n, w_n, attn_n, qd_n, kt_n, gl_n = inp
        v_new = u_n - jnp.einsum('bhck,bhkv->bhcv', w_n, state)
        o = jnp.einsum('bhck,bhkv->bhcv', qd_n, state) + jnp.einsum('bhij,bhjv->bhiv', attn_n, v_new)
        state = state * gl_n[..., None, None] + jnp.einsum('bhck,bhcv->bhkv', kt_n, v_new)
        return state, o

    state0 = jnp.zeros((B, H, Dk, Dv), jnp.float32)
    _, o = lax.scan(step, state0, xs)
    o = o.transpose(1, 0, 3, 2, 4).reshape(B, S, H, Dv)
    return o.astype(dt)


def setup_inputs(seed: int = 0) -> dict:
    key = jax.random.key(seed)
    ks = jax.random.split(key, 24)
    f32 = jnp.float32
    L, D = DEPTH, D_MODEL

    def nrm(k, shape, std):
        return jax.random.normal(k, shape, f32) * std

    x = nrm(ks[0], (BATCH, SEQ, D), 1.0)
    c = nrm(ks[1], (BATCH, D), 1.0)
    norm_mix_w = 1.0 + nrm(ks[2], (L, D), 0.02)
    norm_mlp_w = 1.0 + nrm(ks[3], (L, D), 0.02)
    w_mod = nrm(ks[4], (L, D, N_MOD * D), 0.005)
    gate_offset = jnp.array([0.0, 0.0, 1.0, 0.0, 0.0, 1.0], f32)[None, :, None]
    b_mod = (nrm(ks[5], (L, N_MOD, D), 0.02) + gate_offset).reshape(L, N_MOD * D)
    w_in = nrm(ks[6], (L, D, IN_COLS), D ** -0.5)
    lru_conv_w = nrm(ks[7], (L, CONV_WIDTH, LRU_WIDTH), CONV_WIDTH ** -0.5)
    lru_conv_b = nrm(ks[8], (L, LRU_WIDTH), 0.01)
    lru_gate_a_w = nrm(ks[9], (L, LRU_BLOCKS, LRU_BLOCK, LRU_BLOCK), LRU_BLOCK ** -0.5)
    lru_gate_a_b = nrm(ks[10], (L, LRU_WIDTH), 0.01)
    lru_gate_x_w = nrm(ks[11], (L, LRU_BLOCKS, LRU_BLOCK, LRU_BLOCK), LRU_BLOCK ** -0.5)
    lru_gate_x_b = nrm(ks[12], (L, LRU_WIDTH), 0.01)
    u = jax.random.uniform(ks[13], (L, LRU_WIDTH), f32, 0.9, 0.999)
    p = u ** (1.0 / LRU_C)
    lru_lambda = jnp.log(p) - jnp.log1p(-p)
    lru_norm_w = 1.0 + nrm(ks[14], (L, LRU_WIDTH), 0.02)
    gdn_conv_w = nrm(ks[15], (L, CONV_WIDTH, 3 * GDN_WIDTH), CONV_WIDTH ** -0.5)
    gdn_a_log = jnp.log(jax.random.uniform(ks[16], (L, GDN_HEADS), f32, 1.0, 16.0))
    dt0 = jnp.exp(jax.random.uniform(ks[17], (L, GDN_HEADS), f32, math.log(1e-3), math.log(1e-1)))
    gdn_dt_bias = dt0 + jnp.log(-jnp.expm1(-dt0))
    gdn_norm_w = 1.0 + nrm(ks[18], (L, GDN_HEAD_DIM), 0.02)
    w_out = nrm(ks[19], (L, MIX_WIDTH, D), MIX_WIDTH ** -0.5)
    w_up = nrm(ks[20], (L, D, D_FF), D ** -0.5)
    w_down = nrm(ks[21], (L, D_FF, D), D_FF ** -0.5)
    final_norm_w = 1.0 + nrm(ks[22], (D,), 0.02)
    return {
        "x": x, "c": c,
        "norm_mix_w": norm_mix_w, "norm_mlp_w": norm_mlp_w,
        "w_mod": w_mod, "b_mod": b_mod,
        "w_in": w_in,
        "lru_conv_w": lru_conv_w, "lru_conv_b": lru_conv_b,
        "lru_gate_a_w": lru_gate_a_w, "lru_gate_a_b": lru_gate_a_b,
        "lru_gate_x_w": lru_gate_x_w, "lru_gate_x_b": lru_gate_x_b,
        "lru_lambda": lru_lambda, "lru_norm_w": lru_norm_w,
        "gdn_conv_w": gdn_conv_w, "gdn_a_log": gdn_a_log, "gdn_dt_bias": gdn_dt_bias,
        "gdn_norm_w": gdn_norm_w,
        "w_out": w_out, "w_up": w_up, "w_down": w_down,
        "final_norm_w": final_norm_w,
    }


def reference(x, c, norm_mix_w, norm_mlp_w, w_mod, b_mod, w_in,
              lru_conv_w, lru_conv_b, lru_gate_a_w, lru_gate_a_b,
              lru_gate_x_w, lru_gate_x_b, lru_lambda, lru_norm_w,
              gdn_conv_w, gdn_a_log, gdn_dt_bias, gdn_norm_w,
              w_out, w_up, w_down, final_norm_w):
    B, S, D = x.shape
    o_lx = 0
    o_ly = o_lx + LRU_WIDTH
    o_q = o_ly + LRU_WIDTH
    o_v_end = o_q + 3 * GDN_WIDTH
    o_z = o_v_end
    o_beta = o_z + GDN_WIDTH
    o_alpha = o_beta + GDN_HEADS
    c_act = jax.nn.silu(c)

    for l in range(DEPTH):
        mod = c_act @ w_mod[l] + b_mod[l]
        sh1, sc1, g1, sh2, sc2, g2 = jnp.split(mod[:, None, :], N_MOD, axis=-1)

        h = rms_norm(x, norm_mix_w[l]) * (1.0 + sc1) + sh1
        proj = h @ w_in[l]

        x_lru = proj[..., o_lx:o_ly]
        y_lru = proj[..., o_ly:o_q]
        xr = causal_depthwise_conv(x_lru, lru_conv_w[l]) + lru_conv_b[l]
        xb = xr.reshape(B, S, LRU_BLOCKS, LRU_BLOCK)
        r_pre = jnp.einsum('bsgi,gij->bsgj', xb, lru_gate_a_w[l]).reshape(B, S, LRU_WIDTH) + lru_gate_a_b[l]
        i_pre = jnp.einsum('bsgi,gij->bsgj', xb, lru_gate_x_w[l]).reshape(B, S, LRU_WIDTH) + lru_gate_x_b[l]
        h_lru = rg_lru(xr, r_pre, i_pre, lru_lambda[l])
        out_lru = rms_norm(h_lru * jax.nn.gelu(y_lru), lru_norm_w[l])

        qkv = jax.nn.silu(causal_depthwise_conv(proj[..., o_q:o_v_end], gdn_conv_w[l]))
        q, k, v = jnp.split(qkv.reshape(B, S, 3, GDN_HEADS, GDN_HEAD_DIM), 3, axis=2)
        q, k, v = q[:, :, 0], k[:, :, 0], v[:, :, 0]
        z = proj[..., o_z:o_beta].reshape(B, S, GDN_HEADS, GDN_HEAD_DIM)
        beta = jax.nn.sigmoid(proj[..., o_beta:o_alpha].astype(jnp.float32))
        g = -jnp.exp(gdn_a_log[l].astype(jnp.float32)) * jax.nn.softplus(
            proj[..., o_alpha:o_alpha + GDN_HEADS].astype(jnp.float32) + gdn_dt_bias[l].astype(jnp.float32))
        o = gated_delta_rule_chunked(q, k, v, g, beta)
        out_gdn = (rms_norm(o, gdn_norm_w[l]) * jax.nn.silu(z)).reshape(B, S, GDN_WIDTH)

        mix = jnp.concatenate([out_lru, out_gdn], axis=-1) @ w_out[l]
        x = x + g1 * mix

        h = rms_norm(x, norm_mlp_w[l]) * (1.0 + sc2) + sh2
        x = x + g2 * (jnp.square(jax.nn.relu(h @ w_up[l])) @ w_down[l])

    return rms_norm(x, final_norm_w)
```

```python
import numpy as np
from contextlib import ExitStack
import concourse.bass as bass
import concourse.mybir as mybir
from concourse.bass_utils import run_bass_kernel_spmd

F32 = mybir.dt.float32
BF16 = mybir.dt.bfloat16
ALU = mybir.AluOpType
AF = mybir.ActivationFunctionType

D = 1024
KC = 8
TT = 512
NL = 4
SEQ = 4096
BATCH = 4
NSLOT = 6
ENGS = ['pe', 'act', 'dve', 'pool', 'sp']
EPS = 1e-6

SM = {}
_off = 0


def _reg(name, w):
    global _off
    SM[name] = (_off, w)
    _off += w


_reg('cT', 8)
for _l in range(NL):
    for _n, _w in [('bmod', 48), ('nmw', 8), ('nlw', 8), ('lcw', 16), ('lcb', 4), ('gab', 4), ('gxb', 4),
                   ('lam', 4), ('lnw', 4), ('gcw', 48), ('gnw', 1), ('alog', 4), ('dtb', 4)]:
        _reg(f'{_n}{_l}', _w)
_reg('fnw', 8)
for _n in ['ident', 'tri', 'negL', 'negU', 'ones']:
    _reg(_n, 128)
NS = _off


class Tile:
    def __init__(self, name, ap):
        self.name = name
        self.ap = ap
        self.lw = None
        self.rd = {}
        self.rdd = []

    def __getitem__(self, k):
        return V(self, self.ap[k])

    @property
    def v(self):
        return V(self, self.ap)


class V:
    def __init__(self, tile, ap):
        self.tile = tile
        self.ap = ap

    def __getitem__(self, k):
        return V(self.tile, self.ap[k])

    def bc(self, shape):
        return V(self.tile, self.ap.broadcast_to(list(shape)))

    def un(self, ax):
        return V(self.tile, self.ap.unsqueeze(ax))

    def rr(self, pat, **kw):
        return V(self.tile, self.ap.rearrange(pat, **kw))

    def cast(self, dt):
        return V(self.tile, self.ap.bitcast(dt))


class Op:
    __slots__ = ('eng', 'fn', 'waits', 'sem', 'val', 'isdma', 'seq')


def _ap(x):
    return x.ap if isinstance(x, V) else x


class Sched:
    def __init__(self, nc, es):
        self.nc = nc
        self.es = es
        self.q = {e: [] for e in ENGS}
        self.cnt = {e: 0 for e in ENGS}
        self.seen = {e: {} for e in ENGS}
        self.semh = {}
        self.dcnt = {}
        for e in ENGS:
            self.semh[e] = es.enter_context(nc.semaphore('s_' + e))

    def dsem(self, name):
        if name not in self.semh:
            self.semh[name] = self.es.enter_context(self.nc.semaphore('d_' + name))
            self.dcnt[name] = 0
        return name

    def add(self, eng, fn, reads, writes, dma=None):
        op = Op()
        op.eng = eng
        op.fn = fn
        op.isdma = dma is not None
        deps = {}

        def need(d):
            if d is None:
                return
            if d.isdma:
                deps[d.sem] = max(deps.get(d.sem, 0), d.val)
            elif d.eng == eng and not op.isdma:
                if eng != 'pe' and self.cnt[eng] - d.seq < 2:
                    deps[d.sem] = max(deps.get(d.sem, 0), d.val)
            else:
                deps[d.sem] = max(deps.get(d.sem, 0), d.val)

        rt = [x.tile for x in reads if isinstance(x, V)]
        wt = [x.tile for x in writes if isinstance(x, V)]
        for t in rt:
            need(t.lw)
        for t in wt:
            need(t.lw)
            for r in t.rd.values():
                need(r)
            for r in t.rdd:
                need(r)
        waits = []
        sn = self.seen[eng]
        for sem, val in deps.items():
            if sn.get(sem, 0) >= val:
                continue
            sn[sem] = val
            waits.append((sem, val))
        op.waits = waits
        if op.isdma:
            self.dsem(dma)
            self.dcnt[dma] += 16
            op.sem = dma
            op.val = self.dcnt[dma]
            op.seq = None
        else:
            self.cnt[eng] += 1
            op.sem = eng
            op.val = self.cnt[eng]
            op.seq = op.val
        for t in wt:
            t.lw = op
            t.rd = {}
            t.rdd = []
        wset = set(id(t) for t in wt)
        for t in rt:
            if id(t) in wset:
                continue
            if op.isdma:
                t.rdd.append(op)
            else:
                t.rd[eng] = op
        self.q[eng].append(op)
        return op

    def mm(self, items):
        reads = []
        writes = []
        for (o, l, r, st, sp) in items:
            reads += [l, r]
            writes.append(o)

        def fn(e):
            ins = None
            for (o, l, r, st, sp) in items:
                ins = e.matmul(_ap(o), _ap(l), _ap(r), start=st, stop=sp)
            return ins
        self.add('pe', fn, reads, writes)

    def tr(self, items):
        reads = []
        writes = []
        for (o, i, idn) in items:
            reads += [i, idn]
            writes.append(o)

        def fn(e):
            ins = None
            for (o, i, idn) in items:
                ins = e.transpose(_ap(o), _ap(i), _ap(idn))
            return ins
        self.add('pe', fn, reads, writes)

    def act(self, out, in_, func, bias=None, scale=None):
        reads = [in_]
        kw = {}
        if bias is not None:
            kw['bias'] = _ap(bias)
            reads.append(bias)
        if scale is not None:
            kw['scale'] = _ap(scale)
            reads.append(scale)
        self.add('act', lambda e: e.activation(_ap(out), _ap(in_), func, **kw), reads, [out])

    def tt(self, eng, out, in0, in1, op):
        self.add(eng, lambda e: e.tensor_tensor(_ap(out), _ap(in0), _ap(in1), op), [in0, in1], [out])

    def ts(self, eng, out, in0, s1, s2, op0, op1=None):
        kw = {}
        if op1 is not None:
            kw['op1'] = op1
        self.add(eng, lambda e: e.tensor_scalar(_ap(out), _ap(in0), _ap(s1), _ap(s2) if s2 is not None else None,
                                                op0, **kw), [in0, s1, s2], [out])

    def stt(self, out, in0, scalar, in1, op0, op1):
        self.add('dve', lambda e: e.scalar_tensor_tensor(_ap(out), _ap(in0), _ap(scalar), _ap(in1), op0, op1),
                 [in0, scalar, in1], [out])

    def scan(self, out, d0, d1, init):
        self.add('dve', lambda e: e.tensor_tensor_scan(_ap(out), _ap(d0), _ap(d1), _ap(init), ALU.mult, ALU.add),
                 [d0, d1, init], [out])

    def cp(self, eng, out, in_):
        if eng == 'act':
            self.add('act', lambda e: e.copy(_ap(out), _ap(in_)), [in_], [out])
        else:
            self.add(eng, lambda e: e.tensor_copy(_ap(out), _ap(in_)), [in_], [out])

    def memset(self, eng, out, val):
        self.add(eng, lambda e: e.memset(_ap(out), val), [], [out])

    def dma(self, q, out, in_, sem):
        self.add(q, lambda e: e.dma_start(out=_ap(out), in_=_ap(in_)), [in_], [out], dma=sem)

    def rsq(self, out, in_, eps):
        self.act(out, in_, AF.Ln, bias=float(eps))
        self.act(out, out, AF.Exp, scale=-0.5)

    def finish(self):
        op = Op()
        op.eng = 'sp'
        op.fn = None
        op.isdma = False
        op.waits = [(k, v) for k, v in self.dcnt.items()]
        self.q['sp'].append(op)

    def emit(self):
        nc = self.nc
        with nc.Block() as blk:
            def mk(en):
                def body(e):
                    for op in self.q[en]:
                        for (sem, val) in op.waits:
                            e.wait_ge(self.semh[sem], val)
                        if op.fn is None:
                            continue
                        ins = op.fn(e)
                        ins.then_inc(self.semh[op.sem], 16 if op.isdma else 1)
                return body
            blk.tensor(mk('pe'))
            blk.scalar(mk('act'))
            blk.vector(mk('dve'))
            blk.gpsimd(mk('pool'))
            blk.sync(mk('sp'))


def build_nc(n_tiles, layers, final_norm=True, gelu_mode=0, stage=99):
    ntok = n_tiles * TT
    nc = bass.Bass("TRN2", target_bir_lowering=False)
    x_d = nc.dram_tensor("xT", [D, ntok], F32, kind="ExternalInput").ap()
    wst_d = nc.dram_tensor("wst", [NL, 24, 128, 4096], F32, kind="ExternalInput").ap()
    wmod_d = nc.dram_tensor("wmod", [NL, 24, 128, 2048], F32, kind="ExternalInput").ap()
    sm_d = nc.dram_tensor("small", [128, NS], F32, kind="ExternalInput").ap()
    gw_d = nc.dram_tensor("gatew", [128, NL * 2 * 4 * 128], F32, kind="ExternalInput").ap()
    wt_d = nc.dram_tensor("wtail", [128, NL * 64], F32, kind="ExternalInput").ap()
    mk_d = nc.dram_tensor("masks", [128, 14 * 128], F32, kind="ExternalInput").ap()
    out_d = nc.dram_tensor("outT", [D, ntok], F32, kind="ExternalOutput").ap()

    es = ExitStack()
    with es:
        S = Sched(nc, es)

        def sb(name, shape, dt=F32):
            return Tile(name, es.enter_context(nc.sbuf_tensor(name, list(shape), dt))[:])

        def ps(name, shape, dt=F32):
            return Tile(name, es.enter_context(nc.psum_tensor(name, list(shape), dt))[:])

        xT = sb('xT_sb', [128, KC, TT])
        ring = [sb(f'ring{i}', [128, 4096], BF16) for i in range(NSLOT)]
        sm = sb('sm', [128, NS])
        gatew = sb('gatew_sb', [128, NL, 2, 4, 128], BF16)
        wtail = sb('wtail_sb', [128, NL, KC, 8], BF16)
        ident_bf = sb('ident_bf', [128, 128], BF16)
        masks = sb('masks_sb', [128, 14, 128], BF16)
        ones_bf = sb('ones_bf', [128, 128], BF16)
        scT = sb('scT', [128, 8])
        modT = [sb(f'modT{l}', [128, 48]) for l in range(NL)]
        gam1 = [sb(f'gam1_{l}', [128, 8]) for l in range(NL)]
        gam2 = [sb(f'gam2_{l}', [128, 8]) for l in range(NL)]
        ccol = [sb(f'ccol{l}', [128, 4]) for l in range(NL)]
        c2col = [sb(f'c2col{l}', [128, 4]) for l in range(NL)]
        lnws = [sb(f'lnws{l}', [128, 4]) for l in range(NL)]
        gnws = [sb(f'gnws{l}', [128, 1]) for l in range(NL)]
        negA = [sb(f'negA{l}', [128, 4]) for l in range(NL)]
        fnws = sb('fnws', [128, 8])
        Sst = [sb(f'S{l}', [128, 4, 128]) for l in range(NL)]
        Sbf = [sb(f'Sbf{l}', [128, 4, 128], BF16) for l in range(NL)]
        hst = [sb(f'hst{l}', [128, 4]) for l in range(NL)]
        halo = [sb(f'halo{l}', [128, 16, 3]) for l in range(NL)]
        hT = sb('hT', [128, KC, TT], BF16)
        tmpr = [sb(f'tmpr{i}', [128, TT]) for i in range(2)]
        pre = [sb(f'pre{i}', [128, TT + 3]) for i in range(2)]
        Fs = [sb(f'F{i}', [128, TT]) for i in range(7)]
        hg = sb('hg', [128, 4, TT])
        gy = sb('gy', [128, 4, TT], BF16)
        xrb = sb('xrb', [128, TT], BF16)
        sqh = sb('sqh', [128, TT], BF16)
        qkv = sb('qkv', [128, 12, TT], BF16)
        sz = sb('sz', [128, 4, TT], BF16)
        mixT = sb('mixT', [128, 8, TT], BF16)
        sqb = sb('sqb', [128, 8, 128], BF16)
        Bn = {n: sb('B_' + n, [128, 4, 128], BF16) for n in
              ['qn', 'kn', 'kbg', 'kt', 'vb', 'L0', 'U0', 'Oa', 'Ob', 'Da0', 'Da1', 'Db0', 'Db1', 'Ya', 'Yb', 'wT', 'qd', 'attnT', 'vnew', 'osq']}
        gsm = {n: sb('g_' + n, [128, 4, 4]) for n in ['beta', 'xg', 'ax', 'e', 'l1', 'sp', 'g']}
        csm = {n: sb('c_' + n, [128, 4]) for n in ['gcol', 'egcol', 'dk', 'ekt', 'bg']}
        hidr = [sb(f'hidr{i}', [128, TT], BF16) for i in range(2)]
        hid = sb('hid', [128, 4, TT], BF16)
        lsm = {n: sb('l_' + n, [128, 4]) for n in ['e', 'l1']}
        pbig = [ps('pbig0', [128, 512]), ps('pbig1', [128, 512])]
        pstat = ps('pstat', [128, 512])
        pg = [ps('pg0', [128, 512]), ps('pg1', [128, 512]), ps('pg2', [128, 512])]
        ptr = ps('ptr', [128, 1024], BF16)
        psm = ps('psm', [128, 512])

        def smv(name):
            o, w = SM[name]
            return sm[:, o:o + w]

        identf = smv('ident')
        tri = smv('tri')
        negL = smv('negL')
        negU = smv('negU')
        onesf = smv('ones')

        pieces = []
        for l in layers:
            for q in range(24):
                pieces.append((wmod_d[l, q], True))
        for s in range(n_tiles):
            for l in layers:
                for g in range(24):
                    pieces.append((wst_d[l, g], False))
        rstate = {'use': 0, 'iss': 0}

        def next_slot():
            i = rstate['use']
            while rstate['iss'] <= min(i + NSLOT - 2, len(pieces) - 1):
                j = rstate['iss']
                src, f32v = pieces[j]
                slot = ring[j % NSLOT]
                if f32v:
                    S.dma('sp', slot.v.cast(F32), src, f'ringh{j % NSLOT}')
                else:
                    S.dma('pool', slot.v, src, f'rings{j % NSLOT}')
                rstate['iss'] += 1
            rstate['use'] += 1
            return ring[i % NSLOT]

        S.dma('sp', sm.v, sm_d, 'const')
        S.dma('pool', gatew.v.rr('p l g c m -> p (l g c m)'), gw_d, 'const2')
        S.dma('pool', wtail.v.rr('p l k j -> p (l k j)'), wt_d, 'const3')
        S.dma('pool', masks.v.rr('p a b -> p (a b)'), mk_d, 'const4')
        S.cp('dve', ident_bf.v, identf)
        S.cp('dve', ones_bf.v, onesf)
        S.act(scT.v, smv('cT'), AF.Silu)
        for l in range(NL):
            S.memset('pool', Sst[l].v, 0.0)
            S.memset('pool', Sbf[l].v, 0.0)
            S.memset('pool', hst[l].v, 0.0)
            S.memset('pool', halo[l].v, 0.0)
        for l in layers:
            for q in range(24):
                slot = next_slot()
                wv = slot.v.cast(F32).rr('p (k n) -> p k n', k=KC)
                items = []
                for j in range(2):
                    col = q * 2 + j
                    for kc in range(KC):
                        items.append((psm[:, col:col + 1], wv[:, kc, j * 128:(j + 1) * 128], scT[:, kc:kc + 1],
                                      kc == 0, kc == KC - 1))
                S.mm(items)
            S.tt('dve', modT[l].v, psm[:, 0:48], smv(f'bmod{l}'), ALU.add)
            S.stt(gam1[l].v, modT[l][:, 8:16], 1.0, smv(f'nmw{l}'), ALU.add, ALU.mult)
            S.ts('dve', gam1[l].v, gam1[l].v, 32.0, None, ALU.mult)
            S.stt(gam2[l].v, modT[l][:, 32:40], 1.0, smv(f'nlw{l}'), ALU.add, ALU.mult)
            S.ts('dve', gam2[l].v, gam2[l].v, 32.0, None, ALU.mult)
            S.act(lsm['e'].v, smv(f'lam{l}'), AF.Exp, scale=-1.0)
            S.act(lsm['l1'].v, lsm['e'].v, AF.Ln, bias=1.0)
            S.ts('dve', ccol[l].v, lsm['l1'].v, -8.0, None, ALU.mult)
            S.ts('dve', c2col[l].v, lsm['l1'].v, -16.0, None, ALU.mult)
            S.ts('dve', lnws[l].v, smv(f'lnw{l}'), float(np.sqrt(512.0)), None, ALU.mult)
            S.ts('dve', gnws[l].v, smv(f'gnw{l}'), float(np.sqrt(128.0)), None, ALU.mult)
            S.act(negA[l].v, smv(f'alog{l}'), AF.Exp)
            S.ts('dve', negA[l].v, negA[l].v, -1.0, None, ALU.mult)
        S.ts('dve', fnws.v, smv('fnw'), 32.0, None, ALU.mult)

        bigi = {'i': 0}
        if stage == 0:
            S.dma('sp', xT.v, x_d.rearrange('(k p) n -> p k n', p=128)[:, :, 0:TT], 'xload')
            S.cp('dve', xT[:, 0, 0:48], modT[layers[0]].v)
            S.dma('sp', out_d.rearrange('(k p) n -> p k n', p=128)[:, :, 0:TT], xT.v, 'ostore')
            S.finish()
            S.emit()
            return nc

        def nbig():
            b = pbig[bigi['i'] % 2]
            bigi['i'] += 1
            return b

        def modnorm(gam, shcols, dst):
            S.act(hT.v, xT.v, AF.Square)
            S.mm([(pstat.v, ones_bf.v, hT[:, kc, :], kc == 0, kc == KC - 1) for kc in range(KC)])
            rs = Fs[6]
            S.rsq(rs.v, pstat.v, D * EPS)
            for kc in range(KC):
                t = tmpr[kc % 2]
                S.stt(t.v, xT[:, kc, :], gam[:, kc:kc + 1], rs.v, ALU.mult, ALU.mult)
                if shcols is None:
                    S.cp('act', dst[:, kc, :], t.v)
                else:
                    S.act(dst[:, kc, :], t.v, AF.Identity, bias=shcols[:, kc:kc + 1])

        for s in range(n_tiles):
            S.dma('sp', xT.v, x_d.rearrange('(k p) n -> p k n', p=128)[:, :, s * TT:(s + 1) * TT], 'xload')
            for l in layers:
                modnorm(gam1[l], modT[l][:, 0:8], hT)

                if stage == 1:
                    S.dma('sp', out_d.rearrange('(k p) n -> p k n', p=128)[:, :, 0:TT], xT.v, 'ostore')
                    S.finish()
                    S.emit()
                    return nc
                wslots = {}

                def wcol(cc):
                    pidx = {1: 0, 0: 1}.get(cc // 4, cc // 4)
                    if pidx not in wslots:
                        assert len(wslots) == pidx
                        wslots[pidx] = next_slot()
                    sl = wslots[pidx].v.rr('p (k n) -> p k n', k=KC)
                    return [sl[:, kc, (cc % 4) * 128:(cc % 4 + 1) * 128] for kc in range(KC)]

                def proj(cc):
                    pb = nbig()
                    wc = wcol(cc)
                    S.mm([(pb.v, wc[kc], hT[:, kc, :], kc == 0, kc == KC - 1) for kc in range(KC)])
                    return pb

                def conv(pb, ci, wname, widx, bias):
                    pr = pre[ci % 2]
                    S.cp('act', pr[:, 3:TT + 3], pb.v)
                    S.cp('pool', pr[:, 0:3], halo[l][:, ci, :])
                    S.cp('pool', halo[l][:, ci, :], pr[:, TT:TT + 3])
                    o, w = SM[wname]
                    wv = sm[:, o + widx * 4:o + widx * 4 + 4]
                    acc = Fs[6]
                    if bias is not None:
                        S.ts('dve', acc.v, pr[:, 0:TT], wv[:, 0:1], bias, ALU.mult, ALU.add)
                    else:
                        S.ts('dve', acc.v, pr[:, 0:TT], wv[:, 0:1], None, ALU.mult)
                    for k in range(1, 4):
                        S.stt(acc.v, pr[:, k:TT + k], wv[:, k:k + 1], acc.v, ALU.mult, ALU.add)
                    return acc

                for c in range(4):
                    pb = proj(4 + c)
                    if gelu_mode == 0:
                        S.act(gy[:, c, :], pb.v, AF.Gelu_apprx_tanh)
                    else:
                        y = Fs[0]
                        S.cp('act', y.v, pb.v)
                        S.tt('dve', Fs[1].v, y.v, y.v, ALU.mult)
                        S.ts('dve', Fs[1].v, Fs[1].v, 0.044715, 1.0, ALU.mult, ALU.add)
                        S.tt('dve', Fs[1].v, Fs[1].v, y.v, ALU.mult)
                        S.act(Fs[1].v, Fs[1].v, AF.Sigmoid, scale=float(2.0 * np.sqrt(2.0 / np.pi)))
                        S.tt('dve', gy[:, c, :], Fs[1].v, y.v, ALU.mult)

                if stage == 2:
                    S.dma('sp', out_d.rearrange('(k p) n -> p k n', p=128)[:, :, 0:TT], xT.v, 'ostore')
                    S.finish()
                    S.emit()
                    return nc
                for c in range(4):
                    pb = proj(c)
                    lcb = smv(f'lcb{l}')
                    xr = conv(pb, c, f'lcw{l}', c, lcb[:, c:c + 1])
                    S.cp('act', xrb.v, xr.v)
                    pr_ = nbig()
                    pi_ = nbig()
                    S.mm([(pr_.v, gatew[:, l, 0, c, :], xrb.v, True, True)])
                    S.mm([(pi_.v, gatew[:, l, 1, c, :], xrb.v, True, True)])
                    r, ig, a, a2, b, h = Fs[0], Fs[1], Fs[2], Fs[3], Fs[4], Fs[5]
                    S.act(r.v, pr_.v, AF.Sigmoid, bias=smv(f'gab{l}')[:, c:c + 1])
                    S.act(ig.v, pi_.v, AF.Sigmoid, bias=smv(f'gxb{l}')[:, c:c + 1])
                    S.act(a.v, r.v, AF.Exp, scale=ccol[l][:, c:c + 1])
                    S.act(a2.v, r.v, AF.Exp, scale=c2col[l][:, c:c + 1])
                    S.ts('dve', a2.v, a2.v, -1.0, 1.0, ALU.mult, ALU.add)
                    S.ts('dve', a2.v, a2.v, 1e-12, None, ALU.max)
                    S.act(a2.v, a2.v, AF.Sqrt)
                    S.tt('dve', b.v, ig.v, xr.v, ALU.mult)
                    S.tt('dve', b.v, b.v, a2.v, ALU.mult)
                    S.scan(h.v, a.v, b.v, hst[l][:, c:c + 1])
                    S.cp('pool', hst[l][:, c:c + 1], h[:, TT - 1:TT])
                    S.tt('dve', hg[:, c, :], h.v, gy[:, c, :], ALU.mult)
                    S.act(sqh.v, hg[:, c, :], AF.Square)
                    S.mm([(pstat.v, ones_bf.v, sqh.v, c == 0, c == 3)])
                rsl = Fs[0]
                S.rsq(rsl.v, pstat.v, 512 * EPS)
                for c in range(4):
                    S.stt(mixT[:, c, :], hg[:, c, :], lnws[l][:, c:c + 1], rsl.v, ALU.mult, ALU.mult)

                if stage == 3:
                    S.dma('sp', out_d.rearrange('(k p) n -> p k n', p=128)[:, :, 0:TT], xT.v, 'ostore')
                    S.finish()
                    S.emit()
                    return nc
                for c in range(12):
                    pb = proj(8 + c)
                    cv = conv(pb, 4 + c, f'gcw{l}', c, None)
                    S.act(qkv[:, c, :], cv.v, AF.Silu)
                for c in range(4):
                    pb = proj(20 + c)
                    S.act(sz[:, c, :], pb.v, AF.Silu)
                for n in range(4):
                    S.mm([(psm[:, n * 8:(n + 1) * 8], hT[:, kc, n * 128:(n + 1) * 128], wtail[:, l, kc, :],
                           kc == 0, kc == KC - 1) for kc in range(KC)])
                ab = psm[:, 0:32].rr('p (n j) -> p n j', j=8)
                S.act(gsm['beta'].v, ab[:, :, 0:4], AF.Sigmoid)
                S.tt('dve', gsm['xg'].v, ab[:, :, 4:8], smv(f'dtb{l}').un(1).bc([128, 4, 4]), ALU.add)
                S.act(gsm['ax'].v, gsm['xg'].v, AF.Abs)
                S.act(gsm['e'].v, gsm['ax'].v, AF.Exp, scale=-1.0)
                S.act(gsm['l1'].v, gsm['e'].v, AF.Ln, bias=1.0)
                S.stt(gsm['sp'].v, gsm['xg'].v, 0.0, gsm['l1'].v, ALU.max, ALU.add)
                S.tt('dve', gsm['g'].v, gsm['sp'].v, negA[l].v.un(1).bc([128, 4, 4]), ALU.mult)


                if stage == 4:
                    S.dma('sp', out_d.rearrange('(k p) n -> p k n', p=128)[:, :, 0:TT], xT.v, 'ostore')
                    S.finish()
                    S.emit()
                    return nc
                B4 = [128, 4, 128]
                for n in range(4):
                    tsl = slice(n * 128, (n + 1) * 128)
                    qT_ = qkv[:, 0:4, tsl]
                    kT_ = qkv[:, 4:8, tsl]
                    vT_ = qkv[:, 8:12, tsl]
                    beta_n = gsm['beta'][:, n, :]
                    g_n = gsm['g'][:, n, :]
                    S.act(sqb.v, qkv[:, 0:8, tsl], AF.Square)
                    S.mm([(pg[0].v, ones_bf.v, sqb[:, 0:4, :], True, True)])
                    S.mm([(pg[1].v, ones_bf.v, sqb[:, 4:8, :], True, True)])
                    rq, rk = Fs[0], Fs[1]
                    S.rsq(rq.v, pg[0].v, 1e-6)
                    S.rsq(rk.v, pg[1].v, 1e-6)
                    rq3 = rq.v.rr('p (h t) -> p h t', h=4)
                    rk3 = rk.v.rr('p (h t) -> p h t', h=4)
                    S.stt(Bn['qn'].v, qT_, float(128.0 ** -0.5), rq3, ALU.mult, ALU.mult)
                    S.tt('dve', Bn['kn'].v, kT_, rk3, ALU.mult)

                    if stage == 41:
                        S.dma('sp', out_d.rearrange('(k p) n -> p k n', p=128)[:, :, 0:TT], xT.v, 'ostore')
                        S.finish()
                        S.emit()
                        return nc
                    R = Fs[0].v.rr('p (h t) -> p h t', h=4)
                    S.tt('dve', R, tri.un(1).bc(B4), g_n.un(2).bc(B4), ALU.mult)
                    S.mm([(pg[2].v, onesf, Fs[0].v, True, True)])
                    S.mm([(psm[:, 64:68], tri, g_n, True, True)])
                    S.cp('act', csm['gcol'].v, psm[:, 64:68])
                    Grow = pg[2].v.rr('p (h t) -> p h t', h=4)
                    t3 = Fs[1].v.rr('p (h t) -> p h t', h=4)
                    S.tt('dve', t3, csm['gcol'].v.un(2).bc(B4), Grow, ALU.subtract)
                    aL = Fs[2].v.rr('p (h t) -> p h t', h=4)
                    aU = Fs[3].v.rr('p (h t) -> p h t', h=4)
                    S.tt('pool', aL, t3, negL.un(1).bc(B4), ALU.add)
                    S.tt('pool', aU, negU.un(1).bc(B4), t3, ALU.subtract)
                    S.act(Fs[2].v, Fs[2].v, AF.Exp)
                    S.act(Fs[3].v, Fs[3].v, AF.Exp)
                    S.act(Fs[4].v, pg[2].v, AF.Exp)
                    Eg = Fs[4].v.rr('p (h t) -> p h t', h=4)
                    S.act(csm['egcol'].v, csm['gcol'].v, AF.Exp)
                    S.tt('dve', csm['dk'].v, Grow[:, :, 127], csm['gcol'].v, ALU.subtract)
                    S.act(csm['ekt'].v, csm['dk'].v, AF.Exp)
                    S.tt('dve', csm['bg'].v, beta_n, csm['egcol'].v, ALU.mult)
                    S.tt('pool', aL, aL, beta_n.un(2).bc(B4), ALU.mult)

                    if stage == 42:
                        S.dma('sp', out_d.rearrange('(k p) n -> p k n', p=128)[:, :, 0:TT], xT.v, 'ostore')
                        S.finish()
                        S.emit()
                        return nc
                    ptk = ptr[:, 0:512].rr('p (h t) -> p h t', h=4)
                    ptv = ptr[:, 512:1024].rr('p (h t) -> p h t', h=4)
                    S.tr([(ptk[:, h, :], Bn['kn'][:, h, :], ident_bf.v) for h in range(4)])
                    S.tr([(ptv[:, h, :], vT_[:, h, :], ident_bf.v) for h in range(4)])
                    S.tt('dve', Bn['kbg'].v, ptk, csm['bg'].v.un(2).bc(B4), ALU.mult)
                    S.tt('dve', Bn['kt'].v, ptk, csm['ekt'].v.un(2).bc(B4), ALU.mult)
                    S.tt('dve', Bn['vb'].v, ptv, beta_n.un(2).bc(B4), ALU.mult)

                    if stage == 43:
                        S.dma('sp', out_d.rearrange('(k p) n -> p k n', p=128)[:, :, 0:TT], xT.v, 'ostore')
                        S.finish()
                        S.emit()
                        return nc
                    pgr = pg[0].v.rr('p (h t) -> p h t', h=4)
                    pqk = pg[1].v.rr('p (h t) -> p h t', h=4)
                    S.mm([(pgr[:, h, :], Bn['kn'][:, h, :], Bn['kn'][:, h, :], True, True) for h in range(4)])
                    S.mm([(pqk[:, h, :], Bn['kn'][:, h, :], Bn['qn'][:, h, :], True, True) for h in range(4)])
                    S.tt('dve', Bn['L0'].v, pgr, aL, ALU.mult)
                    S.tt('dve', Bn['attnT'].v, pqk, aU, ALU.mult)
                    S.tt('pool', Bn['qd'].v, Bn['qn'].v, Eg, ALU.mult)

                    if stage == 44:
                        S.dma('sp', out_d.rearrange('(k p) n -> p k n', p=128)[:, :, 0:TT], xT.v, 'ostore')
                        S.finish()
                        S.emit()
                        return nc
                    S.tr([(ptk[:, h, :], Bn['L0'][:, h, :], ident_bf.v) for h in range(4)])
                    S.cp('dve', Bn['U0'].v, ptk)
                    idb = ident_bf.v.un(1).bc(B4)
                    pY = pg[0].v.rr('p (h t) -> p h t', h=4)
                    pYp = pg[1].v.rr('p (h t) -> p h t', h=4)
                    pZ = pg[2].v.rr('p (h t) -> p h t', h=4)
                    S.tt('pool', Bn['Oa'].v, Bn['U0'].v, masks[:, 0, :].un(1).bc(B4), ALU.mult)
                    S.tt('pool', Bn['Ob'].v, Bn['L0'].v, masks[:, 7, :].un(1).bc(B4), ALU.mult)
                    S.tt('pool', Bn['Da0'].v, idb, Bn['Oa'].v, ALU.subtract)
                    S.tt('pool', Bn['Db0'].v, idb, Bn['Ob'].v, ALU.subtract)
                    Dc, Dpc = 'Da0', 'Db0'
                    for lev in range(1, 7):
                        Dn, Dpn = ('Da1', 'Db1') if lev % 2 == 1 else ('Da0', 'Db0')
                        S.tt('pool', Bn['Oa'].v, Bn['U0'].v, masks[:, lev, :].un(1).bc(B4), ALU.mult)
                        S.tt('pool', Bn['Ob'].v, Bn['L0'].v, masks[:, 7 + lev, :].un(1).bc(B4), ALU.mult)
                        S.mm([(pY[:, h, :], Bn['Ob'][:, h, :], Bn[Dc][:, h, :], True, True) for h in range(4)])
                        S.cp('act', Bn['Ya'].v, pY)
                        if lev < 6:
                            S.mm([(pYp[:, h, :], Bn['Oa'][:, h, :], Bn[Dpc][:, h, :], True, True) for h in range(4)])
                            S.cp('dve', Bn['Yb'].v, pYp)
                        S.mm([(pZ[:, h, :], Bn[Dpc][:, h, :], Bn['Ya'][:, h, :], True, True) for h in range(4)])
                        S.tt('dve', Bn[Dn].v, Bn[Dc].v, pZ, ALU.subtract)
                        if lev < 6:
                            S.mm([(pY[:, h, :], Bn[Dc][:, h, :], Bn['Yb'][:, h, :], True, True) for h in range(4)])
                            S.tt('dve', Bn[Dpn].v, Bn[Dpc].v, pY, ALU.subtract)
                        Dc, Dpc = Dn, Dpn
                    Mc = Dc
                    Mf = Bn[Mc]

                    if stage == 45:
                        S.dma('sp', out_d.rearrange('(k p) n -> p k n', p=128)[:, :, 0:TT], xT.v, 'ostore')
                        S.finish()
                        S.emit()
                        return nc
                    pu = pg[0].v.rr('p (h t) -> p h t', h=4)
                    pw = pg[1].v.rr('p (h t) -> p h t', h=4)
                    S.mm([(pu[:, h, :], Mf[:, h, :], Bn['vb'][:, h, :], True, True) for h in range(4)])
                    S.mm([(pw[:, h, :], Bn['kbg'][:, h, :], Mf[:, h, :], True, True) for h in range(4)])
                    S.cp('act', Fs[5].v, pg[0].v)
                    S.cp('dve', Bn['wT'].v, pw)

                    if stage == 46:
                        S.dma('sp', out_d.rearrange('(k p) n -> p k n', p=128)[:, :, 0:TT], xT.v, 'ostore')
                        S.finish()
                        S.emit()
                        return nc
                    pws = pg[2].v.rr('p (h t) -> p h t', h=4)
                    S.mm([(pws[:, h, :], Bn['wT'][:, h, :], Sbf[l][:, h, :], True, True) for h in range(4)])
                    S.tt('dve', Bn['vnew'].v, Fs[5].v.rr('p (h t) -> p h t', h=4), pws, ALU.subtract)
                    po = pg[0].v.rr('p (h t) -> p h t', h=4)
                    items = []
                    for h in range(4):
                        items.append((po[:, h, :], Sbf[l][:, h, :], Bn['qd'][:, h, :], True, False))
                        items.append((po[:, h, :], Bn['vnew'][:, h, :], Bn['attnT'][:, h, :], False, True))
                    S.mm(items)
                    pS = pg[1].v.rr('p (h t) -> p h t', h=4)
                    S.mm([(pS[:, h, :], Bn['kt'][:, h, :], Bn['vnew'][:, h, :], True, True) for h in range(4)])
                    S.tt('pool', Sst[l].v, Sst[l].v, Eg[:, :, 127].un(2).bc(B4), ALU.mult)
                    S.tt('dve', Sst[l].v, Sst[l].v, pS, ALU.add)
                    S.cp('act', Sbf[l].v, Sst[l].v)

                    if stage == 47:
                        S.dma('sp', out_d.rearrange('(k p) n -> p k n', p=128)[:, :, 0:TT], xT.v, 'ostore')
                        S.finish()
                        S.emit()
                        return nc
                    S.act(Bn['osq'].v, po, AF.Square)
                    S.cp('act', Fs[0].v, pg[0].v)
                    S.mm([(pg[2].v, ones_bf.v, Bn['osq'].v, True, True)])
                    S.rsq(Fs[1].v, pg[2].v, 128 * EPS)
                    S.tt('dve', Fs[6].v, Fs[0].v, Fs[1].v, ALU.mult)
                    S.stt(mixT[:, 4:8, tsl], Fs[6].v.rr('p (h t) -> p h t', h=4), gnws[l][:, 0:1],
                          sz[:, 0:4, tsl], ALU.mult, ALU.mult)


                if stage == 5:
                    S.dma('sp', out_d.rearrange('(k p) n -> p k n', p=128)[:, :, 0:TT], xT.v, 'ostore')
                    S.finish()
                    S.emit()
                    return nc
                oslots = [next_slot() for g in range(2)]
                for fc in range(8):
                    sl = oslots[fc // 4].v.rr('p (k n) -> p k n', k=KC)
                    pb = nbig()
                    S.mm([(pb.v, sl[:, kc, (fc % 4) * 128:(fc % 4 + 1) * 128], mixT[:, kc, :], kc == 0, kc == KC - 1)
                          for kc in range(KC)])
                    S.stt(xT[:, fc, :], pb.v, modT[l][:, 16 + fc:17 + fc], xT[:, fc, :], ALU.mult, ALU.add)


                if stage == 6:
                    S.dma('sp', out_d.rearrange('(k p) n -> p k n', p=128)[:, :, 0:TT], xT.v, 'ostore')
                    S.finish()
                    S.emit()
                    return nc
                modnorm(gam2[l], modT[l][:, 24:32], hT)
                for hgp in range(8):
                    su = next_slot().v.rr('p (k n) -> p k n', k=KC)
                    sd = next_slot().v.rr('p (c f) -> p c f', c=4)
                    for hc in range(4):
                        pb = nbig()
                        S.mm([(pb.v, su[:, kc, hc * 128:(hc + 1) * 128], hT[:, kc, :], kc == 0, kc == KC - 1)
                              for kc in range(KC)])
                        hr = hidr[hc % 2]
                        S.act(hr.v, pb.v, AF.Relu)
                        S.tt('pool', hid[:, hc, :], hr.v, hr.v, ALU.mult)
                    for fc in range(8):
                        pb = nbig()
                        S.mm([(pb.v, sd[:, hc, fc * 128:(fc + 1) * 128], hid[:, hc, :], hc == 0, hc == 3)
                              for hc in range(4)])
                        S.stt(xT[:, fc, :], pb.v, modT[l][:, 40 + fc:41 + fc], xT[:, fc, :], ALU.mult, ALU.add)

            if final_norm:
                S.act(hT.v, xT.v, AF.Square)
                S.mm([(pstat.v, ones_bf.v, hT[:, kc, :], kc == 0, kc == KC - 1) for kc in range(KC)])
                S.rsq(Fs[6].v, pstat.v, D * EPS)
                for kc in range(KC):
                    S.stt(xT[:, kc, :], xT[:, kc, :], fnws[:, kc:kc + 1], Fs[6].v, ALU.mult, ALU.mult)
            S.dma('sp', out_d.rearrange('(k p) n -> p k n', p=128)[:, :, s * TT:(s + 1) * TT], xT.v, 'ostore')

        S.finish()
        S.emit()
    return nc


def _kpiece(W, c0, ncols):
    return np.ascontiguousarray(W[:, c0:c0 + ncols].reshape(KC, 128, ncols).transpose(1, 0, 2).reshape(128, KC * ncols))


def _col(v):
    v = np.asarray(v, np.float32).reshape(-1)
    return v.reshape(-1, 128).T


def prep_shared(inp):
    wst = np.empty((NL, 24, 128, 4096), np.float32)
    wmod = np.empty((NL, 24, 128, 2048), np.float32)
    gatew = np.zeros((128, NL, 2, 4, 128), np.float32)
    wtail = np.empty((128, NL, KC, 8), np.float32)
    for l in range(NL):
        for g in range(6):
            wst[l, g] = _kpiece(inp['w_in'][l], 512 * {0: 1, 1: 0}.get(g, g), 512)
        for g in range(2):
            wst[l, 6 + g] = _kpiece(inp['w_out'][l], 512 * g, 512)
        for hg in range(8):
            wst[l, 8 + 2 * hg] = _kpiece(inp['w_up'][l], 512 * hg, 512)
            wd = inp['w_down'][l][512 * hg:512 * hg + 512, :]
            wst[l, 9 + 2 * hg] = wd.reshape(4, 128, 1024).transpose(1, 0, 2).reshape(128, 4096)
        for q in range(24):
            wmod[l, q] = _kpiece(inp['w_mod'][l], 256 * q, 256)
        for gi, nm in enumerate(['lru_gate_a_w', 'lru_gate_x_w']):
            W = inp[nm][l]
            for c in range(4):
                for gb in range(2):
                    gatew[gb * 64:(gb + 1) * 64, l, gi, c, gb * 64:(gb + 1) * 64] = W[2 * c + gb]
        wtail[:, l] = inp['w_in'][l][:, 3072:3080].reshape(KC, 128, 8).transpose(1, 0, 2)
    return wst, wmod, gatew.reshape(128, -1), wtail.reshape(128, -1)


def prep_small(inp, b):
    sm = np.zeros((128, NS), np.float32)

    def put(name, arr):
        o, w = SM[name]
        sm[:, o:o + w] = np.asarray(arr, np.float32).reshape(128, w)
    put('cT', _col(inp['c'][b]))
    for l in range(NL):
        put(f'bmod{l}', _col(inp['b_mod'][l]))
        put(f'nmw{l}', _col(inp['norm_mix_w'][l]))
        put(f'nlw{l}', _col(inp['norm_mlp_w'][l]))
        put(f'lcw{l}', inp['lru_conv_w'][l].reshape(4, 4, 128).transpose(2, 1, 0))
        put(f'lcb{l}', _col(inp['lru_conv_b'][l]))
        put(f'gab{l}', _col(inp['lru_gate_a_b'][l]))
        put(f'gxb{l}', _col(inp['lru_gate_x_b'][l]))
        put(f'lam{l}', _col(inp['lru_lambda'][l]))
        put(f'lnw{l}', _col(inp['lru_norm_w'][l]))
        put(f'gcw{l}', inp['gdn_conv_w'][l].reshape(4, 12, 128).transpose(2, 1, 0))
        put(f'gnw{l}', _col(inp['gdn_norm_w'][l]))
        put(f'alog{l}', np.broadcast_to(inp['gdn_a_log'][l][None, :], (128, 4)))
        put(f'dtb{l}', np.broadcast_to(inp['gdn_dt_bias'][l][None, :], (128, 4)))
    put('fnw', _col(inp['final_norm_w']))
    i = np.arange(128)
    put('ident', np.eye(128, dtype=np.float32))
    put('tri', (i[:, None] <= i[None, :]).astype(np.float32))
    put('negL', np.where(i[None, :] < i[:, None], 0.0, -1e30))
    put('negU', np.where(i[:, None] <= i[None, :], 0.0, -1e30))
    put('ones', np.ones((128, 128), np.float32))
    return sm


def prep_masks():
    i = np.arange(128)
    m = np.zeros((128, 14, 128), np.float32)
    for lev in range(7):
        bj = i[:, None] >> lev
        bi = i[None, :] >> lev
        mu = ((bj % 2 == 0) & (bi == bj + 1)).astype(np.float32)
        m[:, lev, :] = mu
        m[:, 7 + lev, :] = mu.T
    return m.reshape(128, -1)


_NC_CACHE = {}


def run(inp, n_tiles, layers, final_norm=True, ncores=BATCH, gelu_mode=0, stage=99):
    inp = {k: np.asarray(v) for k, v in inp.items()}
    key = (n_tiles, tuple(layers), final_norm, gelu_mode, stage)
    if key not in _NC_CACHE:
        _NC_CACHE[key] = build_nc(n_tiles, layers, final_norm, gelu_mode, stage)
    nc = _NC_CACHE[key]
    wst, wmod, gatew, wtail = prep_shared(inp)
    ntok = n_tiles * TT
    in_maps = []
    for b in range(ncores):
        xT = np.ascontiguousarray(inp['x'][b, :ntok, :].T.astype(np.float32))
        in_maps.append({"xT": xT, "wst": wst, "wmod": wmod, "small": prep_small(inp, b),
                        "gatew": gatew, "wtail": wtail, "masks": prep_masks()})
    res = run_bass_kernel_spmd(nc, in_maps, core_ids=list(range(ncores)))
    out = np.stack([np.asarray(r["outT"]).T for r in res.results], axis=0)
    return out.astype(np.float32)


def kernel(**inputs):
    return run(inputs, SEQ // TT, list(range(NL)), True, BATCH, gelu_mode=GELU_MODE)


GELU_MODE = 1
```

```python
import numpy as np
from contextlib import ExitStack
import concourse.bass as bass
import concourse.mybir as mybir
from concourse.bass_utils import run_bass_kernel_spmd

F32 = mybir.dt.float32
BF16 = mybir.dt.bfloat16
ALU = mybir.AluOpType
AF = mybir.ActivationFunctionType

D = 1024
KC = 8
TT = 512
NL = 4
SEQ = 4096
BATCH = 4
NSLOT = 6
ENGS = ['pe', 'act', 'dve', 'pool', 'sp']
EPS = 1e-6

SM = {}
_off = 0


def _reg(name, w):
    global _off
    SM[name] = (_off, w)
    _off += w


_reg('cT', 8)
for _l in range(NL):
    for _n, _w in [('bmod', 48), ('nmw', 8), ('nlw', 8), ('lcw', 16), ('lcb', 4), ('gab', 4), ('gxb', 4),
                   ('lam', 4), ('lnw', 4), ('gcw', 48), ('gnw', 1), ('alog', 4), ('dtb', 4)]:
        _reg(f'{_n}{_l}', _w)
_reg('fnw', 8)
for _n in ['ident', 'tri', 'negL', 'negU', 'ones']:
    _reg(_n, 128)
NS = _off


class Tile:
    def __init__(self, name, ap):
        self.name = name
        self.ap = ap
        self.lw = None
        self.rd = {}
        self.rdd = []

    def __getitem__(self, k):
        return V(self, self.ap[k])

    @property
    def v(self):
        return V(self, self.ap)


class V:
    def __init__(self, tile, ap):
        self.tile = tile
        self.ap = ap

    def __getitem__(self, k):
        return V(self.tile, self.ap[k])

    def bc(self, shape):
        return V(self.tile, self.ap.broadcast_to(list(shape)))

    def un(self, ax):
        return V(self.tile, self.ap.unsqueeze(ax))

    def rr(self, pat, **kw):
        return V(self.tile, self.ap.rearrange(pat, **kw))

    def cast(self, dt):
        return V(self.tile, self.ap.bitcast(dt))


class Op:
    __slots__ = ('eng', 'fn', 'waits', 'sem', 'val', 'isdma', 'seq')


def _ap(x):
    return x.ap if isinstance(x, V) else x


class Sched:
    def __init__(self, nc, es):
        self.nc = nc
        self.es = es
        self.q = {e: [] for e in ENGS}
        self.cnt = {e: 0 for e in ENGS}
        self.seen = {e: {} for e in ENGS}
        self.semh = {}
        self.dcnt = {}
        for e in ENGS:
            self.semh[e] = es.enter_context(nc.semaphore('s_' + e))

    def dsem(self, name):
        if name not in self.semh:
            self.semh[name] = self.es.enter_context(self.nc.semaphore('d_' + name))
            self.dcnt[name] = 0
        return name

    def add(self, eng, fn, reads, writes, dma=None):
        op = Op()
        op.eng = eng
        op.fn = fn
        op.isdma = dma is not None
        deps = {}

        def need(d):
            if d is None:
                return
            if d.isdma:
                deps[d.sem] = max(deps.get(d.sem, 0), d.val)
            elif d.eng == eng and not op.isdma:
                if eng != 'pe':
                    deps[d.sem] = max(deps.get(d.sem, 0), d.val)
            else:
                deps[d.sem] = max(deps.get(d.sem, 0), d.val)

        rt = [x.tile for x in reads if isinstance(x, V)]
        wt = [x.tile for x in writes if isinstance(x, V)]
        for t in rt:
            need(t.lw)
        for t in wt:
            need(t.lw)
            for r in t.rd.values():
                need(r)
            for r in t.rdd:
                need(r)
        waits = []
        sn = self.seen[eng]
        for sem, val in deps.items():
            if sn.get(sem, 0) >= val:
                continue
            sn[sem] = val
            waits.append((sem, val))
        op.waits = waits
        if op.isdma:
            self.dsem(dma)
            self.dcnt[dma] += 16
            op.sem = dma
            op.val = self.dcnt[dma]
            op.seq = None
        else:
            self.cnt[eng] += 1
            op.sem = eng
            op.val = self.cnt[eng]
            op.seq = op.val
        for t in wt:
            t.lw = op
            t.rd = {}
            t.rdd = []
        wset = set(id(t) for t in wt)
        for t in rt:
            if id(t) in wset:
                continue
            if op.isdma:
                t.rdd.append(op)
            else:
                t.rd[eng] = op
        self.q[eng].append(op)
        return op

    def mm(self, items):
        reads = []
        writes = []
        for (o, l, r, st, sp) in items:
            reads += [l, r]
            writes.append(o)

        def fn(e):
            ins = None
            for (o, l, r, st, sp) in items:
                ins = e.matmul(_ap(o), _ap(l), _ap(r), start=st, stop=sp)
            return ins
        self.add('pe', fn, reads, writes)

    def tr(self, items):
        reads = []
        writes = []
        for (o, i, idn) in items:
            reads += [i, idn]
            writes.append(o)

        def fn(e):
            ins = None
            for (o, i, idn) in items:
                ins = e.transpose(_ap(o), _ap(i), _ap(idn))
            return ins
        self.add('pe', fn, reads, writes)

    def act(self, out, in_, func, bias=None, scale=None):
        reads = [in_]
        kw = {}
        if bias is not None:
            kw['bias'] = _ap(bias)
            reads.append(bias)
        if scale is not None:
            kw['scale'] = _ap(scale)
            reads.append(scale)
        self.add('act', lambda e: e.activation(_ap(out), _ap(in_), func, **kw), reads, [out])

    def tt(self, eng, out, in0, in1, op):
        self.add(eng, lambda e: e.tensor_tensor(_ap(out), _ap(in0), _ap(in1), op), [in0, in1], [out])

    def ts(self, eng, out, in0, s1, s2, op0, op1=None):
        kw = {}
        if op1 is not None:
            kw['op1'] = op1
        self.add(eng, lambda e: e.tensor_scalar(_ap(out), _ap(in0), _ap(s1), _ap(s2) if s2 is not None else None,
                                                op0, **kw), [in0, s1, s2], [out])

    def stt(self, out, in0, scalar, in1, op0, op1):
        self.add('dve', lambda e: e.scalar_tensor_tensor(_ap(out), _ap(in0), _ap(scalar), _ap(in1), op0, op1),
                 [in0, scalar, in1], [out])

    def scan(self, out, d0, d1, init):
        self.add('dve', lambda e: e.tensor_tensor_scan(_ap(out), _ap(d0), _ap(d1), _ap(init), ALU.mult, ALU.add),
                 [d0, d1, init], [out])

    def cp(self, eng, out, in_):
        if eng == 'act':
            self.add('act', lambda e: e.copy(_ap(out), _ap(in_)), [in_], [out])
        else:
            self.add(eng, lambda e: e.tensor_copy(_ap(out), _ap(in_)), [in_], [out])

    def memset(self, eng, out, val):
        self.add(eng, lambda e: e.memset(_ap(out), val), [], [out])

    def dma(self, q, out, in_, sem):
        self.add(q, lambda e: e.dma_start(out=_ap(out), in_=_ap(in_)), [in_], [out], dma=sem)

    def rsq(self, out, in_, eps):
        self.act(out, in_, AF.Ln, bias=float(eps))
        self.act(out, out, AF.Exp, scale=-0.5)

    def finish(self):
        op = Op()
        op.eng = 'sp'
        op.fn = None
        op.isdma = False
        op.waits = [(k, v) for k, v in self.dcnt.items()]
        self.q['sp'].append(op)

    def emit(self):
        nc = self.nc
        with nc.Block() as blk:
            def mk(en):
                def body(e):
                    for op in self.q[en]:
                        for (sem, val) in op.waits:
                            e.wait_ge(self.semh[sem], val)
                        if op.fn is None:
                            continue
                        ins = op.fn(e)
                        ins.then_inc(self.semh[op.sem], 16 if op.isdma else 1)
                return body
            blk.tensor(mk('pe'))
            blk.scalar(mk('act'))
            blk.vector(mk('dve'))
            blk.gpsimd(mk('pool'))
            blk.sync(mk('sp'))


def build_nc(n_tiles, layers, final_norm=True, gelu_mode=0, stage=99):
    ntok = n_tiles * TT
    nc = bass.Bass("TRN2", target_bir_lowering=False)
    x_d = nc.dram_tensor("xT", [D, ntok], F32, kind="ExternalInput").ap()
    wst_d = nc.dram_tensor("wst", [NL, 24, 128, 4096], F32, kind="ExternalInput").ap()
    wmod_d = nc.dram_tensor("wmod", [NL, 24, 128, 2048], F32, kind="ExternalInput").ap()
    sm_d = nc.dram_tensor("small", [128, NS], F32, kind="ExternalInput").ap()
    gw_d = nc.dram_tensor("gatew", [128, NL * 2 * 4 * 128], F32, kind="ExternalInput").ap()
    wt_d = nc.dram_tensor("wtail", [128, NL * 64], F32, kind="ExternalInput").ap()
    mk_d = nc.dram_tensor("masks", [128, 14 * 128], F32, kind="ExternalInput").ap()
    out_d = nc.dram_tensor("outT", [D, ntok], F32, kind="ExternalOutput").ap()

    es = ExitStack()
    with es:
        S = Sched(nc, es)

        def sb(name, shape, dt=F32):
            return Tile(name, es.enter_context(nc.sbuf_tensor(name, list(shape), dt))[:])

        def ps(name, shape, dt=F32):
            return Tile(name, es.enter_context(nc.psum_tensor(name, list(shape), dt))[:])

        xT = sb('xT_sb', [128, KC, TT])
        ring = [sb(f'ring{i}', [128, 4096], BF16) for i in range(NSLOT)]
        sm = sb('sm', [128, NS])
        gatew = sb('gatew_sb', [128, NL, 2, 4, 128], BF16)
        wtail = sb('wtail_sb', [128, NL, KC, 8], BF16)
        ident_bf = sb('ident_bf', [128, 128], BF16)
        masks = sb('masks_sb', [128, 14, 128], BF16)
        ones_bf = sb('ones_bf', [128, 128], BF16)
        scT = sb('scT', [128, 8])
        modT = [sb(f'modT{l}', [128, 48]) for l in range(NL)]
        gam1 = [sb(f'gam1_{l}', [128, 8]) for l in range(NL)]
        gam2 = [sb(f'gam2_{l}', [128, 8]) for l in range(NL)]
        ccol = [sb(f'ccol{l}', [128, 4]) for l in range(NL)]
        c2col = [sb(f'c2col{l}', [128, 4]) for l in range(NL)]
        lnws = [sb(f'lnws{l}', [128, 4]) for l in range(NL)]
        gnws = [sb(f'gnws{l}', [128, 1]) for l in range(NL)]
        negA = [sb(f'negA{l}', [128, 4]) for l in range(NL)]
        fnws = sb('fnws', [128, 8])
        Sst = [sb(f'S{l}', [128, 4, 128]) for l in range(NL)]
        Sbf = [sb(f'Sbf{l}', [128, 4, 128], BF16) for l in range(NL)]
        hst = [sb(f'hst{l}', [128, 4]) for l in range(NL)]
        halo = [sb(f'halo{l}', [128, 16, 3]) for l in range(NL)]
        hT = sb('hT', [128, KC, TT], BF16)
        tmpr = [sb(f'tmpr{i}', [128, TT]) for i in range(2)]
        pre = [sb(f'pre{i}', [128, TT + 3]) for i in range(2)]
        Fs = [sb(f'F{i}', [128, TT]) for i in range(7)]
        hg = sb('hg', [128, 4, TT])
        gy = sb('gy', [128, 4, TT], BF16)
        xrb = sb('xrb', [128, TT], BF16)
        sqh = sb('sqh', [128, TT], BF16)
        qkv = sb('qkv', [128, 12, TT], BF16)
        sz = sb('sz', [128, 4, TT], BF16)
        mixT = sb('mixT', [128, 8, TT], BF16)
        sqb = sb('sqb', [128, 8, 128], BF16)
        Bn = {n: sb('B_' + n, [128, 4, 128], BF16) for n in
              ['qn', 'kn', 'kbg', 'kt', 'vb', 'L0', 'U0', 'Oa', 'Ob', 'Da0', 'Da1', 'Db0', 'Db1', 'Ya', 'Yb', 'wT', 'qd', 'attnT', 'vnew', 'osq']}
        gsm = {n: sb('g_' + n, [128, 4, 4]) for n in ['beta', 'xg', 'ax', 'e', 'l1', 'sp', 'g']}
        csm = {n: sb('c_' + n, [128, 4]) for n in ['gcol', 'egcol', 'dk', 'ekt', 'bg']}
        hidr = [sb(f'hidr{i}', [128, TT], BF16) for i in range(2)]
        hid = sb('hid', [128, 4, TT], BF16)
        lsm = {n: sb('l_' + n, [128, 4]) for n in ['e', 'l1']}
        pbig = [ps('pbig0', [128, 512]), ps('pbig1', [128, 512])]
        pstat = ps('pstat', [128, 512])
        pg = [ps('pg0', [128, 512]), ps('pg1', [128, 512]), ps('pg2', [128, 512])]
        ptr = ps('ptr', [128, 1024], BF16)
        psm = ps('psm', [128, 512])

        Fs1 = [sb(f'G{i}', [128, TT]) for i in range(7)]
        carved = []

        def carve(parent_ap_bf, idx, name, shape3=True):
            ap = parent_ap_bf[:, idx * 512:(idx + 1) * 512]
            if shape3:
                ap = ap.rearrange('p (h t) -> p h t', h=4)
            t = Tile(name, ap)
            carved.append(t)
            return t
        hT_flat = hT.ap.rearrange('p k t -> p (k t)')
        hg_flat = hg.ap.rearrange('p k t -> p (k t)').bitcast(BF16)
        gy_flat = gy.ap.rearrange('p k t -> p (k t)')
        names1 = list(Bn.keys())
        Bn1 = {}
        for i, nm in enumerate(names1):
            if i < 8:
                Bn1[nm] = carve(hT_flat, i, 'C_' + nm)
            elif i < 16:
                Bn1[nm] = carve(hg_flat, i - 8, 'C_' + nm)
            else:
                Bn1[nm] = carve(gy_flat, i - 16, 'C_' + nm)
        sqb1 = Tile('C_sqb', tmpr[0].ap.bitcast(BF16).rearrange('p (a t) -> p a t', a=8))
        carved.append(sqb1)
        csm1 = {n: sb('c1_' + n, [128, 4]) for n in ['gcol', 'egcol', 'dk', 'ekt', 'bg']}
        Tsets = [
            {'F': Fs, 'B': Bn, 'sqb': sqb, 'csm': csm, 'pg': pg, 'pc': 64},
            {'F': Fs1, 'B': Bn1, 'sqb': sqb1, 'csm': csm1, 'pg': [pbig[0], pbig[1], pstat], 'pc': 72},
        ]

        def fence(frm, to):
            rd = {}
            rdd = []
            for t in frm:
                for e, op in t.rd.items():
                    if e not in rd or rd[e].seq < op.seq:
                        rd[e] = op
                rdd += t.rdd
                if t.lw is not None:
                    if t.lw.isdma:
                        rdd.append(t.lw)
                    elif t.lw.eng not in rd or rd[t.lw.eng].seq < t.lw.seq:
                        rd[t.lw.eng] = t.lw
            for t in to:
                for e, op in rd.items():
                    if e not in t.rd or t.rd[e].seq < op.seq:
                        t.rd[e] = op
                t.rdd = t.rdd + rdd

        def smv(name):
            o, w = SM[name]
            return sm[:, o:o + w]

        identf = smv('ident')
        tri = smv('tri')
        negL = smv('negL')
        negU = smv('negU')
        onesf = smv('ones')

        pieces = []
        for l in layers:
            for q in range(24):
                pieces.append((wmod_d[l, q], True))
        for s in range(n_tiles):
            for l in layers:
                for g in range(24):
                    pieces.append((wst_d[l, g], False))
        rstate = {'use': 0, 'iss': 0}

        def next_slot():
            i = rstate['use']
            while rstate['iss'] <= min(i + NSLOT - 2, len(pieces) - 1):
                j = rstate['iss']
                src, f32v = pieces[j]
                slot = ring[j % NSLOT]
                if f32v:
                    S.dma('sp', slot.v.cast(F32), src, f'ringh{j % NSLOT}')
                else:
                    S.dma('pool', slot.v, src, f'rings{j % NSLOT}')
                rstate['iss'] += 1
            rstate['use'] += 1
            return ring[i % NSLOT]

        S.dma('sp', sm.v, sm_d, 'const')
        S.dma('pool', gatew.v.rr('p l g c m -> p (l g c m)'), gw_d, 'const2')
        S.dma('pool', wtail.v.rr('p l k j -> p (l k j)'), wt_d, 'const3')
        S.dma('pool', masks.v.rr('p a b -> p (a b)'), mk_d, 'const4')
        S.cp('dve', ident_bf.v, identf)
        S.cp('dve', ones_bf.v, onesf)
        S.act(scT.v, smv('cT'), AF.Silu)
        for l in range(NL):
            S.memset('pool', Sst[l].v, 0.0)
            S.memset('pool', Sbf[l].v, 0.0)
            S.memset('pool', hst[l].v, 0.0)
            S.memset('pool', halo[l].v, 0.0)
        for l in layers:
            for q in range(24):
                slot = next_slot()
                wv = slot.v.cast(F32).rr('p (k n) -> p k n', k=KC)
                items = []
                for j in range(2):
                    col = q * 2 + j
                    for kc in range(KC):
                        items.append((psm[:, col:col + 1], wv[:, kc, j * 128:(j + 1) * 128], scT[:, kc:kc + 1],
                                      kc == 0, kc == KC - 1))
                S.mm(items)
            S.tt('dve', modT[l].v, psm[:, 0:48], smv(f'bmod{l}'), ALU.add)
            S.stt(gam1[l].v, modT[l][:, 8:16], 1.0, smv(f'nmw{l}'), ALU.add, ALU.mult)
            S.ts('dve', gam1[l].v, gam1[l].v, 32.0, None, ALU.mult)
            S.stt(gam2[l].v, modT[l][:, 32:40], 1.0, smv(f'nlw{l}'), ALU.add, ALU.mult)
            S.ts('dve', gam2[l].v, gam2[l].v, 32.0, None, ALU.mult)
            S.act(lsm['e'].v, smv(f'lam{l}'), AF.Exp, scale=-1.0)
            S.act(lsm['l1'].v, lsm['e'].v, AF.Ln, bias=1.0)
            S.ts('dve', ccol[l].v, lsm['l1'].v, -8.0, None, ALU.mult)
            S.ts('dve', c2col[l].v, lsm['l1'].v, -16.0, None, ALU.mult)
            S.ts('dve', lnws[l].v, smv(f'lnw{l}'), float(np.sqrt(512.0)), None, ALU.mult)
            S.ts('dve', gnws[l].v, smv(f'gnw{l}'), float(np.sqrt(128.0)), None, ALU.mult)
            S.act(negA[l].v, smv(f'alog{l}'), AF.Exp)
            S.ts('dve', negA[l].v, negA[l].v, -1.0, None, ALU.mult)
        S.ts('dve', fnws.v, smv('fnw'), 32.0, None, ALU.mult)

        bigi = {'i': 0}
        if stage == 0:
            S.dma('sp', xT.v, x_d.rearrange('(k p) n -> p k n', p=128)[:, :, 0:TT], 'xload')
            S.cp('dve', xT[:, 0, 0:48], modT[layers[0]].v)
            S.dma('sp', out_d.rearrange('(k p) n -> p k n', p=128)[:, :, 0:TT], xT.v, 'ostore')
            S.finish()
            S.emit()
            return nc

        def nbig():
            b = pbig[bigi['i'] % 2]
            bigi['i'] += 1
            return b

        def modnorm(gam, shcols, dst):
            S.act(hT.v, xT.v, AF.Square)
            S.mm([(pstat.v, ones_bf.v, hT[:, kc, :], kc == 0, kc == KC - 1) for kc in range(KC)])
            rs = Fs[6]
            S.rsq(rs.v, pstat.v, D * EPS)
            for kc in range(KC):
                t = tmpr[kc % 2]
                S.stt(t.v, xT[:, kc, :], gam[:, kc:kc + 1], rs.v, ALU.mult, ALU.mult)
                if shcols is None:
                    S.cp('act', dst[:, kc, :], t.v)
                else:
                    S.act(dst[:, kc, :], t.v, AF.Identity, bias=shcols[:, kc:kc + 1])

        for s in range(n_tiles):
            S.dma('sp', xT.v, x_d.rearrange('(k p) n -> p k n', p=128)[:, :, s * TT:(s + 1) * TT], 'xload')
            for l in layers:
                modnorm(gam1[l], modT[l][:, 0:8], hT)

                if stage == 1:
                    S.dma('sp', out_d.rearrange('(k p) n -> p k n', p=128)[:, :, 0:TT], xT.v, 'ostore')
                    S.finish()
                    S.emit()
                    return nc
                wslots = {}

                def wcol(cc):
                    pidx = {1: 0, 0: 1}.get(cc // 4, cc // 4)
                    if pidx not in wslots:
                        assert len(wslots) == pidx
                        wslots[pidx] = next_slot()
                    sl = wslots[pidx].v.rr('p (k n) -> p k n', k=KC)
                    return [sl[:, kc, (cc % 4) * 128:(cc % 4 + 1) * 128] for kc in range(KC)]

                def proj(cc):
                    pb = nbig()
                    wc = wcol(cc)
                    S.mm([(pb.v, wc[kc], hT[:, kc, :], kc == 0, kc == KC - 1) for kc in range(KC)])
                    return pb

                def conv(pb, ci, wname, widx, bias):
                    pr = pre[ci % 2]
                    S.cp('act', pr[:, 3:TT + 3], pb.v)
                    S.cp('pool', pr[:, 0:3], halo[l][:, ci, :])
                    S.cp('pool', halo[l][:, ci, :], pr[:, TT:TT + 3])
                    o, w = SM[wname]
                    wv = sm[:, o + widx * 4:o + widx * 4 + 4]
                    acc = Fs[6]
                    if bias is not None:
                        S.ts('dve', acc.v, pr[:, 0:TT], wv[:, 0:1], bias, ALU.mult, ALU.add)
                    else:
                        S.ts('dve', acc.v, pr[:, 0:TT], wv[:, 0:1], None, ALU.mult)
                    for k in range(1, 4):
                        S.stt(acc.v, pr[:, k:TT + k], wv[:, k:k + 1], acc.v, ALU.mult, ALU.add)
                    return acc

                for c in range(4):
                    pb = proj(4 + c)
                    if gelu_mode == 0:
                        S.act(gy[:, c, :], pb.v, AF.Gelu_apprx_tanh)
                    else:
                        y = Fs[0]
                        S.cp('act', y.v, pb.v)
                        S.tt('dve', Fs[1].v, y.v, y.v, ALU.mult)
                        S.ts('dve', Fs[1].v, Fs[1].v, 0.044715, 1.0, ALU.mult, ALU.add)
                        S.tt('dve', Fs[1].v, Fs[1].v, y.v, ALU.mult)
                        S.act(Fs[1].v, Fs[1].v, AF.Sigmoid, scale=float(2.0 * np.sqrt(2.0 / np.pi)))
                        S.tt('dve', gy[:, c, :], Fs[1].v, y.v, ALU.mult)

                if stage == 2:
                    S.dma('sp', out_d.rearrange('(k p) n -> p k n', p=128)[:, :, 0:TT], xT.v, 'ostore')
                    S.finish()
                    S.emit()
                    return nc
                for c in range(4):
                    pb = proj(c)
                    lcb = smv(f'lcb{l}')
                    xr = conv(pb, c, f'lcw{l}', c, lcb[:, c:c + 1])
                    S.cp('act', xrb.v, xr.v)
                    pr_ = nbig()
                    pi_ = nbig()
                    S.mm([(pr_.v, gatew[:, l, 0, c, :], xrb.v, True, True)])
                    S.mm([(pi_.v, gatew[:, l, 1, c, :], xrb.v, True, True)])
                    r, ig, a, a2, b, h = Fs[0], Fs[1], Fs[2], Fs[3], Fs[4], Fs[5]
                    S.act(r.v, pr_.v, AF.Sigmoid, bias=smv(f'gab{l}')[:, c:c + 1])
                    S.act(ig.v, pi_.v, AF.Sigmoid, bias=smv(f'gxb{l}')[:, c:c + 1])
                    S.act(a.v, r.v, AF.Exp, scale=ccol[l][:, c:c + 1])
                    S.act(a2.v, r.v, AF.Exp, scale=c2col[l][:, c:c + 1])
                    S.ts('dve', a2.v, a2.v, -1.0, 1.0, ALU.mult, ALU.add)
                    S.ts('dve', a2.v, a2.v, 1e-12, None, ALU.max)
                    S.act(a2.v, a2.v, AF.Sqrt)
                    S.tt('dve', b.v, ig.v, xr.v, ALU.mult)
                    S.tt('dve', b.v, b.v, a2.v, ALU.mult)
                    S.scan(h.v, a.v, b.v, hst[l][:, c:c + 1])
                    S.cp('pool', hst[l][:, c:c + 1], h[:, TT - 1:TT])
                    S.tt('dve', hg[:, c, :], h.v, gy[:, c, :], ALU.mult)
                    S.act(sqh.v, hg[:, c, :], AF.Square)
                    S.mm([(pstat.v, ones_bf.v, sqh.v, c == 0, c == 3)])
                rsl = Fs[0]
                S.rsq(rsl.v, pstat.v, 512 * EPS)
                for c in range(4):
                    S.stt(mixT[:, c, :], hg[:, c, :], lnws[l][:, c:c + 1], rsl.v, ALU.mult, ALU.mult)

                if stage == 3:
                    S.dma('sp', out_d.rearrange('(k p) n -> p k n', p=128)[:, :, 0:TT], xT.v, 'ostore')
                    S.finish()
                    S.emit()
                    return nc
                for c in range(12):
                    pb = proj(8 + c)
                    cv = conv(pb, 4 + c, f'gcw{l}', c, None)
                    S.act(qkv[:, c, :], cv.v, AF.Silu)
                for c in range(4):
                    pb = proj(20 + c)
                    S.act(sz[:, c, :], pb.v, AF.Silu)
                for n in range(4):
                    S.mm([(psm[:, n * 8:(n + 1) * 8], hT[:, kc, n * 128:(n + 1) * 128], wtail[:, l, kc, :],
                           kc == 0, kc == KC - 1) for kc in range(KC)])
                ab = psm[:, 0:32].rr('p (n j) -> p n j', j=8)
                S.act(gsm['beta'].v, ab[:, :, 0:4], AF.Sigmoid)
                S.tt('dve', gsm['xg'].v, ab[:, :, 4:8], smv(f'dtb{l}').un(1).bc([128, 4, 4]), ALU.add)
                S.act(gsm['ax'].v, gsm['xg'].v, AF.Abs)
                S.act(gsm['e'].v, gsm['ax'].v, AF.Exp, scale=-1.0)
                S.act(gsm['l1'].v, gsm['e'].v, AF.Ln, bias=1.0)
                S.stt(gsm['sp'].v, gsm['xg'].v, 0.0, gsm['l1'].v, ALU.max, ALU.add)
                S.tt('dve', gsm['g'].v, gsm['sp'].v, negA[l].v.un(1).bc([128, 4, 4]), ALU.mult)


                if stage == 4:
                    S.dma('sp', out_d.rearrange('(k p) n -> p k n', p=128)[:, :, 0:TT], xT.v, 'ostore')
                    S.finish()
                    S.emit()
                    return nc
                B4 = [128, 4, 128]
                fence([hT, hg, gy, tmpr[0]], carved)

                def gdn_chunk(n, T):
                    F, Bn, sqb_, cs, pgs, pc = T['F'], T['B'], T['sqb'], T['csm'], T['pg'], T['pc']
                    r4 = lambda v: v.rr('p (h t) -> p h t', h=4)
                    tsl = slice(n * 128, (n + 1) * 128)
                    qT_ = qkv[:, 0:4, tsl]
                    kT_ = qkv[:, 4:8, tsl]
                    vT_ = qkv[:, 8:12, tsl]
                    beta_n = gsm['beta'][:, n, :]
                    g_n = gsm['g'][:, n, :]
                    S.act(sqb_.v, qkv[:, 0:8, tsl], AF.Square)
                    S.mm([(pgs[0].v, ones_bf.v, sqb_[:, 0:4, :], True, True)])
                    S.mm([(pgs[1].v, ones_bf.v, sqb_[:, 4:8, :], True, True)])
                    yield
                    S.rsq(F[0].v, pgs[0].v, 1e-6)
                    S.rsq(F[1].v, pgs[1].v, 1e-6)
                    yield
                    S.stt(Bn['qn'].v, qT_, float(128.0 ** -0.5), r4(F[0].v), ALU.mult, ALU.mult)
                    S.tt('dve', Bn['kn'].v, kT_, r4(F[1].v), ALU.mult)
                    yield
                    S.tt('dve', r4(F[0].v), tri.un(1).bc(B4), g_n.un(2).bc(B4), ALU.mult)
                    S.mm([(pgs[2].v, onesf, F[0].v, True, True)])
                    S.mm([(psm[:, pc:pc + 4], tri, g_n, True, True)])
                    ptk = r4(ptr[:, 0:512])
                    ptv = r4(ptr[:, 512:1024])
                    yield
                    S.cp('act', cs['gcol'].v, psm[:, pc:pc + 4])
                    Grow = r4(pgs[2].v)
                    t3 = r4(F[1].v)
                    yield
                    S.tt('dve', t3, cs['gcol'].v.un(2).bc(B4), Grow, ALU.subtract)
                    Eg = r4(F[4].v)
                    S.act(cs['egcol'].v, cs['gcol'].v, AF.Exp)
                    S.tt('dve', cs['dk'].v, Grow[:, :, 127], cs['gcol'].v, ALU.subtract)
                    yield
                    aL = r4(F[2].v)
                    aU = r4(F[3].v)
                    S.tt('pool', aL, t3, negL.un(1).bc(B4), ALU.add)
                    S.tt('pool', aU, negU.un(1).bc(B4), t3, ALU.subtract)
                    S.act(cs['ekt'].v, cs['dk'].v, AF.Exp)
                    S.tt('dve', cs['bg'].v, beta_n, cs['egcol'].v, ALU.mult)
                    yield
                    S.act(F[2].v, F[2].v, AF.Exp)
                    S.act(F[3].v, F[3].v, AF.Exp)
                    S.act(F[4].v, pgs[2].v, AF.Exp)
                    yield
                    S.tr([(ptk[:, h, :], Bn['kn'][:, h, :], ident_bf.v) for h in range(4)])
                    S.tr([(ptv[:, h, :], vT_[:, h, :], ident_bf.v) for h in range(4)])
                    S.tt('dve', Bn['kbg'].v, ptk, cs['bg'].v.un(2).bc(B4), ALU.mult)
                    S.tt('dve', Bn['kt'].v, ptk, cs['ekt'].v.un(2).bc(B4), ALU.mult)
                    S.tt('dve', Bn['vb'].v, ptv, beta_n.un(2).bc(B4), ALU.mult)
                    yield
                    S.tt('pool', aL, aL, beta_n.un(2).bc(B4), ALU.mult)
                    pgr = r4(pgs[0].v)
                    pqk = r4(pgs[1].v)
                    S.mm([(pgr[:, h, :], Bn['kn'][:, h, :], Bn['kn'][:, h, :], True, True) for h in range(4)])
                    S.mm([(pqk[:, h, :], Bn['kn'][:, h, :], Bn['qn'][:, h, :], True, True) for h in range(4)])
                    S.tt('pool', Bn['qd'].v, Bn['qn'].v, Eg, ALU.mult)
                    yield
                    S.tt('dve', Bn['L0'].v, pgr, aL, ALU.mult)
                    S.tt('dve', Bn['attnT'].v, pqk, aU, ALU.mult)
                    yield
                    S.tr([(ptk[:, h, :], Bn['L0'][:, h, :], ident_bf.v) for h in range(4)])
                    S.cp('dve', Bn['U0'].v, ptk)
                    yield
                    idb = ident_bf.v.un(1).bc(B4)
                    pY = r4(pgs[0].v)
                    pYp = r4(pgs[1].v)
                    pZ = r4(pgs[2].v)
                    S.tt('pool', Bn['Oa'].v, Bn['U0'].v, masks[:, 0, :].un(1).bc(B4), ALU.mult)
                    S.tt('pool', Bn['Ob'].v, Bn['L0'].v, masks[:, 7, :].un(1).bc(B4), ALU.mult)
                    S.tt('pool', Bn['Da0'].v, idb, Bn['Oa'].v, ALU.subtract)
                    S.tt('pool', Bn['Db0'].v, idb, Bn['Ob'].v, ALU.subtract)
                    Dc, Dpc = 'Da0', 'Db0'
                    for lev in range(1, 7):
                        Dn, Dpn = ('Da1', 'Db1') if lev % 2 == 1 else ('Da0', 'Db0')
                        S.tt('pool', Bn['Oa'].v, Bn['U0'].v, masks[:, lev, :].un(1).bc(B4), ALU.mult)
                        S.tt('pool', Bn['Ob'].v, Bn['L0'].v, masks[:, 7 + lev, :].un(1).bc(B4), ALU.mult)
                        yield
                        S.mm([(pY[:, h, :], Bn['Ob'][:, h, :], Bn[Dc][:, h, :], True, True) for h in range(4)])
                        if lev < 6:
                            S.mm([(pYp[:, h, :], Bn['Oa'][:, h, :], Bn[Dpc][:, h, :], True, True) for h in range(4)])
                        yield
                        S.cp('act', Bn['Ya'].v, pY)
                        if lev < 6:
                            S.cp('dve', Bn['Yb'].v, pYp)
                        yield
                        S.mm([(pZ[:, h, :], Bn[Dpc][:, h, :], Bn['Ya'][:, h, :], True, True) for h in range(4)])
                        if lev < 6:
                            S.mm([(pY[:, h, :], Bn[Dc][:, h, :], Bn['Yb'][:, h, :], True, True) for h in range(4)])
                        yield
                        S.tt('dve', Bn[Dn].v, Bn[Dc].v, pZ, ALU.subtract)
                        if lev < 6:
                            S.tt('dve', Bn[Dpn].v, Bn[Dpc].v, pY, ALU.subtract)
                        Dc, Dpc = Dn, Dpn
                        yield
                    Mf = Bn[Dc]
                    pu = r4(pgs[0].v)
                    pw = r4(pgs[1].v)
                    S.mm([(pu[:, h, :], Mf[:, h, :], Bn['vb'][:, h, :], True, True) for h in range(4)])
                    S.mm([(pw[:, h, :], Bn['kbg'][:, h, :], Mf[:, h, :], True, True) for h in range(4)])
                    yield
                    S.cp('act', F[5].v, pgs[0].v)
                    S.cp('dve', Bn['wT'].v, pw)
                    yield
                    yield 'SCAN'
                    pws = r4(pgs[2].v)
                    S.mm([(pws[:, h, :], Bn['wT'][:, h, :], Sbf[l][:, h, :], True, True) for h in range(4)])
                    yield
                    S.tt('dve', Bn['vnew'].v, r4(F[5].v), pws, ALU.subtract)
                    yield
                    po = r4(pgs[0].v)
                    items = []
                    for h in range(4):
                        items.append((po[:, h, :], Sbf[l][:, h, :], Bn['qd'][:, h, :], True, False))
                        items.append((po[:, h, :], Bn['vnew'][:, h, :], Bn['attnT'][:, h, :], False, True))
                    S.mm(items)
                    pS = r4(pgs[1].v)
                    S.mm([(pS[:, h, :], Bn['kt'][:, h, :], Bn['vnew'][:, h, :], True, True) for h in range(4)])
                    S.tt('pool', Sst[l].v, Sst[l].v, Eg[:, :, 127].un(2).bc(B4), ALU.mult)
                    yield
                    S.tt('dve', Sst[l].v, Sst[l].v, pS, ALU.add)
                    S.cp('act', Sbf[l].v, Sst[l].v)
                    S.act(Bn['osq'].v, po, AF.Square)
                    S.cp('act', F[0].v, pgs[0].v)
                    yield
                    S.mm([(pgs[2].v, ones_bf.v, Bn['osq'].v, True, True)])
                    yield
                    S.rsq(F[1].v, pgs[2].v, 128 * EPS)
                    yield
                    S.tt('dve', F[6].v, F[0].v, F[1].v, ALU.mult)
                    S.stt(mixT[:, 4:8, tsl], r4(F[6].v), gnws[l][:, 0:1], sz[:, 0:4, tsl], ALU.mult, ALU.mult)
                    yield

                for pair in range(2):
                    gens = [gdn_chunk(2 * pair + i, Tsets[i]) for i in range(2)]
                    alive = [True, True]
                    while any(alive):
                        for i in range(2):
                            if alive[i]:
                                if next(gens[i]) == 'SCAN':
                                    alive[i] = False
                    for i in range(2):
                        for _ in gens[i]:
                            pass
                fence(carved, [hT, hg, gy, tmpr[0]])

                oslots = [next_slot() for g in range(2)]
                for fc in range(8):
                    sl = oslots[fc // 4].v.rr('p (k n) -> p k n', k=KC)
                    pb = nbig()
                    S.mm([(pb.v, sl[:, kc, (fc % 4) * 128:(fc % 4 + 1) * 128], mixT[:, kc, :], kc == 0, kc == KC - 1)
                          for kc in range(KC)])
                    S.stt(xT[:, fc, :], pb.v, modT[l][:, 16 + fc:17 + fc], xT[:, fc, :], ALU.mult, ALU.add)


                if stage == 6:
                    S.dma('sp', out_d.rearrange('(k p) n -> p k n', p=128)[:, :, 0:TT], xT.v, 'ostore')
                    S.finish()
                    S.emit()
                    return nc
                modnorm(gam2[l], modT[l][:, 24:32], hT)
                for hgp in range(8):
                    su = next_slot().v.rr('p (k n) -> p k n', k=KC)
                    sd = next_slot().v.rr('p (c f) -> p c f', c=4)
                    for hc in range(4):
                        pb = nbig()
                        S.mm([(pb.v, su[:, kc, hc * 128:(hc + 1) * 128], hT[:, kc, :], kc == 0, kc == KC - 1)
                              for kc in range(KC)])
                        hr = hidr[hc % 2]
                        S.act(hr.v, pb.v, AF.Relu)
                        S.tt('pool', hid[:, hc, :], hr.v, hr.v, ALU.mult)
                    for fc in range(8):
                        pb = nbig()
                        S.mm([(pb.v, sd[:, hc, fc * 128:(fc + 1) * 128], hid[:, hc, :], hc == 0, hc == 3)
                              for hc in range(4)])
                        S.stt(xT[:, fc, :], pb.v, modT[l][:, 40 + fc:41 + fc], xT[:, fc, :], ALU.mult, ALU.add)

            if final_norm:
                S.act(hT.v, xT.v, AF.Square)
                S.mm([(pstat.v, ones_bf.v, hT[:, kc, :], kc == 0, kc == KC - 1) for kc in range(KC)])
                S.rsq(Fs[6].v, pstat.v, D * EPS)
                for kc in range(KC):
                    S.stt(xT[:, kc, :], xT[:, kc, :], fnws[:, kc:kc + 1], Fs[6].v, ALU.mult, ALU.mult)
            S.dma('sp', out_d.rearrange('(k p) n -> p k n', p=128)[:, :, s * TT:(s + 1) * TT], xT.v, 'ostore')

        S.finish()
        S.emit()
    return nc


def _kpiece(W, c0, ncols):
    return np.ascontiguousarray(W[:, c0:c0 + ncols].reshape(KC, 128, ncols).transpose(1, 0, 2).reshape(128, KC * ncols))


def _col(v):
    v = np.asarray(v, np.float32).reshape(-1)
    return v.reshape(-1, 128).T


def prep_shared(inp):
    wst = np.empty((NL, 24, 128, 4096), np.float32)
    wmod = np.empty((NL, 24, 128, 2048), np.float32)
    gatew = np.zeros((128, NL, 2, 4, 128), np.float32)
    wtail = np.empty((128, NL, KC, 8), np.float32)
    for l in range(NL):
        for g in range(6):
            wst[l, g] = _kpiece(inp['w_in'][l], 512 * {0: 1, 1: 0}.get(g, g), 512)
        for g in range(2):
            wst[l, 6 + g] = _kpiece(inp['w_out'][l], 512 * g, 512)
        for hg in range(8):
            wst[l, 8 + 2 * hg] = _kpiece(inp['w_up'][l], 512 * hg, 512)
            wd = inp['w_down'][l][512 * hg:512 * hg + 512, :]
            wst[l, 9 + 2 * hg] = wd.reshape(4, 128, 1024).transpose(1, 0, 2).reshape(128, 4096)
        for q in range(24):
            wmod[l, q] = _kpiece(inp['w_mod'][l], 256 * q, 256)
        for gi, nm in enumerate(['lru_gate_a_w', 'lru_gate_x_w']):
            W = inp[nm][l]
            for c in range(4):
                for gb in range(2):
                    gatew[gb * 64:(gb + 1) * 64, l, gi, c, gb * 64:(gb + 1) * 64] = W[2 * c + gb]
        wtail[:, l] = inp['w_in'][l][:, 3072:3080].reshape(KC, 128, 8).transpose(1, 0, 2)
    return wst, wmod, gatew.reshape(128, -1), wtail.reshape(128, -1)


def prep_small(inp, b):
    sm = np.zeros((128, NS), np.float32)

    def put(name, arr):
        o, w = SM[name]
        sm[:, o:o + w] = np.asarray(arr, np.float32).reshape(128, w)
    put('cT', _col(inp['c'][b]))
    for l in range(NL):
        put(f'bmod{l}', _col(inp['b_mod'][l]))
        put(f'nmw{l}', _col(inp['norm_mix_w'][l]))
        put(f'nlw{l}', _col(inp['norm_mlp_w'][l]))
        put(f'lcw{l}', inp['lru_conv_w'][l].reshape(4, 4, 128).transpose(2, 1, 0))
        put(f'lcb{l}', _col(inp['lru_conv_b'][l]))
        put(f'gab{l}', _col(inp['lru_gate_a_b'][l]))
        put(f'gxb{l}', _col(inp['lru_gate_x_b'][l]))
        put(f'lam{l}', _col(inp['lru_lambda'][l]))
        put(f'lnw{l}', _col(inp['lru_norm_w'][l]))
        put(f'gcw{l}', inp['gdn_conv_w'][l].reshape(4, 12, 128).transpose(2, 1, 0))
        put(f'gnw{l}', _col(inp['gdn_norm_w'][l]))
        put(f'alog{l}', np.broadcast_to(inp['gdn_a_log'][l][None, :], (128, 4)))
        put(f'dtb{l}', np.broadcast_to(inp['gdn_dt_bias'][l][None, :], (128, 4)))
    put('fnw', _col(inp['final_norm_w']))
    i = np.arange(128)
    put('ident', np.eye(128, dtype=np.float32))
    put('tri', (i[:, None] <= i[None, :]).astype(np.float32))
    put('negL', np.where(i[None, :] < i[:, None], 0.0, -1e30))
    put('negU', np.where(i[:, None] <= i[None, :], 0.0, -1e30))
    put('ones', np.ones((128, 128), np.float32))
    return sm


def prep_masks():
    i = np.arange(128)
    m = np.zeros((128, 14, 128), np.float32)
    for lev in range(7):
        bj = i[:, None] >> lev
        bi = i[None, :] >> lev
        mu = ((bj % 2 == 0) & (bi == bj + 1)).astype(np.float32)
        m[:, lev, :] = mu
        m[:, 7 + lev, :] = mu.T
    return m.reshape(128, -1)


_NC_CACHE = {}


def run(inp, n_tiles, layers, final_norm=True, ncores=BATCH, gelu_mode=0, stage=99):
    inp = {k: np.asarray(v) for k, v in inp.items()}
    key = (n_tiles, tuple(layers), final_norm, gelu_mode, stage)
    if key not in _NC_CACHE:
        _NC_CACHE[key] = build_nc(n_tiles, layers, final_norm, gelu_mode, stage)
    nc = _NC_CACHE[key]
    wst, wmod, gatew, wtail = prep_shared(inp)
    ntok = n_tiles * TT
    in_maps = []
    for b in range(ncores):
        xT = np.ascontiguousarray(inp['x'][b, :ntok, :].T.astype(np.float32))
        in_maps.append({"xT": xT, "wst": wst, "wmod": wmod, "small": prep_small(inp, b),
                        "gatew": gatew, "wtail": wtail, "masks": prep_masks()})
    res = run_bass_kernel_spmd(nc, in_maps, core_ids=list(range(ncores)))
    out = np.stack([np.asarray(r["outT"]).T for r in res.results], axis=0)
    return out.astype(np.float32)


def kernel(**inputs):
    return run(inputs, SEQ // TT, list(range(NL)), True, BATCH, gelu_mode=GELU_MODE)


GELU_MODE = 1
```

```python
import numpy as np
from contextlib import ExitStack
import concourse.bass as bass
import concourse.mybir as mybir
from concourse.bass_utils import run_bass_kernel_spmd

F32 = mybir.dt.float32
BF16 = mybir.dt.bfloat16
ALU = mybir.AluOpType
AF = mybir.ActivationFunctionType

D = 1024
KC = 8
TT = 512
NL = 4
SEQ = 4096
BATCH = 4
NSLOT = 6
ENGS = ['pe', 'act', 'dve', 'pool', 'sp']
EPS = 1e-6

SM = {}
_off = 0


def _reg(name, w):
    global _off
    SM[name] = (_off, w)
    _off += w


_reg('cT', 8)
for _l in range(NL):
    for _n, _w in [('bmod', 48), ('nmw', 8), ('nlw', 8), ('lcw', 16), ('lcb', 4), ('gab', 4), ('gxb', 4),
                   ('lam', 4), ('lnw', 4), ('gcw', 48), ('gnw', 1), ('alog', 4), ('dtb', 4)]:
        _reg(f'{_n}{_l}', _w)
_reg('fnw', 8)
for _n in ['ident', 'tri', 'negL', 'negU', 'ones']:
    _reg(_n, 128)
NS = _off


class Tile:
    def __init__(self, name, ap):
        self.name = name
        self.ap = ap
        self.lw = None
        self.rd = {}
        self.rdd = []

    def __getitem__(self, k):
        return V(self, self.ap[k])

    @property
    def v(self):
        return V(self, self.ap)


class V:
    def __init__(self, tile, ap):
        self.tile = tile
        self.ap = ap

    def __getitem__(self, k):
        return V(self.tile, self.ap[k])

    def bc(self, shape):
        return V(self.tile, self.ap.broadcast_to(list(shape)))

    def un(self, ax):
        return V(self.tile, self.ap.unsqueeze(ax))

    def rr(self, pat, **kw):
        return V(self.tile, self.ap.rearrange(pat, **kw))

    def cast(self, dt):
        return V(self.tile, self.ap.bitcast(dt))


class Op:
    __slots__ = ('eng', 'fn', 'waits', 'sem', 'val', 'isdma', 'seq')


def _ap(x):
    return x.ap if isinstance(x, V) else x


class Sched:
    def __init__(self, nc, es):
        self.nc = nc
        self.es = es
        self.q = {e: [] for e in ENGS}
        self.cnt = {e: 0 for e in ENGS}
        self.seen = {e: {} for e in ENGS}
        self.semh = {}
        self.dcnt = {}
        for e in ENGS:
            self.semh[e] = es.enter_context(nc.semaphore('s_' + e))

    def dsem(self, name):
        if name not in self.semh:
            self.semh[name] = self.es.enter_context(self.nc.semaphore('d_' + name))
            self.dcnt[name] = 0
        return name

    def add(self, eng, fn, reads, writes, dma=None):
        op = Op()
        op.eng = eng
        op.fn = fn
        op.isdma = dma is not None
        deps = {}

        def need(d):
            if d is None:
                return
            if d.isdma:
                deps[d.sem] = max(deps.get(d.sem, 0), d.val)
            elif d.eng == eng and not op.isdma:
                if eng != 'pe':
                    deps[d.sem] = max(deps.get(d.sem, 0), d.val)
            else:
                deps[d.sem] = max(deps.get(d.sem, 0), d.val)

        rt = [x.tile for x in reads if isinstance(x, V)]
        wt = [x.tile for x in writes if isinstance(x, V)]
        for t in rt:
            need(t.lw)
        for t in wt:
            need(t.lw)
            for r in t.rd.values():
                need(r)
            for r in t.rdd:
                need(r)
        waits = []
        sn = self.seen[eng]
        for sem, val in deps.items():
            if sn.get(sem, 0) >= val:
                continue
            sn[sem] = val
            waits.append((sem, val))
        op.waits = waits
        if op.isdma:
            self.dsem(dma)
            self.dcnt[dma] += 16
            op.sem = dma
            op.val = self.dcnt[dma]
            op.seq = None
        else:
            self.cnt[eng] += 1
            op.sem = eng
            op.val = self.cnt[eng]
            op.seq = op.val
        for t in wt:
            t.lw = op
            t.rd = {}
            t.rdd = []
        wset = set(id(t) for t in wt)
        for t in rt:
            if id(t) in wset:
                continue
            if op.isdma:
                t.rdd.append(op)
            else:
                t.rd[eng] = op
        self.q[eng].append(op)
        return op

    def mm(self, items):
        reads = []
        writes = []
        for (o, l, r, st, sp) in items:
            reads += [l, r]
            writes.append(o)

        def fn(e):
            ins = None
            for (o, l, r, st, sp) in items:
                ins = e.matmul(_ap(o), _ap(l), _ap(r), start=st, stop=sp)
            return ins
        self.add('pe', fn, reads, writes)

    def tr(self, items):
        reads = []
        writes = []
        for (o, i, idn) in items:
            reads += [i, idn]
            writes.append(o)

        def fn(e):
            ins = None
            for (o, i, idn) in items:
                ins = e.transpose(_ap(o), _ap(i), _ap(idn))
            return ins
        self.add('pe', fn, reads, writes)

    def act(self, out, in_, func, bias=None, scale=None):
        reads = [in_]
        kw = {}
        if bias is not None:
            kw['bias'] = _ap(bias)
            reads.append(bias)
        if scale is not None:
            kw['scale'] = _ap(scale)
            reads.append(scale)
        self.add('act', lambda e: e.activation(_ap(out), _ap(in_), func, **kw), reads, [out])

    def tt(self, eng, out, in0, in1, op):
        self.add(eng, lambda e: e.tensor_tensor(_ap(out), _ap(in0), _ap(in1), op), [in0, in1], [out])

    def ts(self, eng, out, in0, s1, s2, op0, op1=None):
        kw = {}
        if op1 is not None:
            kw['op1'] = op1
        self.add(eng, lambda e: e.tensor_scalar(_ap(out), _ap(in0), _ap(s1), _ap(s2) if s2 is not None else None,
                                                op0, **kw), [in0, s1, s2], [out])

    def stt(self, out, in0, scalar, in1, op0, op1):
        self.add('dve', lambda e: e.scalar_tensor_tensor(_ap(out), _ap(in0), _ap(scalar), _ap(in1), op0, op1),
                 [in0, scalar, in1], [out])

    def scan(self, out, d0, d1, init):
        self.add('dve', lambda e: e.tensor_tensor_scan(_ap(out), _ap(d0), _ap(d1), _ap(init), ALU.mult, ALU.add),
                 [d0, d1, init], [out])

    def cp(self, eng, out, in_):
        if eng == 'act':
            self.add('act', lambda e: e.copy(_ap(out), _ap(in_)), [in_], [out])
        else:
            self.add(eng, lambda e: e.tensor_copy(_ap(out), _ap(in_)), [in_], [out])

    def memset(self, eng, out, val):
        self.add(eng, lambda e: e.memset(_ap(out), val), [], [out])

    def dma(self, q, out, in_, sem):
        self.add(q, lambda e: e.dma_start(out=_ap(out), in_=_ap(in_)), [in_], [out], dma=sem)

    def rsq(self, out, in_, eps):
        self.act(out, in_, AF.Ln, bias=float(eps))
        self.act(out, out, AF.Exp, scale=-0.5)

    def finish(self):
        op = Op()
        op.eng = 'sp'
        op.fn = None
        op.isdma = False
        op.waits = [(k, v) for k, v in self.dcnt.items()]
        self.q['sp'].append(op)

    def emit(self):
        nc = self.nc
        with nc.Block() as blk:
            def mk(en):
                def body(e):
                    for op in self.q[en]:
                        for (sem, val) in op.waits:
                            e.wait_ge(self.semh[sem], val)
                        if op.fn is None:
                            continue
                        ins = op.fn(e)
                        ins.then_inc(self.semh[op.sem], 16 if op.isdma else 1)
                return body
            blk.tensor(mk('pe'))
            blk.scalar(mk('act'))
            blk.vector(mk('dve'))
            blk.gpsimd(mk('pool'))
            blk.sync(mk('sp'))


def build_nc(n_tiles, layers, final_norm=True, gelu_mode=0, stage=99):
    ntok = n_tiles * TT
    nc = bass.Bass("TRN2", target_bir_lowering=False)
    x_d = nc.dram_tensor("xT", [D, ntok], F32, kind="ExternalInput").ap()
    wst_d = nc.dram_tensor("wst", [NL, 24, 128, 4096], F32, kind="ExternalInput").ap()
    wmod_d = nc.dram_tensor("wmod", [NL, 24, 128, 2048], F32, kind="ExternalInput").ap()
    sm_d = nc.dram_tensor("small", [128, NS], F32, kind="ExternalInput").ap()
    gw_d = nc.dram_tensor("gatew", [128, NL * 2 * 4 * 128], F32, kind="ExternalInput").ap()
    wt_d = nc.dram_tensor("wtail", [128, NL * 64], F32, kind="ExternalInput").ap()
    mk_d = nc.dram_tensor("masks", [128, 14 * 128], F32, kind="ExternalInput").ap()
    out_d = nc.dram_tensor("outT", [D, ntok], F32, kind="ExternalOutput").ap()

    es = ExitStack()
    with es:
        S = Sched(nc, es)

        def sb(name, shape, dt=F32):
            return Tile(name, es.enter_context(nc.sbuf_tensor(name, list(shape), dt))[:])

        def ps(name, shape, dt=F32):
            return Tile(name, es.enter_context(nc.psum_tensor(name, list(shape), dt))[:])

        xT = sb('xT_sb', [128, KC, TT])
        ring = [sb(f'ring{i}', [128, 4096], BF16) for i in range(NSLOT)]
        sm = sb('sm', [128, NS])
        gatew = sb('gatew_sb', [128, NL, 2, 4, 128], BF16)
        wtail = sb('wtail_sb', [128, NL, KC, 8], BF16)
        ident_bf = sb('ident_bf', [128, 128], BF16)
        masks = sb('masks_sb', [128, 14, 128], BF16)
        ones_bf = sb('ones_bf', [128, 128], BF16)
        scT = sb('scT', [128, 8])
        modT = [sb(f'modT{l}', [128, 48]) for l in range(NL)]
        gam1 = [sb(f'gam1_{l}', [128, 8]) for l in range(NL)]
        gam2 = [sb(f'gam2_{l}', [128, 8]) for l in range(NL)]
        ccol = [sb(f'ccol{l}', [128, 4]) for l in range(NL)]
        c2col = [sb(f'c2col{l}', [128, 4]) for l in range(NL)]
        lnws = [sb(f'lnws{l}', [128, 4]) for l in range(NL)]
        gnws = [sb(f'gnws{l}', [128, 1]) for l in range(NL)]
        negA = [sb(f'negA{l}', [128, 4]) for l in range(NL)]
        fnws = sb('fnws', [128, 8])
        Sst = [sb(f'S{l}', [128, 4, 128]) for l in range(NL)]
        Sbf = [sb(f'Sbf{l}', [128, 4, 128], BF16) for l in range(NL)]
        hst = [sb(f'hst{l}', [128, 4]) for l in range(NL)]
        halo = [sb(f'halo{l}', [128, 16, 3]) for l in range(NL)]
        hT = sb('hT', [128, KC, TT], BF16)
        tmpr = [sb(f'tmpr{i}', [128, TT]) for i in range(2)]
        pre = [sb(f'pre{i}', [128, TT + 3]) for i in range(2)]
        Fs = [sb(f'F{i}', [128, TT]) for i in range(7)]
        hg = sb('hg', [128, 4, TT])
        gy = sb('gy', [128, 4, TT], BF16)
        xrb = sb('xrb', [128, TT], BF16)
        sqh = sb('sqh', [128, TT], BF16)
        qkv = sb('qkv', [128, 12, TT], BF16)
        sz = sb('sz', [128, 4, TT], BF16)
        mixT = sb('mixT', [128, 8, TT], BF16)
        sqb = sb('sqb', [128, 8, 128], BF16)
        Bn = {n: sb('B_' + n, [128, 4, 128], BF16) for n in
              ['qn', 'kn', 'kbg', 'kt', 'vb', 'L0', 'U0', 'Oa', 'Ob', 'Da0', 'Da1', 'Db0', 'Db1', 'Ya', 'Yb', 'wT', 'qd', 'attnT', 'vnew', 'osq']}
        gsm = {n: sb('g_' + n, [128, 4, 4]) for n in ['beta', 'xg', 'ax', 'e', 'l1', 'sp', 'g']}
        csm = {n: sb('c_' + n, [128, 4]) for n in ['gcol', 'egcol', 'dk', 'ekt', 'bg']}
        hidr = [sb(f'hidr{i}', [128, TT], BF16) for i in range(2)]
        hid = sb('hid', [128, 4, TT], BF16)
        lsm = {n: sb('l_' + n, [128, 4]) for n in ['e', 'l1']}
        pbig = [ps('pbig0', [128, 512]), ps('pbig1', [128, 512])]
        pstat = ps('pstat', [128, 512])
        pg = [ps('pg0', [128, 512]), ps('pg1', [128, 512]), ps('pg2', [128, 512])]
        ptr = ps('ptr', [128, 1024], BF16)
        psm = ps('psm', [128, 512])

        Fs1 = [sb(f'G{i}', [128, TT]) for i in range(7)]
        carved = []

        def carve(parent_ap_bf, idx, name, shape3=True):
            ap = parent_ap_bf[:, idx * 512:(idx + 1) * 512]
            if shape3:
                ap = ap.rearrange('p (h t) -> p h t', h=4)
            t = Tile(name, ap)
            carved.append(t)
            return t
        hT_flat = hT.ap.rearrange('p k t -> p (k t)')
        hg_flat = hg.ap.rearrange('p k t -> p (k t)').bitcast(BF16)
        gy_flat = gy.ap.rearrange('p k t -> p (k t)')
        names1 = list(Bn.keys())
        Bn1 = {}
        for i, nm in enumerate(names1):
            if i < 8:
                Bn1[nm] = carve(hT_flat, i, 'C_' + nm)
            elif i < 16:
                Bn1[nm] = carve(hg_flat, i - 8, 'C_' + nm)
            else:
                Bn1[nm] = carve(gy_flat, i - 16, 'C_' + nm)
        sqb1 = Tile('C_sqb', tmpr[0].ap.bitcast(BF16).rearrange('p (a t) -> p a t', a=8))
        carved.append(sqb1)
        csm1 = {n: sb('c1_' + n, [128, 4]) for n in ['gcol', 'egcol', 'dk', 'ekt', 'bg']}
        Tsets = [
            {'F': Fs, 'B': Bn, 'sqb': sqb, 'csm': csm, 'pg': pg, 'pc': 64},
            {'F': Fs1, 'B': Bn1, 'sqb': sqb1, 'csm': csm1, 'pg': [pbig[0], pbig[1], pstat], 'pc': 72},
        ]

        def fence(frm, to):
            rd = {}
            rdd = []
            for t in frm:
                for e, op in t.rd.items():
                    if e not in rd or rd[e].seq < op.seq:
                        rd[e] = op
                rdd += t.rdd
                if t.lw is not None:
                    if t.lw.isdma:
                        rdd.append(t.lw)
                    elif t.lw.eng not in rd or rd[t.lw.eng].seq < t.lw.seq:
                        rd[t.lw.eng] = t.lw
            for t in to:
                for e, op in rd.items():
                    if e not in t.rd or t.rd[e].seq < op.seq:
                        t.rd[e] = op
                t.rdd = t.rdd + rdd

        def smv(name):
            o, w = SM[name]
            return sm[:, o:o + w]

        identf = smv('ident')
        tri = smv('tri')
        negL = smv('negL')
        negU = smv('negU')
        onesf = smv('ones')

        pieces = []
        for l in layers:
            for q in range(24):
                pieces.append((wmod_d[l, q], True))
        for s in range(n_tiles):
            for l in layers:
                for g in range(24):
                    pieces.append((wst_d[l, g], False))
        rstate = {'use': 0, 'iss': 0}

        def next_slot():
            i = rstate['use']
            while rstate['iss'] <= min(i + NSLOT - 2, len(pieces) - 1):
                j = rstate['iss']
                src, f32v = pieces[j]
                slot = ring[j % NSLOT]
                if f32v:
                    S.dma('sp', slot.v.cast(F32), src, f'ringh{j % NSLOT}')
                else:
                    S.dma('pool', slot.v, src, f'rings{j % NSLOT}')
                rstate['iss'] += 1
            rstate['use'] += 1
            return ring[i % NSLOT]

        S.dma('sp', sm.v, sm_d, 'const')
        S.dma('pool', gatew.v.rr('p l g c m -> p (l g c m)'), gw_d, 'const2')
        S.dma('pool', wtail.v.rr('p l k j -> p (l k j)'), wt_d, 'const3')
        S.dma('pool', masks.v.rr('p a b -> p (a b)'), mk_d, 'const4')
        S.cp('dve', ident_bf.v, identf)
        S.cp('dve', ones_bf.v, onesf)
        S.act(scT.v, smv('cT'), AF.Silu)
        for l in range(NL):
            S.memset('pool', Sst[l].v, 0.0)
            S.memset('pool', Sbf[l].v, 0.0)
            S.memset('pool', hst[l].v, 0.0)
            S.memset('pool', halo[l].v, 0.0)
        for l in layers:
            for q in range(24):
                slot = next_slot()
                wv = slot.v.cast(F32).rr('p (k n) -> p k n', k=KC)
                items = []
                for j in range(2):
                    col = q * 2 + j
                    for kc in range(KC):
                        items.append((psm[:, col:col + 1], wv[:, kc, j * 128:(j + 1) * 128], scT[:, kc:kc + 1],
                                      kc == 0, kc == KC - 1))
                S.mm(items)
            S.tt('dve', modT[l].v, psm[:, 0:48], smv(f'bmod{l}'), ALU.add)
            S.stt(gam1[l].v, modT[l][:, 8:16], 1.0, smv(f'nmw{l}'), ALU.add, ALU.mult)
            S.ts('dve', gam1[l].v, gam1[l].v, 32.0, None, ALU.mult)
            S.stt(gam2[l].v, modT[l][:, 32:40], 1.0, smv(f'nlw{l}'), ALU.add, ALU.mult)
            S.ts('dve', gam2[l].v, gam2[l].v, 32.0, None, ALU.mult)
            S.act(lsm['e'].v, smv(f'lam{l}'), AF.Exp, scale=-1.0)
            S.act(lsm['l1'].v, lsm['e'].v, AF.Ln, bias=1.0)
            S.ts('dve', ccol[l].v, lsm['l1'].v, -8.0, None, ALU.mult)
            S.ts('dve', c2col[l].v, lsm['l1'].v, -16.0, None, ALU.mult)
            S.ts('dve', lnws[l].v, smv(f'lnw{l}'), float(np.sqrt(512.0)), None, ALU.mult)
            S.ts('dve', gnws[l].v, smv(f'gnw{l}'), float(np.sqrt(128.0)), None, ALU.mult)
            S.act(negA[l].v, smv(f'alog{l}'), AF.Exp)
            S.ts('dve', negA[l].v, negA[l].v, -1.0, None, ALU.mult)
        S.ts('dve', fnws.v, smv('fnw'), 32.0, None, ALU.mult)

        bigi = {'i': 0}
        if stage == 0:
            S.dma('sp', xT.v, x_d.rearrange('(k p) n -> p k n', p=128)[:, :, 0:TT], 'xload')
            S.cp('dve', xT[:, 0, 0:48], modT[layers[0]].v)
            S.dma('sp', out_d.rearrange('(k p) n -> p k n', p=128)[:, :, 0:TT], xT.v, 'ostore')
            S.finish()
            S.emit()
            return nc

        def nbig():
            b = pbig[bigi['i'] % 2]
            bigi['i'] += 1
            return b

        def modnorm(gam, shcols, dst):
            S.act(hT.v, xT.v, AF.Square)
            S.mm([(pstat.v, ones_bf.v, hT[:, kc, :], kc == 0, kc == KC - 1) for kc in range(KC)])
            rs = Fs[6]
            S.rsq(rs.v, pstat.v, D * EPS)
            for kc in range(KC):
                t = tmpr[kc % 2]
                S.stt(t.v, xT[:, kc, :], gam[:, kc:kc + 1], rs.v, ALU.mult, ALU.mult)
                if shcols is None:
                    S.cp('act', dst[:, kc, :], t.v)
                else:
                    S.act(dst[:, kc, :], t.v, AF.Identity, bias=shcols[:, kc:kc + 1])

        for s in range(n_tiles):
            S.dma('sp', xT.v, x_d.rearrange('(k p) n -> p k n', p=128)[:, :, s * TT:(s + 1) * TT], 'xload')
            for l in layers:
                modnorm(gam1[l], modT[l][:, 0:8], hT)

                if stage == 1:
                    S.dma('sp', out_d.rearrange('(k p) n -> p k n', p=128)[:, :, 0:TT], xT.v, 'ostore')
                    S.finish()
                    S.emit()
                    return nc
                wslots = {}

                def wcol(cc):
                    pidx = {1: 0, 0: 1}.get(cc // 4, cc // 4)
                    if pidx not in wslots:
                        assert len(wslots) == pidx
                        wslots[pidx] = next_slot()
                    sl = wslots[pidx].v.rr('p (k n) -> p k n', k=KC)
                    return [sl[:, kc, (cc % 4) * 128:(cc % 4 + 1) * 128] for kc in range(KC)]

                def proj(cc):
                    pb = nbig()
                    wc = wcol(cc)
                    S.mm([(pb.v, wc[kc], hT[:, kc, :], kc == 0, kc == KC - 1) for kc in range(KC)])
                    return pb

                def conv(pb, ci, wname, widx, bias):
                    pr = pre[ci % 2]
                    S.cp('act', pr[:, 3:TT + 3], pb.v)
                    S.cp('pool', pr[:, 0:3], halo[l][:, ci, :])
                    S.cp('pool', halo[l][:, ci, :], pr[:, TT:TT + 3])
                    o, w = SM[wname]
                    wv = sm[:, o + widx * 4:o + widx * 4 + 4]
                    acc = Fs[6]
                    if bias is not None:
                        S.ts('dve', acc.v, pr[:, 0:TT], wv[:, 0:1], bias, ALU.mult, ALU.add)
                    else:
                        S.ts('dve', acc.v, pr[:, 0:TT], wv[:, 0:1], None, ALU.mult)
                    for k in range(1, 4):
                        S.stt(acc.v, pr[:, k:TT + k], wv[:, k:k + 1], acc.v, ALU.mult, ALU.add)
                    return acc

                for c in range(4):
                    pb = proj(4 + c)
                    if gelu_mode == 0:
                        S.act(gy[:, c, :], pb.v, AF.Gelu_apprx_tanh)
                    else:
                        y = Fs[0]
                        S.cp('act', y.v, pb.v)
                        S.tt('dve', Fs[1].v, y.v, y.v, ALU.mult)
                        S.ts('dve', Fs[1].v, Fs[1].v, 0.044715, 1.0, ALU.mult, ALU.add)
                        S.tt('dve', Fs[1].v, Fs[1].v, y.v, ALU.mult)
                        S.act(Fs[1].v, Fs[1].v, AF.Sigmoid, scale=float(2.0 * np.sqrt(2.0 / np.pi)))
                        S.tt('dve', gy[:, c, :], Fs[1].v, y.v, ALU.mult)

                if stage == 2:
                    S.dma('sp', out_d.rearrange('(k p) n -> p k n', p=128)[:, :, 0:TT], xT.v, 'ostore')
                    S.finish()
                    S.emit()
                    return nc
                for c in range(4):
                    pb = proj(c)
                    lcb = smv(f'lcb{l}')
                    xr = conv(pb, c, f'lcw{l}', c, lcb[:, c:c + 1])
                    S.cp('act', xrb.v, xr.v)
                    pr_ = nbig()
                    pi_ = nbig()
                    S.mm([(pr_.v, gatew[:, l, 0, c, :], xrb.v, True, True)])
                    S.mm([(pi_.v, gatew[:, l, 1, c, :], xrb.v, True, True)])
                    r, ig, a, a2, b, h = Fs[0], Fs[1], Fs[2], Fs[3], Fs[4], Fs[5]
                    S.act(r.v, pr_.v, AF.Sigmoid, bias=smv(f'gab{l}')[:, c:c + 1])
                    S.act(ig.v, pi_.v, AF.Sigmoid, bias=smv(f'gxb{l}')[:, c:c + 1])
                    S.act(a.v, r.v, AF.Exp, scale=ccol[l][:, c:c + 1])
                    S.act(a2.v, r.v, AF.Exp, scale=c2col[l][:, c:c + 1])
                    S.ts('dve', a2.v, a2.v, -1.0, 1.0, ALU.mult, ALU.add)
                    S.ts('dve', a2.v, a2.v, 1e-12, None, ALU.max)
                    S.act(a2.v, a2.v, AF.Sqrt)
                    S.tt('dve', b.v, ig.v, xr.v, ALU.mult)
                    S.tt('dve', b.v, b.v, a2.v, ALU.mult)
                    S.scan(h.v, a.v, b.v, hst[l][:, c:c + 1])
                    S.cp('pool', hst[l][:, c:c + 1], h[:, TT - 1:TT])
                    S.tt('dve', hg[:, c, :], h.v, gy[:, c, :], ALU.mult)
                    S.act(sqh.v, hg[:, c, :], AF.Square)
                    S.mm([(pstat.v, ones_bf.v, sqh.v, c == 0, c == 3)])
                rsl = Fs[0]
                S.rsq(rsl.v, pstat.v, 512 * EPS)
                for c in range(4):
                    S.stt(mixT[:, c, :], hg[:, c, :], lnws[l][:, c:c + 1], rsl.v, ALU.mult, ALU.mult)

                if stage == 3:
                    S.dma('sp', out_d.rearrange('(k p) n -> p k n', p=128)[:, :, 0:TT], xT.v, 'ostore')
                    S.finish()
                    S.emit()
                    return nc
                for c in range(12):
                    pb = proj(8 + c)
                    cv = conv(pb, 4 + c, f'gcw{l}', c, None)
                    S.act(qkv[:, c, :], cv.v, AF.Silu)
                for c in range(4):
                    pb = proj(20 + c)
                    S.act(sz[:, c, :], pb.v, AF.Silu)
                for n in range(4):
                    S.mm([(psm[:, n * 8:(n + 1) * 8], hT[:, kc, n * 128:(n + 1) * 128], wtail[:, l, kc, :],
                           kc == 0, kc == KC - 1) for kc in range(KC)])
                ab = psm[:, 0:32].rr('p (n j) -> p n j', j=8)
                S.act(gsm['beta'].v, ab[:, :, 0:4], AF.Sigmoid)
                S.tt('dve', gsm['xg'].v, ab[:, :, 4:8], smv(f'dtb{l}').un(1).bc([128, 4, 4]), ALU.add)
                S.act(gsm['ax'].v, gsm['xg'].v, AF.Abs)
                S.act(gsm['e'].v, gsm['ax'].v, AF.Exp, scale=-1.0)
                S.act(gsm['l1'].v, gsm['e'].v, AF.Ln, bias=1.0)
                S.stt(gsm['sp'].v, gsm['xg'].v, 0.0, gsm['l1'].v, ALU.max, ALU.add)
                S.tt('dve', gsm['g'].v, gsm['sp'].v, negA[l].v.un(1).bc([128, 4, 4]), ALU.mult)


                if stage == 4:
                    S.dma('sp', out_d.rearrange('(k p) n -> p k n', p=128)[:, :, 0:TT], xT.v, 'ostore')
                    S.finish()
                    S.emit()
                    return nc
                B4 = [128, 4, 128]
                fence([hT, hg, gy, tmpr[0]], carved)

                def gdn_chunk(n, T):
                    F, Bn, sqb_, cs, pgs, pc = T['F'], T['B'], T['sqb'], T['csm'], T['pg'], T['pc']
                    r4 = lambda v: v.rr('p (h t) -> p h t', h=4)
                    tsl = slice(n * 128, (n + 1) * 128)
                    qT_ = qkv[:, 0:4, tsl]
                    kT_ = qkv[:, 4:8, tsl]
                    vT_ = qkv[:, 8:12, tsl]
                    beta_n = gsm['beta'][:, n, :]
                    g_n = gsm['g'][:, n, :]
                    S.act(sqb_.v, qkv[:, 0:8, tsl], AF.Square)
                    S.mm([(pgs[0].v, ones_bf.v, sqb_[:, 0:4, :], True, True)])
                    S.mm([(pgs[1].v, ones_bf.v, sqb_[:, 4:8, :], True, True)])
                    yield
                    S.rsq(F[0].v, pgs[0].v, 1e-6)
                    S.rsq(F[1].v, pgs[1].v, 1e-6)
                    yield
                    S.stt(Bn['qn'].v, qT_, float(128.0 ** -0.5), r4(F[0].v), ALU.mult, ALU.mult)
                    S.tt('dve', Bn['kn'].v, kT_, r4(F[1].v), ALU.mult)
                    yield
                    S.tt('dve', r4(F[0].v), tri.un(1).bc(B4), g_n.un(2).bc(B4), ALU.mult)
                    S.mm([(pgs[2].v, onesf, F[0].v, True, True)])
                    S.mm([(psm[:, pc:pc + 4], tri, g_n, True, True)])
                    ptk = r4(ptr[:, 0:512])
                    ptv = r4(ptr[:, 512:1024])
                    yield
                    S.cp('act', cs['gcol'].v, psm[:, pc:pc + 4])
                    Grow = r4(pgs[2].v)
                    t3 = r4(F[1].v)
                    yield
                    S.tt('dve', t3, cs['gcol'].v.un(2).bc(B4), Grow, ALU.subtract)
                    Eg = r4(F[4].v)
                    S.act(cs['egcol'].v, cs['gcol'].v, AF.Exp)
                    S.tt('dve', cs['dk'].v, Grow[:, :, 127], cs['gcol'].v, ALU.subtract)
                    yield
                    aL = r4(F[2].v)
                    aU = r4(F[3].v)
                    S.tt('pool', aL, t3, negL.un(1).bc(B4), ALU.add)
                    S.tt('pool', aU, negU.un(1).bc(B4), t3, ALU.subtract)
                    S.act(cs['ekt'].v, cs['dk'].v, AF.Exp)
                    S.tt('dve', cs['bg'].v, beta_n, cs['egcol'].v, ALU.mult)
                    yield
                    S.act(F[2].v, F[2].v, AF.Exp)
                    S.act(F[3].v, F[3].v, AF.Exp)
                    S.act(F[4].v, pgs[2].v, AF.Exp)
                    yield
                    S.tr([(ptk[:, h, :], Bn['kn'][:, h, :], ident_bf.v) for h in range(4)])
                    S.tr([(ptv[:, h, :], vT_[:, h, :], ident_bf.v) for h in range(4)])
                    S.tt('dve', Bn['kbg'].v, ptk, cs['bg'].v.un(2).bc(B4), ALU.mult)
                    S.tt('dve', Bn['kt'].v, ptk, cs['ekt'].v.un(2).bc(B4), ALU.mult)
                    S.tt('dve', Bn['vb'].v, ptv, beta_n.un(2).bc(B4), ALU.mult)
                    yield
                    S.tt('pool', aL, aL, beta_n.un(2).bc(B4), ALU.mult)
                    pgr = r4(pgs[0].v)
                    pqk = r4(pgs[1].v)
                    S.mm([(pgr[:, h, :], Bn['kn'][:, h, :], Bn['kn'][:, h, :], True, True) for h in range(4)])
                    S.mm([(pqk[:, h, :], Bn['kn'][:, h, :], Bn['qn'][:, h, :], True, True) for h in range(4)])
                    S.tt('pool', Bn['qd'].v, Bn['qn'].v, Eg, ALU.mult)
                    yield
                    S.tt('dve', Bn['L0'].v, pgr, aL, ALU.mult)
                    S.tt('dve', Bn['attnT'].v, pqk, aU, ALU.mult)
                    yield
                    S.tr([(ptk[:, h, :], Bn['L0'][:, h, :], ident_bf.v) for h in range(4)])
                    S.cp('dve', Bn['U0'].v, ptk)
                    yield
                    idb = ident_bf.v.un(1).bc(B4)
                    pY = r4(pgs[0].v)
                    pYp = r4(pgs[1].v)
                    pZ = r4(pgs[2].v)
                    S.tt('pool', Bn['Oa'].v, Bn['U0'].v, masks[:, 0, :].un(1).bc(B4), ALU.mult)
                    S.tt('pool', Bn['Ob'].v, Bn['L0'].v, masks[:, 7, :].un(1).bc(B4), ALU.mult)
                    S.tt('pool', Bn['Da0'].v, idb, Bn['Oa'].v, ALU.subtract)
                    S.tt('pool', Bn['Db0'].v, idb, Bn['Ob'].v, ALU.subtract)
                    Dc, Dpc = 'Da0', 'Db0'
                    for lev in range(1, 7):
                        Dn, Dpn = ('Da1', 'Db1') if lev % 2 == 1 else ('Da0', 'Db0')
                        S.tt('pool', Bn['Oa'].v, Bn['U0'].v, masks[:, lev, :].un(1).bc(B4), ALU.mult)
                        S.tt('pool', Bn['Ob'].v, Bn['L0'].v, masks[:, 7 + lev, :].un(1).bc(B4), ALU.mult)
                        yield
                        S.mm([(pY[:, h, :], Bn['Ob'][:, h, :], Bn[Dc][:, h, :], True, True) for h in range(4)])
                        if lev < 6:
                            S.mm([(pYp[:, h, :], Bn['Oa'][:, h, :], Bn[Dpc][:, h, :], True, True) for h in range(4)])
                        yield
                        S.cp('act', Bn['Ya'].v, pY)
                        if lev < 6:
                            S.cp('dve', Bn['Yb'].v, pYp)
                        yield
                        S.mm([(pZ[:, h, :], Bn[Dpc][:, h, :], Bn['Ya'][:, h, :], True, True) for h in range(4)])
                        if lev < 6:
                            S.mm([(pY[:, h, :], Bn[Dc][:, h, :], Bn['Yb'][:, h, :], True, True) for h in range(4)])
                        yield
                        S.tt('dve', Bn[Dn].v, Bn[Dc].v, pZ, ALU.subtract)
                        if lev < 6:
                            S.tt('dve', Bn[Dpn].v, Bn[Dpc].v, pY, ALU.subtract)
                        Dc, Dpc = Dn, Dpn
                        yield
                    Mf = Bn[Dc]
                    pu = r4(pgs[0].v)
                    pw = r4(pgs[1].v)
                    S.mm([(pu[:, h, :], Mf[:, h, :], Bn['vb'][:, h, :], True, True) for h in range(4)])
                    S.mm([(pw[:, h, :], Bn['kbg'][:, h, :], Mf[:, h, :], True, True) for h in range(4)])
                    yield
                    S.cp('act', F[5].v, pgs[0].v)
                    S.cp('dve', Bn['wT'].v, pw)
                    yield
                    yield 'SCAN'
                    pws = r4(pgs[2].v)
                    S.mm([(pws[:, h, :], Bn['wT'][:, h, :], Sbf[l][:, h, :], True, True) for h in range(4)])
                    yield
                    S.tt('dve', Bn['vnew'].v, r4(F[5].v), pws, ALU.subtract)
                    yield
                    po = r4(pgs[0].v)
                    items = []
                    for h in range(4):
                        items.append((po[:, h, :], Sbf[l][:, h, :], Bn['qd'][:, h, :], True, False))
                        items.append((po[:, h, :], Bn['vnew'][:, h, :], Bn['attnT'][:, h, :], False, True))
                    S.mm(items)
                    pS = r4(pgs[1].v)
                    S.mm([(pS[:, h, :], Bn['kt'][:, h, :], Bn['vnew'][:, h, :], True, True) for h in range(4)])
                    S.tt('pool', Sst[l].v, Sst[l].v, Eg[:, :, 127].un(2).bc(B4), ALU.mult)
                    yield
                    S.tt('dve', Sst[l].v, Sst[l].v, pS, ALU.add)
                    S.cp('act', Sbf[l].v, Sst[l].v)
                    yield 'TAIL'
                    S.act(Bn['osq'].v, po, AF.Square)
                    S.cp('act', F[0].v, pgs[0].v)
                    yield
                    S.mm([(pgs[2].v, ones_bf.v, Bn['osq'].v, True, True)])
                    yield
                    S.rsq(F[1].v, pgs[2].v, 128 * EPS)
                    yield
                    S.tt('dve', F[6].v, F[0].v, F[1].v, ALU.mult)
                    S.stt(mixT[:, 4:8, tsl], r4(F[6].v), gnws[l][:, 0:1], sz[:, 0:4, tsl], ALU.mult, ALU.mult)
                    yield

                for pair in range(2):
                    gens = [gdn_chunk(2 * pair + i, Tsets[i]) for i in range(2)]
                    alive = [True, True]
                    while any(alive):
                        for i in range(2):
                            if alive[i]:
                                if next(gens[i]) == 'SCAN':
                                    alive[i] = False
                    for i in range(2):
                        while next(gens[i]) != 'TAIL':
                            pass
                    tl = [True, True]
                    while any(tl):
                        for i in range(2):
                            if tl[i]:
                                try:
                                    next(gens[i])
                                except StopIteration:
                                    tl[i] = False
                fence(carved, [hT, hg, gy, tmpr[0]])

                oslots = [next_slot() for g in range(2)]
                for fc in range(8):
                    sl = oslots[fc // 4].v.rr('p (k n) -> p k n', k=KC)
                    pb = nbig()
                    S.mm([(pb.v, sl[:, kc, (fc % 4) * 128:(fc % 4 + 1) * 128], mixT[:, kc, :], kc == 0, kc == KC - 1)
                          for kc in range(KC)])
                    S.stt(xT[:, fc, :], pb.v, modT[l][:, 16 + fc:17 + fc], xT[:, fc, :], ALU.mult, ALU.add)


                if stage == 6:
                    S.dma('sp', out_d.rearrange('(k p) n -> p k n', p=128)[:, :, 0:TT], xT.v, 'ostore')
                    S.finish()
                    S.emit()
                    return nc
                modnorm(gam2[l], modT[l][:, 24:32], hT)
                for hgp in range(8):
                    su = next_slot().v.rr('p (k n) -> p k n', k=KC)
                    sd = next_slot().v.rr('p (c f) -> p c f', c=4)
                    for hc in range(4):
                        pb = nbig()
                        S.mm([(pb.v, su[:, kc, hc * 128:(hc + 1) * 128], hT[:, kc, :], kc == 0, kc == KC - 1)
                              for kc in range(KC)])
                        hr = hidr[hc % 2]
                        S.act(hr.v, pb.v, AF.Relu)
                        S.tt('dve', hid[:, hc, :], hr.v, hr.v, ALU.mult)
                    for fc in range(8):
                        pb = nbig()
                        S.mm([(pb.v, sd[:, hc, fc * 128:(fc + 1) * 128], hid[:, hc, :], hc == 0, hc == 3)
                              for hc in range(4)])
                        S.stt(xT[:, fc, :], pb.v, modT[l][:, 40 + fc:41 + fc], xT[:, fc, :], ALU.mult, ALU.add)

            if final_norm:
                S.act(hT.v, xT.v, AF.Square)
                S.mm([(pstat.v, ones_bf.v, hT[:, kc, :], kc == 0, kc == KC - 1) for kc in range(KC)])
                S.rsq(Fs[6].v, pstat.v, D * EPS)
                for kc in range(KC):
                    S.stt(xT[:, kc, :], xT[:, kc, :], fnws[:, kc:kc + 1], Fs[6].v, ALU.mult, ALU.mult)
            S.dma('sp', out_d.rearrange('(k p) n -> p k n', p=128)[:, :, s * TT:(s + 1) * TT], xT.v, 'ostore')

        S.finish()
        S.emit()
    return nc


def _kpiece(W, c0, ncols):
    return np.ascontiguousarray(W[:, c0:c0 + ncols].reshape(KC, 128, ncols).transpose(1, 0, 2).reshape(128, KC * ncols))


def _col(v):
    v = np.asarray(v, np.float32).reshape(-1)
    return v.reshape(-1, 128).T


def prep_shared(inp):
    wst = np.empty((NL, 24, 128, 4096), np.float32)
    wmod = np.empty((NL, 24, 128, 2048), np.float32)
    gatew = np.zeros((128, NL, 2, 4, 128), np.float32)
    wtail = np.empty((128, NL, KC, 8), np.float32)
    for l in range(NL):
        for g in range(6):
            wst[l, g] = _kpiece(inp['w_in'][l], 512 * {0: 1, 1: 0}.get(g, g), 512)
        for g in range(2):
            wst[l, 6 + g] = _kpiece(inp['w_out'][l], 512 * g, 512)
        for hg in range(8):
            wst[l, 8 + 2 * hg] = _kpiece(inp['w_up'][l], 512 * hg, 512)
            wd = inp['w_down'][l][512 * hg:512 * hg + 512, :]
            wst[l, 9 + 2 * hg] = wd.reshape(4, 128, 1024).transpose(1, 0, 2).reshape(128, 4096)
        for q in range(24):
            wmod[l, q] = _kpiece(inp['w_mod'][l], 256 * q, 256)
        for gi, nm in enumerate(['lru_gate_a_w', 'lru_gate_x_w']):
            W = inp[nm][l]
            for c in range(4):
                for gb in range(2):
                    gatew[gb * 64:(gb + 1) * 64, l, gi, c, gb * 64:(gb + 1) * 64] = W[2 * c + gb]
        wtail[:, l] = inp['w_in'][l][:, 3072:3080].reshape(KC, 128, 8).transpose(1, 0, 2)
    return wst, wmod, gatew.reshape(128, -1), wtail.reshape(128, -1)


def prep_small(inp, b):
    sm = np.zeros((128, NS), np.float32)

    def put(name, arr):
        o, w = SM[name]
        sm[:, o:o + w] = np.asarray(arr, np.float32).reshape(128, w)
    put('cT', _col(inp['c'][b]))
    for l in range(NL):
        put(f'bmod{l}', _col(inp['b_mod'][l]))
        put(f'nmw{l}', _col(inp['norm_mix_w'][l]))
        put(f'nlw{l}', _col(inp['norm_mlp_w'][l]))
        put(f'lcw{l}', inp['lru_conv_w'][l].reshape(4, 4, 128).transpose(2, 1, 0))
        put(f'lcb{l}', _col(inp['lru_conv_b'][l]))
        put(f'gab{l}', _col(inp['lru_gate_a_b'][l]))
        put(f'gxb{l}', _col(inp['lru_gate_x_b'][l]))
        put(f'lam{l}', _col(inp['lru_lambda'][l]))
        put(f'lnw{l}', _col(inp['lru_norm_w'][l]))
        put(f'gcw{l}', inp['gdn_conv_w'][l].reshape(4, 12, 128).transpose(2, 1, 0))
        put(f'gnw{l}', _col(inp['gdn_norm_w'][l]))
        put(f'alog{l}', np.broadcast_to(inp['gdn_a_log'][l][None, :], (128, 4)))
        put(f'dtb{l}', np.broadcast_to(inp['gdn_dt_bias'][l][None, :], (128, 4)))
    put('fnw', _col(inp['final_norm_w']))
    i = np.arange(128)
    put('ident', np.eye(128, dtype=np.float32))
    put('tri', (i[:, None] <= i[None, :]).astype(np.float32))
    put('negL', np.where(i[None, :] < i[:, None], 0.0, -1e30))
    put('negU', np.where(i[:, None] <= i[None, :], 0.0, -1e30))
    put('ones', np.ones((128, 128), np.float32))
    return sm


def prep_masks():
    i = np.arange(128)
    m = np.zeros((128, 14, 128), np.float32)
    for lev in range(7):
        bj = i[:, None] >> lev
        bi = i[None, :] >> lev
        mu = ((bj % 2 == 0) & (bi == bj + 1)).astype(np.float32)
        m[:, lev, :] = mu
        m[:, 7 + lev, :] = mu.T
    return m.reshape(128, -1)


_NC_CACHE = {}


def run(inp, n_tiles, layers, final_norm=True, ncores=BATCH, gelu_mode=0, stage=99):
    inp = {k: np.asarray(v) for k, v in inp.items()}
    key = (n_tiles, tuple(layers), final_norm, gelu_mode, stage)
    if key not in _NC_CACHE:
        _NC_CACHE[key] = build_nc(n_tiles, layers, final_norm, gelu_mode, stage)
    nc = _NC_CACHE[key]
    wst, wmod, gatew, wtail = prep_shared(inp)
    ntok = n_tiles * TT
    in_maps = []
    for b in range(ncores):
        xT = np.ascontiguousarray(inp['x'][b, :ntok, :].T.astype(np.float32))
        in_maps.append({"xT": xT, "wst": wst, "wmod": wmod, "small": prep_small(inp, b),
                        "gatew": gatew, "wtail": wtail, "masks": prep_masks()})
    res = run_bass_kernel_spmd(nc, in_maps, core_ids=list(range(ncores)))
    out = np.stack([np.asarray(r["outT"]).T for r in res.results], axis=0)
    return out.astype(np.float32)


def kernel(**inputs):
    return run(inputs, SEQ // TT, list(range(NL)), True, BATCH, gelu_mode=GELU_MODE)


GELU_MODE = 1
```

```python
import numpy as np
from contextlib import ExitStack
import concourse.bass as bass
import concourse.mybir as mybir
from concourse.bass_utils import run_bass_kernel_spmd

F32 = mybir.dt.float32
BF16 = mybir.dt.bfloat16
ALU = mybir.AluOpType
AF = mybir.ActivationFunctionType

D = 1024
KC = 8
TT = 512
NL = 4
SEQ = 4096
BATCH = 4
NSLOT = 6
ENGS = ['pe', 'act', 'dve', 'pool', 'sp']
EPS = 1e-6

SM = {}
_off = 0


def _reg(name, w):
    global _off
    SM[name] = (_off, w)
    _off += w


_reg('cT', 8)
for _l in range(NL):
    for _n, _w in [('bmod', 48), ('nmw', 8), ('nlw', 8), ('lcw', 16), ('lcb', 4), ('gab', 4), ('gxb', 4),
                   ('lam', 4), ('lnw', 4), ('gcw', 48), ('gnw', 1), ('alog', 4), ('dtb', 4)]:
        _reg(f'{_n}{_l}', _w)
_reg('fnw', 8)
for _n in ['ident', 'tri', 'negL', 'negU', 'ones']:
    _reg(_n, 128)
NS = _off


class Tile:
    def __init__(self, name, ap):
        self.name = name
        self.ap = ap
        self.lw = None
        self.rd = {}
        self.rdd = []

    def __getitem__(self, k):
        return V(self, self.ap[k])

    @property
    def v(self):
        return V(self, self.ap)


class V:
    def __init__(self, tile, ap):
        self.tile = tile
        self.ap = ap

    def __getitem__(self, k):
        return V(self.tile, self.ap[k])

    def bc(self, shape):
        return V(self.tile, self.ap.broadcast_to(list(shape)))

    def un(self, ax):
        return V(self.tile, self.ap.unsqueeze(ax))

    def rr(self, pat, **kw):
        return V(self.tile, self.ap.rearrange(pat, **kw))

    def cast(self, dt):
        return V(self.tile, self.ap.bitcast(dt))


class Op:
    __slots__ = ('eng', 'fn', 'waits', 'sem', 'val', 'isdma', 'seq')


def _ap(x):
    return x.ap if isinstance(x, V) else x


class Sched:
    def __init__(self, nc, es):
        self.nc = nc
        self.es = es
        self.q = {e: [] for e in ENGS}
        self.cnt = {e: 0 for e in ENGS}
        self.seen = {e: {} for e in ENGS}
        self.semh = {}
        self.dcnt = {}
        for e in ENGS:
            self.semh[e] = es.enter_context(nc.semaphore('s_' + e))

    def dsem(self, name):
        if name not in self.semh:
            self.semh[name] = self.es.enter_context(self.nc.semaphore('d_' + name))
            self.dcnt[name] = 0
        return name

    def add(self, eng, fn, reads, writes, dma=None):
        op = Op()
        op.eng = eng
        op.fn = fn
        op.isdma = dma is not None
        deps = {}

        def need(d):
            if d is None:
                return
            if d.isdma:
                deps[d.sem] = max(deps.get(d.sem, 0), d.val)
            elif d.eng == eng and not op.isdma:
                if eng != 'pe':
                    deps[d.sem] = max(deps.get(d.sem, 0), d.val)
            else:
                deps[d.sem] = max(deps.get(d.sem, 0), d.val)

        rt = [x.tile for x in reads if isinstance(x, V)]
        wt = [x.tile for x in writes if isinstance(x, V)]
        for t in rt:
            need(t.lw)
        for t in wt:
            need(t.lw)
            for r in t.rd.values():
                need(r)
            for r in t.rdd:
                need(r)
        waits = []
        sn = self.seen[eng]
        for sem, val in deps.items():
            if sn.get(sem, 0) >= val:
                continue
            sn[sem] = val
            waits.append((sem, val))
        op.waits = waits
        if op.isdma:
            self.dsem(dma)
            self.dcnt[dma] += 16
            op.sem = dma
            op.val = self.dcnt[dma]
            op.seq = None
        else:
            self.cnt[eng] += 1
            op.sem = eng
            op.val = self.cnt[eng]
            op.seq = op.val
        for t in wt:
            t.lw = op
            t.rd = {}
            t.rdd = []
        wset = set(id(t) for t in wt)
        for t in rt:
            if id(t) in wset:
                continue
            if op.isdma:
                t.rdd.append(op)
            else:
                t.rd[eng] = op
        self.q[eng].append(op)
        return op

    def mm(self, items):
        reads = []
        writes = []
        for (o, l, r, st, sp) in items:
            reads += [l, r]
            writes.append(o)

        def fn(e):
            ins = None
            for (o, l, r, st, sp) in items:
                ins = e.matmul(_ap(o), _ap(l), _ap(r), start=st, stop=sp)
            return ins
        self.add('pe', fn, reads, writes)

    def tr(self, items):
        reads = []
        writes = []
        for (o, i, idn) in items:
            reads += [i, idn]
            writes.append(o)

        def fn(e):
            ins = None
            for (o, i, idn) in items:
                ins = e.transpose(_ap(o), _ap(i), _ap(idn))
            return ins
        self.add('pe', fn, reads, writes)

    def act(self, out, in_, func, bias=None, scale=None):
        reads = [in_]
        kw = {}
        if bias is not None:
            kw['bias'] = _ap(bias)
            reads.append(bias)
        if scale is not None:
            kw['scale'] = _ap(scale)
            reads.append(scale)
        self.add('act', lambda e: e.activation(_ap(out), _ap(in_), func, **kw), reads, [out])

    def tt(self, eng, out, in0, in1, op):
        self.add(eng, lambda e: e.tensor_tensor(_ap(out), _ap(in0), _ap(in1), op), [in0, in1], [out])

    def ts(self, eng, out, in0, s1, s2, op0, op1=None):
        kw = {}
        if op1 is not None:
            kw['op1'] = op1
        self.add(eng, lambda e: e.tensor_scalar(_ap(out), _ap(in0), _ap(s1), _ap(s2) if s2 is not None else None,
                                                op0, **kw), [in0, s1, s2], [out])

    def stt(self, out, in0, scalar, in1, op0, op1):
        self.add('dve', lambda e: e.scalar_tensor_tensor(_ap(out), _ap(in0), _ap(scalar), _ap(in1), op0, op1),
                 [in0, scalar, in1], [out])

    def scan(self, out, d0, d1, init):
        self.add('dve', lambda e: e.tensor_tensor_scan(_ap(out), _ap(d0), _ap(d1), _ap(init), ALU.mult, ALU.add),
                 [d0, d1, init], [out])

    def cp(self, eng, out, in_):
        if eng == 'act':
            self.add('act', lambda e: e.copy(_ap(out), _ap(in_)), [in_], [out])
        else:
            self.add(eng, lambda e: e.tensor_copy(_ap(out), _ap(in_)), [in_], [out])

    def memset(self, eng, out, val):
        self.add(eng, lambda e: e.memset(_ap(out), val), [], [out])

    def dma(self, q, out, in_, sem):
        self.add(q, lambda e: e.dma_start(out=_ap(out), in_=_ap(in_)), [in_], [out], dma=sem)

    def rsq(self, out, in_, eps):
        self.act(out, in_, AF.Ln, bias=float(eps))
        self.act(out, out, AF.Exp, scale=-0.5)

    def finish(self):
        op = Op()
        op.eng = 'sp'
        op.fn = None
        op.isdma = False
        op.waits = [(k, v) for k, v in self.dcnt.items()]
        self.q['sp'].append(op)

    def emit(self):
        nc = self.nc
        with nc.Block() as blk:
            def mk(en):
                def body(e):
                    for op in self.q[en]:
                        for (sem, val) in op.waits:
                            e.wait_ge(self.semh[sem], val)
                        if op.fn is None:
                            continue
                        ins = op.fn(e)
                        ins.then_inc(self.semh[op.sem], 16 if op.isdma else 1)
                return body
            blk.tensor(mk('pe'))
            blk.scalar(mk('act'))
            blk.vector(mk('dve'))
            blk.gpsimd(mk('pool'))
            blk.sync(mk('sp'))


def build_nc(n_tiles, layers, final_norm=True, gelu_mode=0, stage=99):
    ntok = n_tiles * TT
    nc = bass.Bass("TRN2", target_bir_lowering=False)
    x_d = nc.dram_tensor("xT", [D, ntok], F32, kind="ExternalInput").ap()
    wst_d = nc.dram_tensor("wst", [NL, 24, 128, 4096], F32, kind="ExternalInput").ap()
    wmod_d = nc.dram_tensor("wmod", [NL, 24, 128, 2048], F32, kind="ExternalInput").ap()
    sm_d = nc.dram_tensor("small", [128, NS], F32, kind="ExternalInput").ap()
    gw_d = nc.dram_tensor("gatew", [128, NL * 2 * 4 * 128], F32, kind="ExternalInput").ap()
    wt_d = nc.dram_tensor("wtail", [128, NL * 64], F32, kind="ExternalInput").ap()
    mk_d = nc.dram_tensor("masks", [128, 14 * 128], F32, kind="ExternalInput").ap()
    out_d = nc.dram_tensor("outT", [D, ntok], F32, kind="ExternalOutput").ap()

    es = ExitStack()
    with es:
        S = Sched(nc, es)

        def sb(name, shape, dt=F32):
            return Tile(name, es.enter_context(nc.sbuf_tensor(name, list(shape), dt))[:])

        def ps(name, shape, dt=F32):
            return Tile(name, es.enter_context(nc.psum_tensor(name, list(shape), dt))[:])

        xT = sb('xT_sb', [128, KC, TT])
        ring = [sb(f'ring{i}', [128, 4096], BF16) for i in range(NSLOT)]
        sm = sb('sm', [128, NS])
        gatew = sb('gatew_sb', [128, NL, 2, 4, 128], BF16)
        wtail = sb('wtail_sb', [128, NL, KC, 8], BF16)
        ident_bf = sb('ident_bf', [128, 128], BF16)
        masks = sb('masks_sb', [128, 14, 128], BF16)
        ones_bf = sb('ones_bf', [128, 128], BF16)
        scT = sb('scT', [128, 8])
        modT = [sb(f'modT{l}', [128, 48]) for l in range(NL)]
        gam1 = [sb(f'gam1_{l}', [128, 8]) for l in range(NL)]
        gam2 = [sb(f'gam2_{l}', [128, 8]) for l in range(NL)]
        ccol = [sb(f'ccol{l}', [128, 4]) for l in range(NL)]
        c2col = [sb(f'c2col{l}', [128, 4]) for l in range(NL)]
        lnws = [sb(f'lnws{l}', [128, 4]) for l in range(NL)]
        gnws = [sb(f'gnws{l}', [128, 1]) for l in range(NL)]
        negA = [sb(f'negA{l}', [128, 4]) for l in range(NL)]
        fnws = sb('fnws', [128, 8])
        Sst = [sb(f'S{l}', [128, 4, 128]) for l in range(NL)]
        Sbf = [sb(f'Sbf{l}', [128, 4, 128], BF16) for l in range(NL)]
        hst = [sb(f'hst{l}', [128, 4]) for l in range(NL)]
        halo = [sb(f'halo{l}', [128, 16, 3]) for l in range(NL)]
        hT = sb('hT', [128, KC, TT], BF16)
        tmpr = [sb(f'tmpr{i}', [128, TT]) for i in range(2)]
        pre = [sb(f'pre{i}', [128, TT + 3]) for i in range(2)]
        Fs = [sb(f'F{i}', [128, TT]) for i in range(7)]
        hg = sb('hg', [128, 4, TT])
        gy = sb('gy', [128, 4, TT], BF16)
        xrb = sb('xrb', [128, TT], BF16)
        sqh = sb('sqh', [128, TT], BF16)
        qkv = sb('qkv', [128, 12, TT], BF16)
        sz = sb('sz', [128, 4, TT], BF16)
        mixT = sb('mixT', [128, 8, TT], BF16)
        sqb = sb('sqb', [128, 8, 128], BF16)
        Bn = {n: sb('B_' + n, [128, 4, 128], BF16) for n in
              ['qn', 'kn', 'kbg', 'kt', 'vb', 'L0', 'U0', 'Oa', 'Ob', 'Da0', 'Da1', 'Db0', 'Db1', 'Ya', 'Yb', 'wT', 'qd', 'attnT', 'vnew', 'osq']}
        gsm = {n: sb('g_' + n, [128, 4, 4]) for n in ['beta', 'xg', 'ax', 'e', 'l1', 'sp', 'g']}
        csm = {n: sb('c_' + n, [128, 4]) for n in ['gcol', 'egcol', 'dk', 'ekt', 'bg']}
        hidr = [sb(f'hidr{i}', [128, TT], BF16) for i in range(2)]
        hid = sb('hid', [128, 4, TT], BF16)
        lsm = {n: sb('l_' + n, [128, 4]) for n in ['e', 'l1']}
        pbig = [ps('pbig0', [128, 512]), ps('pbig1', [128, 512])]
        pstat = ps('pstat', [128, 512])
        pg = [ps('pg0', [128, 512]), ps('pg1', [128, 512]), ps('pg2', [128, 512])]
        ptr = ps('ptr', [128, 1024], BF16)
        psm = ps('psm', [128, 512])

        Fs1 = [sb(f'G{i}', [128, TT]) for i in range(7)]
        carved = []

        def carve(parent_ap_bf, idx, name, shape3=True):
            ap = parent_ap_bf[:, idx * 512:(idx + 1) * 512]
            if shape3:
                ap = ap.rearrange('p (h t) -> p h t', h=4)
            t = Tile(name, ap)
            carved.append(t)
            return t
        hT_flat = hT.ap.rearrange('p k t -> p (k t)')
        hg_flat = hg.ap.rearrange('p k t -> p (k t)').bitcast(BF16)
        gy_flat = gy.ap.rearrange('p k t -> p (k t)')
        names1 = list(Bn.keys())
        Bn1 = {}
        for i, nm in enumerate(names1):
            if i < 8:
                Bn1[nm] = carve(hT_flat, i, 'C_' + nm)
            elif i < 16:
                Bn1[nm] = carve(hg_flat, i - 8, 'C_' + nm)
            else:
                Bn1[nm] = carve(gy_flat, i - 16, 'C_' + nm)
        sqb1 = Tile('C_sqb', tmpr[0].ap.bitcast(BF16).rearrange('p (a t) -> p a t', a=8))
        carved.append(sqb1)
        csm1 = {n: sb('c1_' + n, [128, 4]) for n in ['gcol', 'egcol', 'dk', 'ekt', 'bg']}
        Tsets = [
            {'F': Fs, 'B': Bn, 'sqb': sqb, 'csm': csm, 'pg': pg, 'pc': 64},
            {'F': Fs1, 'B': Bn1, 'sqb': sqb1, 'csm': csm1, 'pg': [pbig[0], pbig[1], pstat], 'pc': 72},
        ]

        def fence(frm, to):
            rd = {}
            rdd = []
            for t in frm:
                for e, op in t.rd.items():
                    if e not in rd or rd[e].seq < op.seq:
                        rd[e] = op
                rdd += t.rdd
                if t.lw is not None:
                    if t.lw.isdma:
                        rdd.append(t.lw)
                    elif t.lw.eng not in rd or rd[t.lw.eng].seq < t.lw.seq:
                        rd[t.lw.eng] = t.lw
            for t in to:
                for e, op in rd.items():
                    if e not in t.rd or t.rd[e].seq < op.seq:
                        t.rd[e] = op
                t.rdd = t.rdd + rdd

        def smv(name):
            o, w = SM[name]
            return sm[:, o:o + w]

        identf = smv('ident')
        tri = smv('tri')
        negL = smv('negL')
        negU = smv('negU')
        onesf = smv('ones')

        pieces = []
        for l in layers:
            for q in range(24):
                pieces.append((wmod_d[l, q], True))
        for s in range(n_tiles):
            for l in layers:
                for g in range(24):
                    pieces.append((wst_d[l, g], False))
        rstate = {'use': 0, 'iss': 0}

        def next_slot():
            i = rstate['use']
            while rstate['iss'] <= min(i + NSLOT - 2, len(pieces) - 1):
                j = rstate['iss']
                src, f32v = pieces[j]
                slot = ring[j % NSLOT]
                if f32v:
                    S.dma('sp', slot.v.cast(F32), src, f'ringh{j % NSLOT}')
                else:
                    S.dma('pool', slot.v, src, f'rings{j % NSLOT}')
                rstate['iss'] += 1
            rstate['use'] += 1
            return ring[i % NSLOT]

        S.dma('sp', sm.v, sm_d, 'const')
        S.dma('pool', gatew.v.rr('p l g c m -> p (l g c m)'), gw_d, 'const2')
        S.dma('pool', wtail.v.rr('p l k j -> p (l k j)'), wt_d, 'const3')
        S.dma('pool', masks.v.rr('p a b -> p (a b)'), mk_d, 'const4')
        S.cp('dve', ident_bf.v, identf)
        S.cp('dve', ones_bf.v, onesf)
        S.act(scT.v, smv('cT'), AF.Silu)
        for l in range(NL):
            S.memset('pool', Sst[l].v, 0.0)
            S.memset('pool', Sbf[l].v, 0.0)
            S.memset('pool', hst[l].v, 0.0)
            S.memset('pool', halo[l].v, 0.0)
        for l in layers:
            for q in range(24):
                slot = next_slot()
                wv = slot.v.cast(F32).rr('p (k n) -> p k n', k=KC)
                items = []
                for j in range(2):
                    col = q * 2 + j
                    for kc in range(KC):
                        items.append((psm[:, col:col + 1], wv[:, kc, j * 128:(j + 1) * 128], scT[:, kc:kc + 1],
                                      kc == 0, kc == KC - 1))
                S.mm(items)
            S.tt('dve', modT[l].v, psm[:, 0:48], smv(f'bmod{l}'), ALU.add)
            S.stt(gam1[l].v, modT[l][:, 8:16], 1.0, smv(f'nmw{l}'), ALU.add, ALU.mult)
            S.ts('dve', gam1[l].v, gam1[l].v, 32.0, None, ALU.mult)
            S.stt(gam2[l].v, modT[l][:, 32:40], 1.0, smv(f'nlw{l}'), ALU.add, ALU.mult)
            S.ts('dve', gam2[l].v, gam2[l].v, 32.0, None, ALU.mult)
            S.act(lsm['e'].v, smv(f'lam{l}'), AF.Exp, scale=-1.0)
            S.act(lsm['l1'].v, lsm['e'].v, AF.Ln, bias=1.0)
            S.ts('dve', ccol[l].v, lsm['l1'].v, -8.0, None, ALU.mult)
            S.ts('dve', c2col[l].v, lsm['l1'].v, -16.0, None, ALU.mult)
            S.ts('dve', lnws[l].v, smv(f'lnw{l}'), float(np.sqrt(512.0)), None, ALU.mult)
            S.ts('dve', gnws[l].v, smv(f'gnw{l}'), float(np.sqrt(128.0)), None, ALU.mult)
            S.act(negA[l].v, smv(f'alog{l}'), AF.Exp)
            S.ts('dve', negA[l].v, negA[l].v, -1.0, None, ALU.mult)
        S.ts('dve', fnws.v, smv('fnw'), 32.0, None, ALU.mult)

        bigi = {'i': 0}
        if stage == 0:
            S.dma('sp', xT.v, x_d.rearrange('(k p) n -> p k n', p=128)[:, :, 0:TT], 'xload')
            S.cp('dve', xT[:, 0, 0:48], modT[layers[0]].v)
            S.dma('sp', out_d.rearrange('(k p) n -> p k n', p=128)[:, :, 0:TT], xT.v, 'ostore')
            S.finish()
            S.emit()
            return nc

        def nbig():
            b = pbig[bigi['i'] % 2]
            bigi['i'] += 1
            return b

        def modnorm(gam, shcols, dst):
            S.act(hT.v, xT.v, AF.Square)
            S.mm([(pstat.v, ones_bf.v, hT[:, kc, :], kc == 0, kc == KC - 1) for kc in range(KC)])
            rs = Fs[6]
            S.rsq(rs.v, pstat.v, D * EPS)
            for kc in range(KC):
                t = tmpr[kc % 2]
                S.stt(t.v, xT[:, kc, :], gam[:, kc:kc + 1], rs.v, ALU.mult, ALU.mult)
                if shcols is None:
                    S.cp('act', dst[:, kc, :], t.v)
                else:
                    S.act(dst[:, kc, :], t.v, AF.Identity, bias=shcols[:, kc:kc + 1])

        for s in range(n_tiles):
            S.dma('sp', xT.v, x_d.rearrange('(k p) n -> p k n', p=128)[:, :, s * TT:(s + 1) * TT], 'xload')
            for l in layers:
                modnorm(gam1[l], modT[l][:, 0:8], hT)

                if stage == 1:
                    S.dma('sp', out_d.rearrange('(k p) n -> p k n', p=128)[:, :, 0:TT], xT.v, 'ostore')
                    S.finish()
                    S.emit()
                    return nc
                wslots = {}

                def wcol(cc):
                    pidx = {1: 0, 0: 1}.get(cc // 4, cc // 4)
                    if pidx not in wslots:
                        assert len(wslots) == pidx
                        wslots[pidx] = next_slot()
                    sl = wslots[pidx].v.rr('p (k n) -> p k n', k=KC)
                    return [sl[:, kc, (cc % 4) * 128:(cc % 4 + 1) * 128] for kc in range(KC)]

                def proj(cc):
                    pb = nbig()
                    wc = wcol(cc)
                    S.mm([(pb.v, wc[kc], hT[:, kc, :], kc == 0, kc == KC - 1) for kc in range(KC)])
                    return pb

                def conv(pb, ci, wname, widx, bias):
                    pr = pre[ci % 2]
                    S.cp('act', pr[:, 3:TT + 3], pb.v)
                    S.cp('pool', pr[:, 0:3], halo[l][:, ci, :])
                    S.cp('pool', halo[l][:, ci, :], pr[:, TT:TT + 3])
                    o, w = SM[wname]
                    wv = sm[:, o + widx * 4:o + widx * 4 + 4]
                    acc = (Fs if ci % 2 == 0 else Fs1)[6]
                    if bias is not None:
                        S.ts('dve', acc.v, pr[:, 0:TT], wv[:, 0:1], bias, ALU.mult, ALU.add)
                    else:
                        S.ts('dve', acc.v, pr[:, 0:TT], wv[:, 0:1], None, ALU.mult)
                    for k in range(1, 4):
                        S.stt(acc.v, pr[:, k:TT + k], wv[:, k:k + 1], acc.v, ALU.mult, ALU.add)
                    return acc

                for c in range(4):
                    pb = proj(4 + c)
                    if gelu_mode == 0:
                        S.act(gy[:, c, :], pb.v, AF.Gelu_apprx_tanh)
                    else:
                        FG = Fs if c % 2 == 0 else Fs1
                        y = FG[0]
                        S.cp('act', y.v, pb.v)
                        S.tt('dve', FG[1].v, y.v, y.v, ALU.mult)
                        S.ts('dve', FG[1].v, FG[1].v, 0.044715, 1.0, ALU.mult, ALU.add)
                        S.tt('dve', FG[1].v, FG[1].v, y.v, ALU.mult)
                        S.act(FG[1].v, FG[1].v, AF.Sigmoid, scale=float(2.0 * np.sqrt(2.0 / np.pi)))
                        S.tt('dve', gy[:, c, :], FG[1].v, y.v, ALU.mult)

                if stage == 2:
                    S.dma('sp', out_d.rearrange('(k p) n -> p k n', p=128)[:, :, 0:TT], xT.v, 'ostore')
                    S.finish()
                    S.emit()
                    return nc
                for c in range(4):
                    pb = proj(c)
                    lcb = smv(f'lcb{l}')
                    xr = conv(pb, c, f'lcw{l}', c, lcb[:, c:c + 1])
                    S.cp('act', xrb.v, xr.v)
                    pr_ = nbig()
                    pi_ = nbig()
                    S.mm([(pr_.v, gatew[:, l, 0, c, :], xrb.v, True, True)])
                    S.mm([(pi_.v, gatew[:, l, 1, c, :], xrb.v, True, True)])
                    FL = Fs if c % 2 == 0 else Fs1
                    r, ig, a, a2, b, h = FL[0], FL[1], FL[2], FL[3], FL[4], FL[5]
                    S.act(r.v, pr_.v, AF.Sigmoid, bias=smv(f'gab{l}')[:, c:c + 1])
                    S.act(ig.v, pi_.v, AF.Sigmoid, bias=smv(f'gxb{l}')[:, c:c + 1])
                    S.act(a.v, r.v, AF.Exp, scale=ccol[l][:, c:c + 1])
                    S.act(a2.v, r.v, AF.Exp, scale=c2col[l][:, c:c + 1])
                    S.ts('dve', a2.v, a2.v, -1.0, 1.0, ALU.mult, ALU.add)
                    S.ts('dve', a2.v, a2.v, 1e-12, None, ALU.max)
                    S.act(a2.v, a2.v, AF.Sqrt)
                    S.tt('dve', b.v, ig.v, xr.v, ALU.mult)
                    S.tt('dve', b.v, b.v, a2.v, ALU.mult)
                    S.scan(h.v, a.v, b.v, hst[l][:, c:c + 1])
                    S.cp('pool', hst[l][:, c:c + 1], h[:, TT - 1:TT])
                    S.tt('dve', hg[:, c, :], h.v, gy[:, c, :], ALU.mult)
                    S.act(sqh.v, hg[:, c, :], AF.Square)
                    S.mm([(pstat.v, ones_bf.v, sqh.v, c == 0, c == 3)])
                rsl = Fs[0]
                S.rsq(rsl.v, pstat.v, 512 * EPS)
                for c in range(4):
                    S.stt(mixT[:, c, :], hg[:, c, :], lnws[l][:, c:c + 1], rsl.v, ALU.mult, ALU.mult)

                if stage == 3:
                    S.dma('sp', out_d.rearrange('(k p) n -> p k n', p=128)[:, :, 0:TT], xT.v, 'ostore')
                    S.finish()
                    S.emit()
                    return nc
                for c in range(12):
                    pb = proj(8 + c)
                    cv = conv(pb, 4 + c, f'gcw{l}', c, None)
                    S.act(qkv[:, c, :], cv.v, AF.Silu)
                for c in range(4):
                    pb = proj(20 + c)
                    S.act(sz[:, c, :], pb.v, AF.Silu)
                for n in range(4):
                    S.mm([(psm[:, n * 8:(n + 1) * 8], hT[:, kc, n * 128:(n + 1) * 128], wtail[:, l, kc, :],
                           kc == 0, kc == KC - 1) for kc in range(KC)])
                ab = psm[:, 0:32].rr('p (n j) -> p n j', j=8)
                S.act(gsm['beta'].v, ab[:, :, 0:4], AF.Sigmoid)
                S.tt('dve', gsm['xg'].v, ab[:, :, 4:8], smv(f'dtb{l}').un(1).bc([128, 4, 4]), ALU.add)
                S.act(gsm['ax'].v, gsm['xg'].v, AF.Abs)
                S.act(gsm['e'].v, gsm['ax'].v, AF.Exp, scale=-1.0)
                S.act(gsm['l1'].v, gsm['e'].v, AF.Ln, bias=1.0)
                S.stt(gsm['sp'].v, gsm['xg'].v, 0.0, gsm['l1'].v, ALU.max, ALU.add)
                S.tt('dve', gsm['g'].v, gsm['sp'].v, negA[l].v.un(1).bc([128, 4, 4]), ALU.mult)


                if stage == 4:
                    S.dma('sp', out_d.rearrange('(k p) n -> p k n', p=128)[:, :, 0:TT], xT.v, 'ostore')
                    S.finish()
                    S.emit()
                    return nc
                B4 = [128, 4, 128]
                fence([hT, hg, gy, tmpr[0]], carved)

                def gdn_chunk(n, T):
                    F, Bn, sqb_, cs, pgs, pc = T['F'], T['B'], T['sqb'], T['csm'], T['pg'], T['pc']
                    r4 = lambda v: v.rr('p (h t) -> p h t', h=4)
                    tsl = slice(n * 128, (n + 1) * 128)
                    qT_ = qkv[:, 0:4, tsl]
                    kT_ = qkv[:, 4:8, tsl]
                    vT_ = qkv[:, 8:12, tsl]
                    beta_n = gsm['beta'][:, n, :]
                    g_n = gsm['g'][:, n, :]
                    S.act(sqb_.v, qkv[:, 0:8, tsl], AF.Square)
                    S.mm([(pgs[0].v, ones_bf.v, sqb_[:, 0:4, :], True, True)])
                    S.mm([(pgs[1].v, ones_bf.v, sqb_[:, 4:8, :], True, True)])
                    yield
                    S.rsq(F[0].v, pgs[0].v, 1e-6)
                    S.rsq(F[1].v, pgs[1].v, 1e-6)
                    yield
                    S.stt(Bn['qn'].v, qT_, float(128.0 ** -0.5), r4(F[0].v), ALU.mult, ALU.mult)
                    S.tt('dve', Bn['kn'].v, kT_, r4(F[1].v), ALU.mult)
                    yield
                    S.tt('dve', r4(F[0].v), tri.un(1).bc(B4), g_n.un(2).bc(B4), ALU.mult)
                    S.mm([(pgs[2].v, onesf, F[0].v, True, True)])
                    S.mm([(psm[:, pc:pc + 4], tri, g_n, True, True)])
                    ptk = r4(ptr[:, 0:512])
                    ptv = r4(ptr[:, 512:1024])
                    yield
                    S.cp('act', cs['gcol'].v, psm[:, pc:pc + 4])
                    Grow = r4(pgs[2].v)
                    t3 = r4(F[1].v)
                    yield
                    S.tt('dve', t3, cs['gcol'].v.un(2).bc(B4), Grow, ALU.subtract)
                    Eg = r4(F[4].v)
                    S.act(cs['egcol'].v, cs['gcol'].v, AF.Exp)
                    S.tt('dve', cs['dk'].v, Grow[:, :, 127], cs['gcol'].v, ALU.subtract)
                    yield
                    aL = r4(F[2].v)
                    aU = r4(F[3].v)
                    S.tt('pool', aL, t3, negL.un(1).bc(B4), ALU.add)
                    S.tt('pool', aU, negU.un(1).bc(B4), t3, ALU.subtract)
                    S.act(cs['ekt'].v, cs['dk'].v, AF.Exp)
                    S.tt('dve', cs['bg'].v, beta_n, cs['egcol'].v, ALU.mult)
                    yield
                    S.act(F[2].v, F[2].v, AF.Exp)
                    S.act(F[3].v, F[3].v, AF.Exp)
                    S.act(F[4].v, pgs[2].v, AF.Exp)
                    yield
                    S.tr([(ptk[:, h, :], Bn['kn'][:, h, :], ident_bf.v) for h in range(4)])
                    S.tr([(ptv[:, h, :], vT_[:, h, :], ident_bf.v) for h in range(4)])
                    S.tt('dve', Bn['kbg'].v, ptk, cs['bg'].v.un(2).bc(B4), ALU.mult)
                    S.tt('dve', Bn['kt'].v, ptk, cs['ekt'].v.un(2).bc(B4), ALU.mult)
                    S.tt('dve', Bn['vb'].v, ptv, beta_n.un(2).bc(B4), ALU.mult)
                    yield
                    S.tt('pool', aL, aL, beta_n.un(2).bc(B4), ALU.mult)
                    pgr = r4(pgs[0].v)
                    pqk = r4(pgs[1].v)
                    S.mm([(pgr[:, h, :], Bn['kn'][:, h, :], Bn['kn'][:, h, :], True, True) for h in range(4)])
                    S.mm([(pqk[:, h, :], Bn['kn'][:, h, :], Bn['qn'][:, h, :], True, True) for h in range(4)])
                    S.tt('pool', Bn['qd'].v, Bn['qn'].v, Eg, ALU.mult)
                    yield
                    S.tt('dve', Bn['L0'].v, pgr, aL, ALU.mult)
                    S.tt('dve', Bn['attnT'].v, pqk, aU, ALU.mult)
                    yield
                    S.tr([(ptk[:, h, :], Bn['L0'][:, h, :], ident_bf.v) for h in range(4)])
                    S.cp('dve', Bn['U0'].v, ptk)
                    yield
                    idb = ident_bf.v.un(1).bc(B4)
                    pY = r4(pgs[0].v)
                    pYp = r4(pgs[1].v)
                    pZ = r4(pgs[2].v)
                    S.tt('pool', Bn['Oa'].v, Bn['U0'].v, masks[:, 0, :].un(1).bc(B4), ALU.mult)
                    S.tt('pool', Bn['Ob'].v, Bn['L0'].v, masks[:, 7, :].un(1).bc(B4), ALU.mult)
                    S.tt('pool', Bn['Da0'].v, idb, Bn['Oa'].v, ALU.subtract)
                    S.tt('pool', Bn['Db0'].v, idb, Bn['Ob'].v, ALU.subtract)
                    Dc, Dpc = 'Da0', 'Db0'
                    for lev in range(1, 7):
                        Dn, Dpn = ('Da1', 'Db1') if lev % 2 == 1 else ('Da0', 'Db0')
                        S.tt('pool', Bn['Oa'].v, Bn['U0'].v, masks[:, lev, :].un(1).bc(B4), ALU.mult)
                        S.tt('pool', Bn['Ob'].v, Bn['L0'].v, masks[:, 7 + lev, :].un(1).bc(B4), ALU.mult)
                        yield
                        S.mm([(pY[:, h, :], Bn['Ob'][:, h, :], Bn[Dc][:, h, :], True, True) for h in range(4)])
                        if lev < 6:
                            S.mm([(pYp[:, h, :], Bn['Oa'][:, h, :], Bn[Dpc][:, h, :], True, True) for h in range(4)])
                        yield
                        S.cp('act', Bn['Ya'].v, pY)
                        if lev < 6:
                            S.cp('dve', Bn['Yb'].v, pYp)
                        yield
                        S.mm([(pZ[:, h, :], Bn[Dpc][:, h, :], Bn['Ya'][:, h, :], True, True) for h in range(4)])
                        if lev < 6:
                            S.mm([(pY[:, h, :], Bn[Dc][:, h, :], Bn['Yb'][:, h, :], True, True) for h in range(4)])
                        yield
                        S.tt('dve', Bn[Dn].v, Bn[Dc].v, pZ, ALU.subtract)
                        if lev < 6:
                            S.tt('dve', Bn[Dpn].v, Bn[Dpc].v, pY, ALU.subtract)
                        Dc, Dpc = Dn, Dpn
                        yield
                    Mf = Bn[Dc]
                    pu = r4(pgs[0].v)
                    pw = r4(pgs[1].v)
                    S.mm([(pu[:, h, :], Mf[:, h, :], Bn['vb'][:, h, :], True, True) for h in range(4)])
                    S.mm([(pw[:, h, :], Bn['kbg'][:, h, :], Mf[:, h, :], True, True) for h in range(4)])
                    yield
                    S.cp('act', F[5].v, pgs[0].v)
                    S.cp('dve', Bn['wT'].v, pw)
                    yield
                    yield 'SCAN'
                    pws = r4(pgs[2].v)
                    S.mm([(pws[:, h, :], Bn['wT'][:, h, :], Sbf[l][:, h, :], True, True) for h in range(4)])
                    yield
                    S.tt('dve', Bn['vnew'].v, r4(F[5].v), pws, ALU.subtract)
                    yield
                    po = r4(pgs[0].v)
                    items = []
                    for h in range(4):
                        items.append((po[:, h, :], Sbf[l][:, h, :], Bn['qd'][:, h, :], True, False))
                        items.append((po[:, h, :], Bn['vnew'][:, h, :], Bn['attnT'][:, h, :], False, True))
                    S.mm(items)
                    pS = r4(pgs[1].v)
                    S.mm([(pS[:, h, :], Bn['kt'][:, h, :], Bn['vnew'][:, h, :], True, True) for h in range(4)])
                    S.tt('pool', Sst[l].v, Sst[l].v, Eg[:, :, 127].un(2).bc(B4), ALU.mult)
                    yield
                    S.tt('dve', Sst[l].v, Sst[l].v, pS, ALU.add)
                    S.cp('act', Sbf[l].v, Sst[l].v)
                    yield 'TAIL'
                    S.act(Bn['osq'].v, po, AF.Square)
                    S.cp('act', F[0].v, pgs[0].v)
                    yield
                    S.mm([(pgs[2].v, ones_bf.v, Bn['osq'].v, True, True)])
                    yield
                    S.rsq(F[1].v, pgs[2].v, 128 * EPS)
                    yield
                    S.tt('dve', F[6].v, F[0].v, F[1].v, ALU.mult)
                    S.stt(mixT[:, 4:8, tsl], r4(F[6].v), gnws[l][:, 0:1], sz[:, 0:4, tsl], ALU.mult, ALU.mult)
                    yield

                for pair in range(2):
                    gens = [gdn_chunk(2 * pair + i, Tsets[i]) for i in range(2)]
                    alive = [True, True]
                    while any(alive):
                        for i in range(2):
                            if alive[i]:
                                if next(gens[i]) == 'SCAN':
                                    alive[i] = False
                    for i in range(2):
                        while next(gens[i]) != 'TAIL':
                            pass
                    tl = [True, True]
                    while any(tl):
                        for i in range(2):
                            if tl[i]:
                                try:
                                    next(gens[i])
                                except StopIteration:
                                    tl[i] = False
                fence(carved, [hT, hg, gy, tmpr[0]])

                oslots = [next_slot() for g in range(2)]
                for fc in range(8):
                    sl = oslots[fc // 4].v.rr('p (k n) -> p k n', k=KC)
                    pb = nbig()
                    S.mm([(pb.v, sl[:, kc, (fc % 4) * 128:(fc % 4 + 1) * 128], mixT[:, kc, :], kc == 0, kc == KC - 1)
                          for kc in range(KC)])
                    S.stt(xT[:, fc, :], pb.v, modT[l][:, 16 + fc:17 + fc], xT[:, fc, :], ALU.mult, ALU.add)


                if stage == 6:
                    S.dma('sp', out_d.rearrange('(k p) n -> p k n', p=128)[:, :, 0:TT], xT.v, 'ostore')
                    S.finish()
                    S.emit()
                    return nc
                modnorm(gam2[l], modT[l][:, 24:32], hT)
                for hgp in range(8):
                    su = next_slot().v.rr('p (k n) -> p k n', k=KC)
                    sd = next_slot().v.rr('p (c f) -> p c f', c=4)
                    for hc in range(4):
                        pb = nbig()
                        S.mm([(pb.v, su[:, kc, hc * 128:(hc + 1) * 128], hT[:, kc, :], kc == 0, kc == KC - 1)
                              for kc in range(KC)])
                        hr = hidr[hc % 2]
                        S.act(hr.v, pb.v, AF.Relu)
                        S.tt('dve', hid[:, hc, :], hr.v, hr.v, ALU.mult)
                    for fc in range(8):
                        pb = nbig()
                        S.mm([(pb.v, sd[:, hc, fc * 128:(fc + 1) * 128], hid[:, hc, :], hc == 0, hc == 3)
                              for hc in range(4)])
                        S.stt(xT[:, fc, :], pb.v, modT[l][:, 40 + fc:41 + fc], xT[:, fc, :], ALU.mult, ALU.add)

            if final_norm:
                S.act(hT.v, xT.v, AF.Square)
                S.mm([(pstat.v, ones_bf.v, hT[:, kc, :], kc == 0, kc == KC - 1) for kc in range(KC)])
                S.rsq(Fs[6].v, pstat.v, D * EPS)
                for kc in range(KC):
                    S.stt(xT[:, kc, :], xT[:, kc, :], fnws[:, kc:kc + 1], Fs[6].v, ALU.mult, ALU.mult)
            S.dma('sp', out_d.rearrange('(k p) n -> p k n', p=128)[:, :, s * TT:(s + 1) * TT], xT.v, 'ostore')

        S.finish()
        S.emit()
    return nc


def _kpiece(W, c0, ncols):
    return np.ascontiguousarray(W[:, c0:c0 + ncols].reshape(KC, 128, ncols).transpose(1, 0, 2).reshape(128, KC * ncols))


def _col(v):
    v = np.asarray(v, np.float32).reshape(-1)
    return v.reshape(-1, 128).T


def prep_shared(inp):
    wst = np.empty((NL, 24, 128, 4096), np.float32)
    wmod = np.empty((NL, 24, 128, 2048), np.float32)
    gatew = np.zeros((128, NL, 2, 4, 128), np.float32)
    wtail = np.empty((128, NL, KC, 8), np.float32)
    for l in range(NL):
        for g in range(6):
            wst[l, g] = _kpiece(inp['w_in'][l], 512 * {0: 1, 1: 0}.get(g, g), 512)
        for g in range(2):
            wst[l, 6 + g] = _kpiece(inp['w_out'][l], 512 * g, 512)
        for hg in range(8):
            wst[l, 8 + 2 * hg] = _kpiece(inp['w_up'][l], 512 * hg, 512)
            wd = inp['w_down'][l][512 * hg:512 * hg + 512, :]
            wst[l, 9 + 2 * hg] = wd.reshape(4, 128, 1024).transpose(1, 0, 2).reshape(128, 4096)
        for q in range(24):
            wmod[l, q] = _kpiece(inp['w_mod'][l], 256 * q, 256)
        for gi, nm in enumerate(['lru_gate_a_w', 'lru_gate_x_w']):
            W = inp[nm][l]
            for c in range(4):
                for gb in range(2):
                    gatew[gb * 64:(gb + 1) * 64, l, gi, c, gb * 64:(gb + 1) * 64] = W[2 * c + gb]
        wtail[:, l] = inp['w_in'][l][:, 3072:3080].reshape(KC, 128, 8).transpose(1, 0, 2)
    return wst, wmod, gatew.reshape(128, -1), wtail.reshape(128, -1)


def prep_small(inp, b):
    sm = np.zeros((128, NS), np.float32)

    def put(name, arr):
        o, w = SM[name]
        sm[:, o:o + w] = np.asarray(arr, np.float32).reshape(128, w)
    put('cT', _col(inp['c'][b]))
    for l in range(NL):
        put(f'bmod{l}', _col(inp['b_mod'][l]))
        put(f'nmw{l}', _col(inp['norm_mix_w'][l]))
        put(f'nlw{l}', _col(inp['norm_mlp_w'][l]))
        put(f'lcw{l}', inp['lru_conv_w'][l].reshape(4, 4, 128).transpose(2, 1, 0))
        put(f'lcb{l}', _col(inp['lru_conv_b'][l]))
        put(f'gab{l}', _col(inp['lru_gate_a_b'][l]))
        put(f'gxb{l}', _col(inp['lru_gate_x_b'][l]))
        put(f'lam{l}', _col(inp['lru_lambda'][l]))
        put(f'lnw{l}', _col(inp['lru_norm_w'][l]))
        put(f'gcw{l}', inp['gdn_conv_w'][l].reshape(4, 12, 128).transpose(2, 1, 0))
        put(f'gnw{l}', _col(inp['gdn_norm_w'][l]))
        put(f'alog{l}', np.broadcast_to(inp['gdn_a_log'][l][None, :], (128, 4)))
        put(f'dtb{l}', np.broadcast_to(inp['gdn_dt_bias'][l][None, :], (128, 4)))
    put('fnw', _col(inp['final_norm_w']))
    i = np.arange(128)
    put('ident', np.eye(128, dtype=np.float32))
    put('tri', (i[:, None] <= i[None, :]).astype(np.float32))
    put('negL', np.where(i[None, :] < i[:, None], 0.0, -1e30))
    put('negU', np.where(i[:, None] <= i[None, :], 0.0, -1e30))
    put('ones', np.ones((128, 128), np.float32))
    return sm


def prep_masks():
    i = np.arange(128)
    m = np.zeros((128, 14, 128), np.float32)
    for lev in range(7):
        bj = i[:, None] >> lev
        bi = i[None, :] >> lev
        mu = ((bj % 2 == 0) & (bi == bj + 1)).astype(np.float32)
        m[:, lev, :] = mu
        m[:, 7 + lev, :] = mu.T
    return m.reshape(128, -1)


_NC_CACHE = {}


def run(inp, n_tiles, layers, final_norm=True, ncores=BATCH, gelu_mode=0, stage=99):
    inp = {k: np.asarray(v) for k, v in inp.items()}
    key = (n_tiles, tuple(layers), final_norm, gelu_mode, stage)
    if key not in _NC_CACHE:
        _NC_CACHE[key] = build_nc(n_tiles, layers, final_norm, gelu_mode, stage)
    nc = _NC_CACHE[key]
    wst, wmod, gatew, wtail = prep_shared(inp)
    ntok = n_tiles * TT
    in_maps = []
    for b in range(ncores):
        xT = np.ascontiguousarray(inp['x'][b, :ntok, :].T.astype(np.float32))
        in_maps.append({"xT": xT, "wst": wst, "wmod": wmod, "small": prep_small(inp, b),
                        "gatew": gatew, "wtail": wtail, "masks": prep_masks()})
    res = run_bass_kernel_spmd(nc, in_maps, core_ids=list(range(ncores)))
    out = np.stack([np.asarray(r["outT"]).T for r in res.results], axis=0)
    return out.astype(np.float32)


def kernel(**inputs):
    return run(inputs, SEQ // TT, list(range(NL)), True, BATCH, gelu_mode=GELU_MODE)


GELU_MODE = 1
```

```python
import numpy as np
from contextlib import ExitStack
import concourse.bass as bass
import concourse.mybir as mybir
from concourse.bass_utils import run_bass_kernel_spmd

F32 = mybir.dt.float32
BF16 = mybir.dt.bfloat16
ALU = mybir.AluOpType
AF = mybir.ActivationFunctionType

D = 1024
KC = 8
TT = 512
NL = 4
SEQ = 4096
BATCH = 4
NSLOT = 6
ENGS = ['pe', 'act', 'dve', 'pool', 'sp']
EPS = 1e-6

SM = {}
_off = 0


def _reg(name, w):
    global _off
    SM[name] = (_off, w)
    _off += w


_reg('cT', 8)
for _l in range(NL):
    for _n, _w in [('bmod', 48), ('nmw', 8), ('nlw', 8), ('lcw', 16), ('lcb', 4), ('gab', 4), ('gxb', 4),
                   ('lam', 4), ('lnw', 4), ('gcw', 48), ('gnw', 1), ('alog', 4), ('dtb', 4)]:
        _reg(f'{_n}{_l}', _w)
_reg('fnw', 8)
for _n in ['ident', 'tri', 'negL', 'negU', 'ones']:
    _reg(_n, 128)
NS = _off


class Tile:
    def __init__(self, name, ap):
        self.name = name
        self.ap = ap
        self.lw = None
        self.rd = {}
        self.rdd = []

    def __getitem__(self, k):
        return V(self, self.ap[k])

    @property
    def v(self):
        return V(self, self.ap)


class V:
    def __init__(self, tile, ap):
        self.tile = tile
        self.ap = ap

    def __getitem__(self, k):
        return V(self.tile, self.ap[k])

    def bc(self, shape):
        return V(self.tile, self.ap.broadcast_to(list(shape)))

    def un(self, ax):
        return V(self.tile, self.ap.unsqueeze(ax))

    def rr(self, pat, **kw):
        return V(self.tile, self.ap.rearrange(pat, **kw))

    def cast(self, dt):
        return V(self.tile, self.ap.bitcast(dt))


class Op:
    __slots__ = ('eng', 'fn', 'waits', 'sem', 'val', 'isdma', 'seq')


def _ap(x):
    return x.ap if isinstance(x, V) else x


class Sched:
    def __init__(self, nc, es):
        self.nc = nc
        self.es = es
        self.q = {e: [] for e in ENGS}
        self.cnt = {e: 0 for e in ENGS}
        self.seen = {e: {} for e in ENGS}
        self.semh = {}
        self.dcnt = {}
        for e in ENGS:
            self.semh[e] = es.enter_context(nc.semaphore('s_' + e))

    def dsem(self, name):
        if name not in self.semh:
            self.semh[name] = self.es.enter_context(self.nc.semaphore('d_' + name))
            self.dcnt[name] = 0
        return name

    def add(self, eng, fn, reads, writes, dma=None):
        op = Op()
        op.eng = eng
        op.fn = fn
        op.isdma = dma is not None
        deps = {}

        def need(d):
            if d is None:
                return
            if d.isdma:
                deps[d.sem] = max(deps.get(d.sem, 0), d.val)
            elif d.eng == eng and not op.isdma:
                if eng != 'pe':
                    deps[d.sem] = max(deps.get(d.sem, 0), d.val)
            else:
                deps[d.sem] = max(deps.get(d.sem, 0), d.val)

        rt = [x.tile for x in reads if isinstance(x, V)]
        wt = [x.tile for x in writes if isinstance(x, V)]
        for t in rt:
            need(t.lw)
        for t in wt:
            need(t.lw)
            for r in t.rd.values():
                need(r)
            for r in t.rdd:
                need(r)
        waits = []
        sn = self.seen[eng]
        for sem, val in deps.items():
            if sn.get(sem, 0) >= val:
                continue
            sn[sem] = val
            waits.append((sem, val))
        op.waits = waits
        if op.isdma:
            self.dsem(dma)
            self.dcnt[dma] += 16
            op.sem = dma
            op.val = self.dcnt[dma]
            op.seq = None
        else:
            self.cnt[eng] += 1
            op.sem = eng
            op.val = self.cnt[eng]
            op.seq = op.val
        for t in wt:
            t.lw = op
            t.rd = {}
            t.rdd = []
        wset = set(id(t) for t in wt)
        for t in rt:
            if id(t) in wset:
                continue
            if op.isdma:
                t.rdd.append(op)
            else:
                t.rd[eng] = op
        self.q[eng].append(op)
        return op

    def mm(self, items):
        reads = []
        writes = []
        for (o, l, r, st, sp) in items:
            reads += [l, r]
            writes.append(o)

        def fn(e):
            ins = None
            for (o, l, r, st, sp) in items:
                ins = e.matmul(_ap(o), _ap(l), _ap(r), start=st, stop=sp)
            return ins
        self.add('pe', fn, reads, writes)

    def tr(self, items):
        reads = []
        writes = []
        for (o, i, idn) in items:
            reads += [i, idn]
            writes.append(o)

        def fn(e):
            ins = None
            for (o, i, idn) in items:
                ins = e.transpose(_ap(o), _ap(i), _ap(idn))
            return ins
        self.add('pe', fn, reads, writes)

    def act(self, out, in_, func, bias=None, scale=None):
        reads = [in_]
        kw = {}
        if bias is not None:
            kw['bias'] = _ap(bias)
            reads.append(bias)
        if scale is not None:
            kw['scale'] = _ap(scale)
            reads.append(scale)
        self.add('act', lambda e: e.activation(_ap(out), _ap(in_), func, **kw), reads, [out])

    def tt(self, eng, out, in0, in1, op):
        self.add(eng, lambda e: e.tensor_tensor(_ap(out), _ap(in0), _ap(in1), op), [in0, in1], [out])

    def ts(self, eng, out, in0, s1, s2, op0, op1=None):
        kw = {}
        if op1 is not None:
            kw['op1'] = op1
        self.add(eng, lambda e: e.tensor_scalar(_ap(out), _ap(in0), _ap(s1), _ap(s2) if s2 is not None else None,
                                                op0, **kw), [in0, s1, s2], [out])

    def stt(self, out, in0, scalar, in1, op0, op1):
        self.add('dve', lambda e: e.scalar_tensor_tensor(_ap(out), _ap(in0), _ap(scalar), _ap(in1), op0, op1),
                 [in0, scalar, in1], [out])

    def scan(self, out, d0, d1, init):
        self.add('dve', lambda e: e.tensor_tensor_scan(_ap(out), _ap(d0), _ap(d1), _ap(init), ALU.mult, ALU.add),
                 [d0, d1, init], [out])

    def cp(self, eng, out, in_):
        if eng == 'act':
            self.add('act', lambda e: e.copy(_ap(out), _ap(in_)), [in_], [out])
        else:
            self.add(eng, lambda e: e.tensor_copy(_ap(out), _ap(in_)), [in_], [out])

    def memset(self, eng, out, val):
        self.add(eng, lambda e: e.memset(_ap(out), val), [], [out])

    def dma(self, q, out, in_, sem):
        self.add(q, lambda e: e.dma_start(out=_ap(out), in_=_ap(in_)), [in_], [out], dma=sem)

    def rsq(self, out, in_, eps):
        self.act(out, in_, AF.Ln, bias=float(eps))
        self.act(out, out, AF.Exp, scale=-0.5)

    def finish(self):
        op = Op()
        op.eng = 'sp'
        op.fn = None
        op.isdma = False
        op.waits = [(k, v) for k, v in self.dcnt.items()]
        self.q['sp'].append(op)

    def emit(self):
        nc = self.nc
        with nc.Block() as blk:
            def mk(en):
                def body(e):
                    for op in self.q[en]:
                        for (sem, val) in op.waits:
                            e.wait_ge(self.semh[sem], val)
                        if op.fn is None:
                            continue
                        ins = op.fn(e)
                        ins.then_inc(self.semh[op.sem], 16 if op.isdma else 1)
                return body
            blk.tensor(mk('pe'))
            blk.scalar(mk('act'))
            blk.vector(mk('dve'))
            blk.gpsimd(mk('pool'))
            blk.sync(mk('sp'))


def build_nc(n_tiles, layers, final_norm=True, gelu_mode=0, stage=99):
    ntok = n_tiles * TT
    nc = bass.Bass("TRN2", target_bir_lowering=False)
    x_d = nc.dram_tensor("xT", [D, ntok], F32, kind="ExternalInput").ap()
    wst_d = nc.dram_tensor("wst", [NL, 24, 128, 4096], F32, kind="ExternalInput").ap()
    wmod_d = nc.dram_tensor("wmod", [NL, 24, 128, 2048], F32, kind="ExternalInput").ap()
    sm_d = nc.dram_tensor("small", [128, NS], F32, kind="ExternalInput").ap()
    gw_d = nc.dram_tensor("gatew", [128, NL * 2 * 4 * 128], F32, kind="ExternalInput").ap()
    wt_d = nc.dram_tensor("wtail", [128, NL * 64], F32, kind="ExternalInput").ap()
    mk_d = nc.dram_tensor("masks", [128, 14 * 128], F32, kind="ExternalInput").ap()
    out_d = nc.dram_tensor("outT", [D, ntok], F32, kind="ExternalOutput").ap()

    es = ExitStack()
    with es:
        S = Sched(nc, es)

        def sb(name, shape, dt=F32):
            return Tile(name, es.enter_context(nc.sbuf_tensor(name, list(shape), dt))[:])

        def ps(name, shape, dt=F32):
            return Tile(name, es.enter_context(nc.psum_tensor(name, list(shape), dt))[:])

        xT = sb('xT_sb', [128, KC, TT])
        ring = [sb(f'ring{i}', [128, 4096], BF16) for i in range(NSLOT)]
        sm = sb('sm', [128, NS])
        gatew = sb('gatew_sb', [128, NL, 2, 4, 128], BF16)
        wtail = sb('wtail_sb', [128, NL, KC, 8], BF16)
        ident_bf = sb('ident_bf', [128, 128], BF16)
        masks = sb('masks_sb', [128, 14, 128], BF16)
        ones_bf = sb('ones_bf', [128, 128], BF16)
        scT = sb('scT', [128, 8])
        modT = [sb(f'modT{l}', [128, 48]) for l in range(NL)]
        gam1 = [sb(f'gam1_{l}', [128, 8]) for l in range(NL)]
        gam2 = [sb(f'gam2_{l}', [128, 8]) for l in range(NL)]
        ccol = [sb(f'ccol{l}', [128, 4]) for l in range(NL)]
        c2col = [sb(f'c2col{l}', [128, 4]) for l in range(NL)]
        lnws = [sb(f'lnws{l}', [128, 4]) for l in range(NL)]
        gnws = [sb(f'gnws{l}', [128, 1]) for l in range(NL)]
        negA = [sb(f'negA{l}', [128, 4]) for l in range(NL)]
        fnws = sb('fnws', [128, 8])
        Sst = [sb(f'S{l}', [128, 4, 128]) for l in range(NL)]
        Sbf = [sb(f'Sbf{l}', [128, 4, 128], BF16) for l in range(NL)]
        hst = [sb(f'hst{l}', [128, 4]) for l in range(NL)]
        halo = [sb(f'halo{l}', [128, 16, 3]) for l in range(NL)]
        hT = sb('hT', [128, KC, TT], BF16)
        tmpr = [sb(f'tmpr{i}', [128, TT]) for i in range(2)]
        pre = [sb(f'pre{i}', [128, TT + 3]) for i in range(2)]
        Fs = [sb(f'F{i}', [128, TT]) for i in range(7)]
        hg = sb('hg', [128, 4, TT])
        gy = sb('gy', [128, 4, TT], BF16)
        xrb = sb('xrb', [128, TT], BF16)
        sqh = sb('sqh', [128, TT], BF16)
        qkv = sb('qkv', [128, 12, TT], BF16)
        sz = sb('sz', [128, 4, TT], BF16)
        mixT = sb('mixT', [128, 8, TT], BF16)
        sqb = sb('sqb', [128, 8, 128], BF16)
        Bn = {n: sb('B_' + n, [128, 4, 128], BF16) for n in
              ['qn', 'kn', 'kbg', 'kt', 'vb', 'L0', 'U0', 'Oa', 'Ob', 'Da0', 'Da1', 'Db0', 'Db1', 'Ya', 'Yb', 'wT', 'qd', 'attnT', 'vnew', 'osq']}
        gsm = {n: sb('g_' + n, [128, 4, 4]) for n in ['beta', 'xg', 'ax', 'e', 'l1', 'sp', 'g']}
        csm = {n: sb('c_' + n, [128, 4]) for n in ['gcol', 'egcol', 'dk', 'ekt', 'bg']}
        hidr = [sb(f'hidr{i}', [128, TT], BF16) for i in range(2)]
        hid = sb('hid', [128, 4, TT], BF16)
        lsm = {n: sb('l_' + n, [128, 4]) for n in ['e', 'l1']}
        pbig = [ps('pbig0', [128, 512]), ps('pbig1', [128, 512])]
        pstat = ps('pstat', [128, 512])
        pg = [ps('pg0', [128, 512]), ps('pg1', [128, 512]), ps('pg2', [128, 512])]
        ptr = ps('ptr', [128, 1024], BF16)
        psm = ps('psm', [128, 512])

        Fs1 = [sb(f'G{i}', [128, TT]) for i in range(7)]
        carved = []

        def carve(parent_ap_bf, idx, name, shape3=True):
            ap = parent_ap_bf[:, idx * 512:(idx + 1) * 512]
            if shape3:
                ap = ap.rearrange('p (h t) -> p h t', h=4)
            t = Tile(name, ap)
            carved.append(t)
            return t
        hT_flat = hT.ap.rearrange('p k t -> p (k t)')
        hg_flat = hg.ap.rearrange('p k t -> p (k t)').bitcast(BF16)
        gy_flat = gy.ap.rearrange('p k t -> p (k t)')
        names1 = list(Bn.keys())
        Bn1 = {}
        for i, nm in enumerate(names1):
            if i < 8:
                Bn1[nm] = carve(hT_flat, i, 'C_' + nm)
            elif i < 16:
                Bn1[nm] = carve(hg_flat, i - 8, 'C_' + nm)
            else:
                Bn1[nm] = carve(gy_flat, i - 16, 'C_' + nm)
        sqb1 = Tile('C_sqb', tmpr[0].ap.bitcast(BF16).rearrange('p (a t) -> p a t', a=8))
        carved.append(sqb1)
        csm1 = {n: sb('c1_' + n, [128, 4]) for n in ['gcol', 'egcol', 'dk', 'ekt', 'bg']}
        Tsets = [
            {'F': Fs, 'B': Bn, 'sqb': sqb, 'csm': csm, 'pg': pg, 'pc': 64},
            {'F': Fs1, 'B': Bn1, 'sqb': sqb1, 'csm': csm1, 'pg': [pbig[0], pbig[1], pstat], 'pc': 72},
        ]

        def fence(frm, to):
            rd = {}
            rdd = []
            for t in frm:
                for e, op in t.rd.items():
                    if e not in rd or rd[e].seq < op.seq:
                        rd[e] = op
                rdd += t.rdd
                if t.lw is not None:
                    if t.lw.isdma:
                        rdd.append(t.lw)
                    elif t.lw.eng not in rd or rd[t.lw.eng].seq < t.lw.seq:
                        rd[t.lw.eng] = t.lw
            for t in to:
                for e, op in rd.items():
                    if e not in t.rd or t.rd[e].seq < op.seq:
                        t.rd[e] = op
                t.rdd = t.rdd + rdd

        def smv(name):
            o, w = SM[name]
            return sm[:, o:o + w]

        identf = smv('ident')
        tri = smv('tri')
        negL = smv('negL')
        negU = smv('negU')
        onesf = smv('ones')

        pieces = []
        for l in layers:
            for q in range(24):
                pieces.append((wmod_d[l, q], True))
        for s in range(n_tiles):
            for l in layers:
                for g in range(24):
                    pieces.append((wst_d[l, g], False))
        rstate = {'use': 0, 'iss': 0}

        def next_slot():
            i = rstate['use']
            while rstate['iss'] <= min(i + NSLOT - 2, len(pieces) - 1):
                j = rstate['iss']
                src, f32v = pieces[j]
                slot = ring[j % NSLOT]
                if f32v:
                    S.dma('sp', slot.v.cast(F32), src, f'ringh{j % NSLOT}')
                else:
                    S.dma('pool', slot.v, src, f'rings{j % NSLOT}')
                rstate['iss'] += 1
            rstate['use'] += 1
            return ring[i % NSLOT]

        S.dma('sp', sm.v, sm_d, 'const')
        S.dma('pool', gatew.v.rr('p l g c m -> p (l g c m)'), gw_d, 'const2')
        S.dma('pool', wtail.v.rr('p l k j -> p (l k j)'), wt_d, 'const3')
        S.dma('pool', masks.v.rr('p a b -> p (a b)'), mk_d, 'const4')
        S.cp('dve', ident_bf.v, identf)
        S.cp('dve', ones_bf.v, onesf)
        S.act(scT.v, smv('cT'), AF.Silu)
        for l in range(NL):
            S.memset('pool', Sst[l].v, 0.0)
            S.memset('pool', Sbf[l].v, 0.0)
            S.memset('pool', hst[l].v, 0.0)
            S.memset('pool', halo[l].v, 0.0)
        for l in layers:
            for q in range(24):
                slot = next_slot()
                wv = slot.v.cast(F32).rr('p (k n) -> p k n', k=KC)
                items = []
                for j in range(2):
                    col = q * 2 + j
                    for kc in range(KC):
                        items.append((psm[:, col:col + 1], wv[:, kc, j * 128:(j + 1) * 128], scT[:, kc:kc + 1],
                                      kc == 0, kc == KC - 1))
                S.mm(items)
            S.tt('dve', modT[l].v, psm[:, 0:48], smv(f'bmod{l}'), ALU.add)
            S.stt(gam1[l].v, modT[l][:, 8:16], 1.0, smv(f'nmw{l}'), ALU.add, ALU.mult)
            S.ts('dve', gam1[l].v, gam1[l].v, 32.0, None, ALU.mult)
            S.stt(gam2[l].v, modT[l][:, 32:40], 1.0, smv(f'nlw{l}'), ALU.add, ALU.mult)
            S.ts('dve', gam2[l].v, gam2[l].v, 32.0, None, ALU.mult)
            S.act(lsm['e'].v, smv(f'lam{l}'), AF.Exp, scale=-1.0)
            S.act(lsm['l1'].v, lsm['e'].v, AF.Ln, bias=1.0)
            S.ts('dve', ccol[l].v, lsm['l1'].v, -8.0, None, ALU.mult)
            S.ts('dve', c2col[l].v, lsm['l1'].v, -16.0, None, ALU.mult)
            S.ts('dve', lnws[l].v, smv(f'lnw{l}'), float(np.sqrt(512.0)), None, ALU.mult)
            S.ts('dve', gnws[l].v, smv(f'gnw{l}'), float(np.sqrt(128.0)), None, ALU.mult)
            S.act(negA[l].v, smv(f'alog{l}'), AF.Exp)
            S.ts('dve', negA[l].v, negA[l].v, -1.0, None, ALU.mult)
        S.ts('dve', fnws.v, smv('fnw'), 32.0, None, ALU.mult)

        bigi = {'i': 0}
        if stage == 0:
            S.dma('sp', xT.v, x_d.rearrange('(k p) n -> p k n', p=128)[:, :, 0:TT], 'xload')
            S.cp('dve', xT[:, 0, 0:48], modT[layers[0]].v)
            S.dma('sp', out_d.rearrange('(k p) n -> p k n', p=128)[:, :, 0:TT], xT.v, 'ostore')
            S.finish()
            S.emit()
            return nc

        def nbig():
            b = pbig[bigi['i'] % 2]
            bigi['i'] += 1
            return b

        def modnorm(gam, shcols, dst):
            S.act(hT.v, xT.v, AF.Square)
            S.mm([(pstat.v, ones_bf.v, hT[:, kc, :], kc == 0, kc == KC - 1) for kc in range(KC)])
            rs = Fs[6]
            S.rsq(rs.v, pstat.v, D * EPS)
            for kc in range(KC):
                t = tmpr[kc % 2]
                S.stt(t.v, xT[:, kc, :], gam[:, kc:kc + 1], rs.v, ALU.mult, ALU.mult)
                if shcols is None:
                    S.cp('act', dst[:, kc, :], t.v)
                else:
                    S.act(dst[:, kc, :], t.v, AF.Identity, bias=shcols[:, kc:kc + 1])

        for s in range(n_tiles):
            S.dma('sp', xT.v, x_d.rearrange('(k p) n -> p k n', p=128)[:, :, s * TT:(s + 1) * TT], 'xload')
            for l in layers:
                modnorm(gam1[l], modT[l][:, 0:8], hT)

                if stage == 1:
                    S.dma('sp', out_d.rearrange('(k p) n -> p k n', p=128)[:, :, 0:TT], xT.v, 'ostore')
                    S.finish()
                    S.emit()
                    return nc
                wslots = {}

                def wcol(cc):
                    pidx = {1: 0, 0: 1}.get(cc // 4, cc // 4)
                    if pidx not in wslots:
                        assert len(wslots) == pidx
                        wslots[pidx] = next_slot()
                    sl = wslots[pidx].v.rr('p (k n) -> p k n', k=KC)
                    return [sl[:, kc, (cc % 4) * 128:(cc % 4 + 1) * 128] for kc in range(KC)]

                def proj(cc):
                    pb = nbig()
                    wc = wcol(cc)
                    S.mm([(pb.v, wc[kc], hT[:, kc, :], kc == 0, kc == KC - 1) for kc in range(KC)])
                    return pb

                def conv(pb, ci, wname, widx, bias):
                    pr = pre[ci % 2]
                    S.cp('act', pr[:, 3:TT + 3], pb.v)
                    S.cp('pool', pr[:, 0:3], halo[l][:, ci, :])
                    S.cp('pool', halo[l][:, ci, :], pr[:, TT:TT + 3])
                    o, w = SM[wname]
                    wv = sm[:, o + widx * 4:o + widx * 4 + 4]
                    acc = (Fs if ci % 2 == 0 else Fs1)[6]
                    if bias is not None:
                        S.ts('dve', acc.v, pr[:, 0:TT], wv[:, 0:1], bias, ALU.mult, ALU.add)
                    else:
                        S.ts('dve', acc.v, pr[:, 0:TT], wv[:, 0:1], None, ALU.mult)
                    for k in range(1, 4):
                        S.stt(acc.v, pr[:, k:TT + k], wv[:, k:k + 1], acc.v, ALU.mult, ALU.add)
                    return acc

                for c in range(4):
                    pb = proj(4 + c)
                    if gelu_mode == 0:
                        S.act(gy[:, c, :], pb.v, AF.Gelu_apprx_tanh)
                    else:
                        FG = Fs if c % 2 == 0 else Fs1
                        y = FG[0]
                        S.cp('act', y.v, pb.v)
                        S.tt('dve', FG[1].v, y.v, y.v, ALU.mult)
                        S.ts('dve', FG[1].v, FG[1].v, 0.044715, 1.0, ALU.mult, ALU.add)
                        S.tt('dve', FG[1].v, FG[1].v, y.v, ALU.mult)
                        S.act(FG[1].v, FG[1].v, AF.Sigmoid, scale=float(2.0 * np.sqrt(2.0 / np.pi)))
                        S.tt('dve', gy[:, c, :], FG[1].v, y.v, ALU.mult)

                if stage == 2:
                    S.dma('sp', out_d.rearrange('(k p) n -> p k n', p=128)[:, :, 0:TT], xT.v, 'ostore')
                    S.finish()
                    S.emit()
                    return nc
                for c in range(4):
                    pb = proj(c)
                    lcb = smv(f'lcb{l}')
                    xr = conv(pb, c, f'lcw{l}', c, lcb[:, c:c + 1])
                    S.cp('act', xrb.v, xr.v)
                    pr_ = nbig()
                    pi_ = nbig()
                    S.mm([(pr_.v, gatew[:, l, 0, c, :], xrb.v, True, True)])
                    S.mm([(pi_.v, gatew[:, l, 1, c, :], xrb.v, True, True)])
                    FL = Fs if c % 2 == 0 else Fs1
                    r, ig, a, a2, b, h = FL[0], FL[1], FL[2], FL[3], FL[4], FL[5]
                    S.act(r.v, pr_.v, AF.Sigmoid, bias=smv(f'gab{l}')[:, c:c + 1])
                    S.act(ig.v, pi_.v, AF.Sigmoid, bias=smv(f'gxb{l}')[:, c:c + 1])
                    S.act(a.v, r.v, AF.Exp, scale=ccol[l][:, c:c + 1])
                    S.act(a2.v, r.v, AF.Exp, scale=c2col[l][:, c:c + 1])
                    S.ts('dve', a2.v, a2.v, -1.0, 1.0, ALU.mult, ALU.add)
                    S.ts('dve', a2.v, a2.v, 1e-12, None, ALU.max)
                    S.act(a2.v, a2.v, AF.Sqrt)
                    S.tt('dve', b.v, ig.v, xr.v, ALU.mult)
                    S.tt('dve', b.v, b.v, a2.v, ALU.mult)
                    S.scan(h.v, a.v, b.v, hst[l][:, c:c + 1])
                    S.cp('pool', hst[l][:, c:c + 1], h[:, TT - 1:TT])
                    S.tt('dve', hg[:, c, :], h.v, gy[:, c, :], ALU.mult)
                    S.act(sqh.v, hg[:, c, :], AF.Square)
                    S.mm([(pstat.v, ones_bf.v, sqh.v, c == 0, c == 3)])
                rsl = Fs[0]
                S.rsq(rsl.v, pstat.v, 512 * EPS)
                for c in range(4):
                    S.stt(mixT[:, c, :], hg[:, c, :], lnws[l][:, c:c + 1], rsl.v, ALU.mult, ALU.mult)

                if stage == 3:
                    S.dma('sp', out_d.rearrange('(k p) n -> p k n', p=128)[:, :, 0:TT], xT.v, 'ostore')
                    S.finish()
                    S.emit()
                    return nc
                for c in range(12):
                    pb = proj(8 + c)
                    cv = conv(pb, 4 + c, f'gcw{l}', c, None)
                    S.act(qkv[:, c, :], cv.v, AF.Silu)
                for c in range(4):
                    pb = proj(20 + c)
                    S.act(sz[:, c, :], pb.v, AF.Silu)
                for n in range(4):
                    S.mm([(psm[:, n * 8:(n + 1) * 8], hT[:, kc, n * 128:(n + 1) * 128], wtail[:, l, kc, :],
                           kc == 0, kc == KC - 1) for kc in range(KC)])
                ab = psm[:, 0:32].rr('p (n j) -> p n j', j=8)
                S.act(gsm['beta'].v, ab[:, :, 0:4], AF.Sigmoid)
                S.tt('dve', gsm['xg'].v, ab[:, :, 4:8], smv(f'dtb{l}').un(1).bc([128, 4, 4]), ALU.add)
                S.act(gsm['ax'].v, gsm['xg'].v, AF.Abs)
                S.act(gsm['e'].v, gsm['ax'].v, AF.Exp, scale=-1.0)
                S.act(gsm['l1'].v, gsm['e'].v, AF.Ln, bias=1.0)
                S.stt(gsm['sp'].v, gsm['xg'].v, 0.0, gsm['l1'].v, ALU.max, ALU.add)
                S.tt('dve', gsm['g'].v, gsm['sp'].v, negA[l].v.un(1).bc([128, 4, 4]), ALU.mult)


                if stage == 4:
                    S.dma('sp', out_d.rearrange('(k p) n -> p k n', p=128)[:, :, 0:TT], xT.v, 'ostore')
                    S.finish()
                    S.emit()
                    return nc
                B4 = [128, 4, 128]
                fence([hT, hg, gy, tmpr[0]], carved)

                def gdn_chunk(n, T):
                    F, Bn, sqb_, cs, pgs, pc = T['F'], T['B'], T['sqb'], T['csm'], T['pg'], T['pc']
                    r4 = lambda v: v.rr('p (h t) -> p h t', h=4)
                    tsl = slice(n * 128, (n + 1) * 128)
                    qT_ = qkv[:, 0:4, tsl]
                    kT_ = qkv[:, 4:8, tsl]
                    vT_ = qkv[:, 8:12, tsl]
                    beta_n = gsm['beta'][:, n, :]
                    g_n = gsm['g'][:, n, :]
                    S.act(sqb_.v, qkv[:, 0:8, tsl], AF.Square)
                    S.mm([(pgs[0].v, ones_bf.v, sqb_[:, 0:4, :], True, True)])
                    S.mm([(pgs[1].v, ones_bf.v, sqb_[:, 4:8, :], True, True)])
                    yield
                    S.rsq(F[0].v, pgs[0].v, 1e-6)
                    S.rsq(F[1].v, pgs[1].v, 1e-6)
                    yield
                    S.stt(Bn['qn'].v, qT_, float(128.0 ** -0.5), r4(F[0].v), ALU.mult, ALU.mult)
                    S.tt('dve', Bn['kn'].v, kT_, r4(F[1].v), ALU.mult)
                    yield
                    S.tt('dve', r4(F[0].v), tri.un(1).bc(B4), g_n.un(2).bc(B4), ALU.mult)
                    S.mm([(pgs[2].v, onesf, F[0].v, True, True)])
                    S.mm([(psm[:, pc:pc + 4], tri, g_n, True, True)])
                    ptk = r4(ptr[:, 0:512])
                    ptv = r4(ptr[:, 512:1024])
                    yield
                    S.cp('act', cs['gcol'].v, psm[:, pc:pc + 4])
                    Grow = r4(pgs[2].v)
                    t3 = r4(F[1].v)
                    yield
                    S.tt('dve', t3, cs['gcol'].v.un(2).bc(B4), Grow, ALU.subtract)
                    Eg = r4(F[4].v)
                    S.act(cs['egcol'].v, cs['gcol'].v, AF.Exp)
                    S.tt('dve', cs['dk'].v, Grow[:, :, 127], cs['gcol'].v, ALU.subtract)
                    yield
                    aL = r4(F[2].v)
                    aU = r4(F[3].v)
                    S.tt('pool', aL, t3, negL.un(1).bc(B4), ALU.add)
                    S.tt('pool', aU, negU.un(1).bc(B4), t3, ALU.subtract)
                    S.act(cs['ekt'].v, cs['dk'].v, AF.Exp)
                    S.tt('dve', cs['bg'].v, beta_n, cs['egcol'].v, ALU.mult)
                    yield
                    S.act(F[2].v, F[2].v, AF.Exp)
                    S.act(F[3].v, F[3].v, AF.Exp)
                    S.act(F[4].v, pgs[2].v, AF.Exp)
                    yield
                    S.tr([(ptk[:, h, :], Bn['kn'][:, h, :], ident_bf.v) for h in range(4)])
                    S.tr([(ptv[:, h, :], vT_[:, h, :], ident_bf.v) for h in range(4)])
                    S.tt('dve', Bn['kbg'].v, ptk, cs['bg'].v.un(2).bc(B4), ALU.mult)
                    S.tt('dve', Bn['kt'].v, ptk, cs['ekt'].v.un(2).bc(B4), ALU.mult)
                    S.tt('dve', Bn['vb'].v, ptv, beta_n.un(2).bc(B4), ALU.mult)
                    yield
                    S.tt('pool', aL, aL, beta_n.un(2).bc(B4), ALU.mult)
                    pgr = r4(pgs[0].v)
                    pqk = r4(pgs[1].v)
                    S.mm([(pgr[:, h, :], Bn['kn'][:, h, :], Bn['kn'][:, h, :], True, True) for h in range(4)])
                    S.mm([(pqk[:, h, :], Bn['kn'][:, h, :], Bn['qn'][:, h, :], True, True) for h in range(4)])
                    S.tt('pool', Bn['qd'].v, Bn['qn'].v, Eg, ALU.mult)
                    yield
                    S.tt('dve', Bn['L0'].v, pgr, aL, ALU.mult)
                    S.tt('dve', Bn['attnT'].v, pqk, aU, ALU.mult)
                    yield
                    S.tr([(ptk[:, h, :], Bn['L0'][:, h, :], ident_bf.v) for h in range(4)])
                    S.cp('dve', Bn['U0'].v, ptk)
                    yield
                    idb = ident_bf.v.un(1).bc(B4)
                    pY = r4(pgs[0].v)
                    pYp = r4(pgs[1].v)
                    pZ = r4(pgs[2].v)
                    S.tt('pool', Bn['Oa'].v, Bn['U0'].v, masks[:, 0, :].un(1).bc(B4), ALU.mult)
                    S.tt('pool', Bn['Ob'].v, Bn['L0'].v, masks[:, 7, :].un(1).bc(B4), ALU.mult)
                    S.tt('pool', Bn['Da0'].v, idb, Bn['Oa'].v, ALU.subtract)
                    S.tt('pool', Bn['Db0'].v, idb, Bn['Ob'].v, ALU.subtract)
                    Dc, Dpc = 'Da0', 'Db0'
                    for lev in range(1, 7):
                        Dn, Dpn = ('Da1', 'Db1') if lev % 2 == 1 else ('Da0', 'Db0')
                        S.tt('pool', Bn['Oa'].v, Bn['U0'].v, masks[:, lev, :].un(1).bc(B4), ALU.mult)
                        S.tt('pool', Bn['Ob'].v, Bn['L0'].v, masks[:, 7 + lev, :].un(1).bc(B4), ALU.mult)
                        yield
                        S.mm([(pY[:, h, :], Bn['Ob'][:, h, :], Bn[Dc][:, h, :], True, True) for h in range(4)])
                        if lev < 6:
                            S.mm([(pYp[:, h, :], Bn['Oa'][:, h, :], Bn[Dpc][:, h, :], True, True) for h in range(4)])
                        yield
                        S.cp('act', Bn['Ya'].v, pY)
                        if lev < 6:
                            S.cp('dve', Bn['Yb'].v, pYp)
                        yield
                        S.mm([(pZ[:, h, :], Bn[Dpc][:, h, :], Bn['Ya'][:, h, :], True, True) for h in range(4)])
                        if lev < 6:
                            S.mm([(pY[:, h, :], Bn[Dc][:, h, :], Bn['Yb'][:, h, :], True, True) for h in range(4)])
                        yield
                        S.tt('dve', Bn[Dn].v, Bn[Dc].v, pZ, ALU.subtract)
                        if lev < 6:
                            S.tt('dve', Bn[Dpn].v, Bn[Dpc].v, pY, ALU.subtract)
                        Dc, Dpc = Dn, Dpn
                        yield
                    Mf = Bn[Dc]
                    pu = r4(pgs[0].v)
                    pw = r4(pgs[1].v)
                    S.mm([(pu[:, h, :], Mf[:, h, :], Bn['vb'][:, h, :], True, True) for h in range(4)])
                    S.mm([(pw[:, h, :], Bn['kbg'][:, h, :], Mf[:, h, :], True, True) for h in range(4)])
                    yield
                    S.cp('act', F[5].v, pgs[0].v)
                    S.cp('dve', Bn['wT'].v, pw)
                    yield
                    yield 'SCAN'
                    pws = r4(pgs[2].v)
                    S.mm([(pws[:, h, :], Bn['wT'][:, h, :], Sbf[l][:, h, :], True, True) for h in range(4)])
                    yield
                    S.tt('dve', Bn['vnew'].v, r4(F[5].v), pws, ALU.subtract)
                    yield
                    po = r4(pgs[0].v)
                    items = []
                    for h in range(4):
                        items.append((po[:, h, :], Sbf[l][:, h, :], Bn['qd'][:, h, :], True, False))
                        items.append((po[:, h, :], Bn['vnew'][:, h, :], Bn['attnT'][:, h, :], False, True))
                    S.mm(items)
                    pS = r4(pgs[1].v)
                    S.mm([(pS[:, h, :], Bn['kt'][:, h, :], Bn['vnew'][:, h, :], True, True) for h in range(4)])
                    S.tt('pool', Sst[l].v, Sst[l].v, Eg[:, :, 127].un(2).bc(B4), ALU.mult)
                    yield
                    S.tt('dve', Sbf[l].v, Sst[l].v, pS, ALU.add)
                    S.tt('dve', Sst[l].v, Sst[l].v, pS, ALU.add)
                    yield 'TAIL'
                    S.act(Bn['osq'].v, po, AF.Square)
                    S.cp('act', F[0].v, pgs[0].v)
                    yield
                    S.mm([(pgs[2].v, ones_bf.v, Bn['osq'].v, True, True)])
                    yield
                    S.rsq(F[1].v, pgs[2].v, 128 * EPS)
                    yield
                    S.tt('dve', F[6].v, F[0].v, F[1].v, ALU.mult)
                    S.stt(mixT[:, 4:8, tsl], r4(F[6].v), gnws[l][:, 0:1], sz[:, 0:4, tsl], ALU.mult, ALU.mult)
                    yield

                for pair in range(2):
                    gens = [gdn_chunk(2 * pair + i, Tsets[i]) for i in range(2)]
                    alive = [True, True]
                    while any(alive):
                        for i in range(2):
                            if alive[i]:
                                if next(gens[i]) == 'SCAN':
                                    alive[i] = False
                    for i in range(2):
                        while next(gens[i]) != 'TAIL':
                            pass
                    tl = [True, True]
                    while any(tl):
                        for i in range(2):
                            if tl[i]:
                                try:
                                    next(gens[i])
                                except StopIteration:
                                    tl[i] = False
                fence(carved, [hT, hg, gy, tmpr[0]])

                oslots = [next_slot() for g in range(2)]
                for fc in range(8):
                    sl = oslots[fc // 4].v.rr('p (k n) -> p k n', k=KC)
                    pb = nbig()
                    S.mm([(pb.v, sl[:, kc, (fc % 4) * 128:(fc % 4 + 1) * 128], mixT[:, kc, :], kc == 0, kc == KC - 1)
                          for kc in range(KC)])
                    S.stt(xT[:, fc, :], pb.v, modT[l][:, 16 + fc:17 + fc], xT[:, fc, :], ALU.mult, ALU.add)


                if stage == 6:
                    S.dma('sp', out_d.rearrange('(k p) n -> p k n', p=128)[:, :, 0:TT], xT.v, 'ostore')
                    S.finish()
                    S.emit()
                    return nc
                modnorm(gam2[l], modT[l][:, 24:32], hT)
                for hgp in range(8):
                    su = next_slot().v.rr('p (k n) -> p k n', k=KC)
                    sd = next_slot().v.rr('p (c f) -> p c f', c=4)
                    for hc in range(4):
                        pb = nbig()
                        S.mm([(pb.v, su[:, kc, hc * 128:(hc + 1) * 128], hT[:, kc, :], kc == 0, kc == KC - 1)
                              for kc in range(KC)])
                        hr = hidr[hc % 2]
                        S.act(hr.v, pb.v, AF.Relu)
                        S.tt('dve', hid[:, hc, :], hr.v, hr.v, ALU.mult)
                    for fc in range(8):
                        pb = nbig()
                        S.mm([(pb.v, sd[:, hc, fc * 128:(fc + 1) * 128], hid[:, hc, :], hc == 0, hc == 3)
                              for hc in range(4)])
                        S.stt(xT[:, fc, :], pb.v, modT[l][:, 40 + fc:41 + fc], xT[:, fc, :], ALU.mult, ALU.add)

            if final_norm:
                S.act(hT.v, xT.v, AF.Square)
                S.mm([(pstat.v, ones_bf.v, hT[:, kc, :], kc == 0, kc == KC - 1) for kc in range(KC)])
                S.rsq(Fs[6].v, pstat.v, D * EPS)
                for kc in range(KC):
                    S.stt(xT[:, kc, :], xT[:, kc, :], fnws[:, kc:kc + 1], Fs[6].v, ALU.mult, ALU.mult)
            S.dma('sp', out_d.rearrange('(k p) n -> p k n', p=128)[:, :, s * TT:(s + 1) * TT], xT.v, 'ostore')

        S.finish()
        S.emit()
    return nc


def _kpiece(W, c0, ncols):
    return np.ascontiguousarray(W[:, c0:c0 + ncols].reshape(KC, 128, ncols).transpose(1, 0, 2).reshape(128, KC * ncols))


def _col(v):
    v = np.asarray(v, np.float32).reshape(-1)
    return v.reshape(-1, 128).T


def prep_shared(inp):
    wst = np.empty((NL, 24, 128, 4096), np.float32)
    wmod = np.empty((NL, 24, 128, 2048), np.float32)
    gatew = np.zeros((128, NL, 2, 4, 128), np.float32)
    wtail = np.empty((128, NL, KC, 8), np.float32)
    for l in range(NL):
        for g in range(6):
            wst[l, g] = _kpiece(inp['w_in'][l], 512 * {0: 1, 1: 0}.get(g, g), 512)
        for g in range(2):
            wst[l, 6 + g] = _kpiece(inp['w_out'][l], 512 * g, 512)
        for hg in range(8):
            wst[l, 8 + 2 * hg] = _kpiece(inp['w_up'][l], 512 * hg, 512)
            wd = inp['w_down'][l][512 * hg:512 * hg + 512, :]
            wst[l, 9 + 2 * hg] = wd.reshape(4, 128, 1024).transpose(1, 0, 2).reshape(128, 4096)
        for q in range(24):
            wmod[l, q] = _kpiece(inp['w_mod'][l], 256 * q, 256)
        for gi, nm in enumerate(['lru_gate_a_w', 'lru_gate_x_w']):
            W = inp[nm][l]
            for c in range(4):
                for gb in range(2):
                    gatew[gb * 64:(gb + 1) * 64, l, gi, c, gb * 64:(gb + 1) * 64] = W[2 * c + gb]
        wtail[:, l] = inp['w_in'][l][:, 3072:3080].reshape(KC, 128, 8).transpose(1, 0, 2)
    return wst, wmod, gatew.reshape(128, -1), wtail.reshape(128, -1)


def prep_small(inp, b):
    sm = np.zeros((128, NS), np.float32)

    def put(name, arr):
        o, w = SM[name]
        sm[:, o:o + w] = np.asarray(arr, np.float32).reshape(128, w)
    put('cT', _col(inp['c'][b]))
    for l in range(NL):
        put(f'bmod{l}', _col(inp['b_mod'][l]))
        put(f'nmw{l}', _col(inp['norm_mix_w'][l]))
        put(f'nlw{l}', _col(inp['norm_mlp_w'][l]))
        put(f'lcw{l}', inp['lru_conv_w'][l].reshape(4, 4, 128).transpose(2, 1, 0))
        put(f'lcb{l}', _col(inp['lru_conv_b'][l]))
        put(f'gab{l}', _col(inp['lru_gate_a_b'][l]))
        put(f'gxb{l}', _col(inp['lru_gate_x_b'][l]))
        put(f'lam{l}', _col(inp['lru_lambda'][l]))
        put(f'lnw{l}', _col(inp['lru_norm_w'][l]))
        put(f'gcw{l}', inp['gdn_conv_w'][l].reshape(4, 12, 128).transpose(2, 1, 0))
        put(f'gnw{l}', _col(inp['gdn_norm_w'][l]))
        put(f'alog{l}', np.broadcast_to(inp['gdn_a_log'][l][None, :], (128, 4)))
        put(f'dtb{l}', np.broadcast_to(inp['gdn_dt_bias'][l][None, :], (128, 4)))
    put('fnw', _col(inp['final_norm_w']))
    i = np.arange(128)
    put('ident', np.eye(128, dtype=np.float32))
    put('tri', (i[:, None] <= i[None, :]).astype(np.float32))
    put('negL', np.where(i[None, :] < i[:, None], 0.0, -1e30))
    put('negU', np.where(i[:, None] <= i[None, :], 0.0, -1e30))
    put('ones', np.ones((128, 128), np.float32))
    return sm


def prep_masks():
    i = np.arange(128)
    m = np.zeros((128, 14, 128), np.float32)
    for lev in range(7):
        bj = i[:, None] >> lev
        bi = i[None, :] >> lev
        mu = ((bj % 2 == 0) & (bi == bj + 1)).astype(np.float32)
        m[:, lev, :] = mu
        m[:, 7 + lev, :] = mu.T
    return m.reshape(128, -1)


_NC_CACHE = {}


def run(inp, n_tiles, layers, final_norm=True, ncores=BATCH, gelu_mode=0, stage=99):
    inp = {k: np.asarray(v) for k, v in inp.items()}
    key = (n_tiles, tuple(layers), final_norm, gelu_mode, stage)
    if key not in _NC_CACHE:
        _NC_CACHE[key] = build_nc(n_tiles, layers, final_norm, gelu_mode, stage)
    nc = _NC_CACHE[key]
    wst, wmod, gatew, wtail = prep_shared(inp)
    ntok = n_tiles * TT
    in_maps = []
    for b in range(ncores):
        xT = np.ascontiguousarray(inp['x'][b, :ntok, :].T.astype(np.float32))
        in_maps.append({"xT": xT, "wst": wst, "wmod": wmod, "small": prep_small(inp, b),
                        "gatew": gatew, "wtail": wtail, "masks": prep_masks()})
    res = run_bass_kernel_spmd(nc, in_maps, core_ids=list(range(ncores)))
    out = np.stack([np.asarray(r["outT"]).T for r in res.results], axis=0)
    return out.astype(np.float32)


def kernel(**inputs):
    return run(inputs, SEQ // TT, list(range(NL)), True, BATCH, gelu_mode=GELU_MODE)


GELU_MODE = 1
```
